# Optimizing a Trainium2 kernel written in Bass

```python
import math
import jax, jax.numpy as jnp
from jax import lax
import numpy as np

D_MODEL = 1024
BATCH = 16
SEQ = 4096
DEPTH = 2
DEC_BATCH = 32
DEC_SEQ = 64
PAST_LEN = 4096

CHUNK = 64
N_AB = (DEPTH + 1) // 2
N_C = DEPTH // 2
MLA_HEADS = 8
Q_LORA = 384
KV_LORA = 256
NOPE_DIM = 64
ROPE_DIM = 32
V_DIM = 64
MLA_WIDTH = MLA_HEADS * V_DIM
ROPE_BASE = 10000.0
Q_BLOCK = 128
ATTN_SCALE = (NOPE_DIM + ROPE_DIM) ** -0.5
S5_WIDTH = 512
S5_GROUP = 16
S5_GROUPS = S5_WIDTH // S5_GROUP
S5_STATE = 64
DT_MIN = 1e-3
DT_MAX = 1e-1
CONV_WIDTH = 31
CONV_CH = D_MODEL
EPS = 1e-6
IN_AB = Q_LORA + KV_LORA + ROPE_DIM + MLA_WIDTH + 2 * S5_WIDTH
AB_SPLITS = (Q_LORA, Q_LORA + KV_LORA, Q_LORA + KV_LORA + ROPE_DIM,
             Q_LORA + KV_LORA + ROPE_DIM + MLA_WIDTH,
             Q_LORA + KV_LORA + ROPE_DIM + MLA_WIDTH + S5_WIDTH)

kernel_name = 'hybrid_stream_mla_s5_conformer_step'


def rms_norm(x, g):
    xf = x.astype(jnp.float32)
    y = xf * lax.rsqrt(jnp.mean(xf * xf, axis=-1, keepdims=True) + EPS)
    return (y * g.astype(jnp.float32)).astype(x.dtype)


def layer_norm(x, g, b):
    xf = x.astype(jnp.float32)
    mu = jnp.mean(xf, axis=-1, keepdims=True)
    xc = xf - mu
    var = jnp.mean(xc * xc, axis=-1, keepdims=True)
    return (xc * lax.rsqrt(var + EPS) * g.astype(jnp.float32) + b.astype(jnp.float32)).astype(x.dtype)


def rope(x, pos):
    half = ROPE_DIM // 2
    inv = ROPE_BASE ** (-jnp.arange(half, dtype=jnp.float32) / half)
    ang = pos.astype(jnp.float32)[:, None] * inv[None, :]
    cos = jnp.cos(ang)[:, None, :]
    sin = jnp.sin(ang)[:, None, :]
    xf = x.astype(jnp.float32)
    x1, x2 = xf[..., :half], xf[..., half:]
    return jnp.concatenate([x1 * cos - x2 * sin, x2 * cos + x1 * sin], axis=-1).astype(x.dtype)


def attend(q_n, q_r, k_n, k_r, v, mask):
    s = (jnp.einsum('bqhd,bkhd->bhqk', q_n, k_n)
         + jnp.einsum('bqhr,bkr->bhqk', q_r, k_r)).astype(jnp.float32) * ATTN_SCALE
    if mask is not None:
        s = jnp.where(mask, s, -1e30)
    p = jax.nn.softmax(s, axis=-1).astype(v.dtype)
    return jnp.einsum('bhqk,bkhd->bqhd', p, v)


def attend_chunk_causal(q_n, q_r, k_n, k_r, v):
    B, T, H, _ = q_n.shape
    nb = T // Q_BLOCK
    key_chunk = jnp.arange(T) // CHUNK

    def one_block(args):
        qn_b, qr_b, start = args
        q_chunk = (start + jnp.arange(Q_BLOCK)) // CHUNK
        mask = key_chunk[None, :] <= q_chunk[:, None]
        return attend(qn_b, qr_b, k_n, k_r, v, mask)

    qn_blocks = jnp.moveaxis(q_n.reshape(B, nb, Q_BLOCK, H, NOPE_DIM), 1, 0)
    qr_blocks = jnp.moveaxis(q_r.reshape(B, nb, Q_BLOCK, H, ROPE_DIM), 1, 0)
    starts = jnp.arange(nb, dtype=jnp.int32) * Q_BLOCK
    out = lax.map(one_block, (qn_blocks, qr_blocks, starts))
    return jnp.moveaxis(out, 0, 1).reshape(B, T, H, V_DIM)


def _linear_combine(e1, e2):
    a1, b1 = e1
    a2, b2 = e2
    return a1 * a2, a2 * b1 + b2


def s5_scan(u, s0_re, s0_im, lam_re, lam_im, log_dt, b_re, b_im, c_re, c_im, d_skip):
    f32 = jnp.float32
    B, T, _ = u.shape
    uf = u.astype(f32)
    ug = uf.reshape(B, T, S5_GROUPS, S5_GROUP)
    lam = lax.complex(lam_re.astype(f32), lam_im.astype(f32))
    dt = jnp.exp(log_dt.astype(f32))[:, None]
    lam_bar = jnp.exp(lam * dt)
    b_bar = ((lam_bar - 1.0) / lam)[..., None] * lax.complex(b_re.astype(f32), b_im.astype(f32))
    bu = jnp.einsum('gpn,btgn->btgp', b_bar, ug.astype(jnp.complex64))
    if s0_re is not None:
        s0 = lax.complex(s0_re.astype(f32), s0_im.astype(f32))
        bu = bu.at[:, 0].add(lam_bar * s0)
    a = jnp.broadcast_to(lam_bar, bu.shape)
    _, states = lax.associative_scan(_linear_combine, (a, bu), axis=1)
    c = lax.complex(c_re.astype(f32), c_im.astype(f32))
    y = jnp.real(jnp.einsum('gnp,btgp->btgn', c, states)).reshape(B, T, S5_WIDTH) + d_skip.astype(f32) * uf
    last = states[:, -1]
    return y.astype(u.dtype), jnp.real(last), jnp.imag(last)


def ab_layer(x, pos, past_lat, past_kr, past_re, past_im, prm):
    (norm_g, w_in, g_q_lat, w_uq, g_kv_lat, w_uk, w_uv, g_q_nope, g_q_rope, g_k_nope, g_k_rope,
     lam_re, lam_im, log_dt, b_re, b_im, c_re, c_im, d_skip, w_glu, b_glu, w_out) = prm
    B, T, _ = x.shape
    h = rms_norm(x, norm_g)
    q_lat, c_kv, k_r, gate_mla, u, gate_s5 = jnp.split(h @ w_in, AB_SPLITS, axis=-1)
    c_kv = rms_norm(c_kv, g_kv_lat)
    k_r = rope(rms_norm(k_r, g_k_rope)[:, :, None, :], pos)[:, :, 0]
    q = (rms_norm(q_lat, g_q_lat) @ w_uq).reshape(B, T, MLA_HEADS, NOPE_DIM + ROPE_DIM)
    q_n = rms_norm(q[..., :NOPE_DIM], g_q_nope)
    q_r = rope(rms_norm(q[..., NOPE_DIM:], g_q_rope), pos)
    if past_lat is None:
        lat_all, kr_all = c_kv, k_r
    else:
        lat_all = jnp.concatenate([past_lat.astype(c_kv.dtype), c_kv], axis=1)
        kr_all = jnp.concatenate([past_kr.astype(k_r.dtype), k_r], axis=1)
    k_n = rms_norm(jnp.einsum('bkc,chd->bkhd', lat_all, w_uk), g_k_nope)
    v = jnp.einsum('bkc,chd->bkhd', lat_all, w_uv)
    if past_lat is None:
        o_mla = attend_chunk_causal(q_n, q_r, k_n, kr_all, v)
    else:
        o_mla = attend(q_n, q_r, k_n, kr_all, v, None)
    y_s5, fin_re, fin_im = s5_scan(u, past_re, past_im, lam_re, lam_im, log_dt, b_re, b_im, c_re, c_im, d_skip)
    z = jax.nn.gelu(y_s5)
    z = z * jax.nn.sigmoid(z @ w_glu + b_glu)
    mixed = jnp.concatenate([o_mla.reshape(B, T, MLA_WIDTH) * jax.nn.silu(gate_mla),
                             z * jax.nn.silu(gate_s5)], axis=-1)
    return x + mixed @ w_out, c_kv, k_r, fin_re, fin_im


def conv_layer(x, past, prm):
    norm_g, w_in, conv_w, conv_b, ln_g, ln_b, w_out = prm
    h = rms_norm(x, norm_g)
    a, b, gate = jnp.split(h @ w_in, 3, axis=-1)
    v = a * jax.nn.sigmoid(b)
    if past is None:
        pad = jnp.zeros((x.shape[0], CONV_WIDTH - 1, CONV_CH), v.dtype)
    else:
        pad = past.astype(v.dtype)
    vp = jnp.concatenate([pad, v], axis=1)
    y = lax.conv_general_dilated(vp, conv_w.astype(v.dtype)[:, None, :], (1,), 'VALID',
                                 dimension_numbers=('NWC', 'WIO', 'NWC'),
                                 feature_group_count=CONV_CH) + conv_b
    y = jax.nn.silu(layer_norm(y, ln_g, ln_b))
    return x + (y * jax.nn.silu(gate)) @ w_out, vp[:, -(CONV_WIDTH - 1):]


def setup_inputs(seed: int = 0) -> dict:
    key = jax.random.key(seed)
    k = jax.random.split(key, 40)
    f32 = jnp.float32

    def nrm(i, shape, scale):
        return jax.random.normal(k[i], shape, f32) * scale

    def gain(i, shape):
        return 1.0 + 0.05 * jax.random.normal(k[i], shape, f32)

    n_idx = jnp.arange(S5_STATE, dtype=f32)
    return {
        'x_prompt': nrm(0, (BATCH, SEQ, D_MODEL), 1.0),
        'x_sample': nrm(1, (DEC_BATCH, DEC_SEQ, D_MODEL), 1.0),
        'cache_mla_latent': nrm(2, (N_AB, DEC_BATCH, PAST_LEN, KV_LORA), 1.0),
        'cache_mla_krope': nrm(3, (N_AB, DEC_BATCH, PAST_LEN, ROPE_DIM), 1.0),
        'state_s5_re': nrm(4, (N_AB, DEC_BATCH, S5_GROUPS, S5_STATE), 0.3),
        'state_s5_im': nrm(5, (N_AB, DEC_BATCH, S5_GROUPS, S5_STATE), 0.3),
        'state_conv': nrm(6, (N_C, DEC_BATCH, CONV_WIDTH - 1, CONV_CH), 0.5),
        'norm_ab': gain(7, (N_AB, D_MODEL)),
        'w_in_ab': nrm(8, (N_AB, D_MODEL, IN_AB), D_MODEL ** -0.5),
        'g_q_lat': gain(9, (N_AB, Q_LORA)),
        'w_uq': nrm(10, (N_AB, Q_LORA, MLA_HEADS * (NOPE_DIM + ROPE_DIM)), Q_LORA ** -0.5),
        'g_kv_lat': gain(11, (N_AB, KV_LORA)),
        'w_uk': nrm(12, (N_AB, KV_LORA, MLA_HEADS, NOPE_DIM), KV_LORA ** -0.5),
        'w_uv': nrm(13, (N_AB, KV_LORA, MLA_HEADS, V_DIM), KV_LORA ** -0.5),
        'g_q_nope': gain(14, (N_AB, NOPE_DIM)),
        'g_q_rope': gain(15, (N_AB, ROPE_DIM)),
        'g_k_nope': gain(16, (N_AB, NOPE_DIM)),
        'g_k_rope': gain(17, (N_AB, ROPE_DIM)),
        's5_lam_re': -0.5 + nrm(18, (N_AB, S5_GROUPS, S5_STATE), 0.01),
        's5_lam_im': math.pi * n_idx + nrm(19, (N_AB, S5_GROUPS, S5_STATE), 0.01),
        's5_log_dt': jax.random.uniform(k[20], (N_AB, S5_GROUPS), f32, math.log(DT_MIN), math.log(DT_MAX)),
        's5_b_re': nrm(21, (N_AB, S5_GROUPS, S5_STATE, S5_GROUP), (2 * S5_GROUP) ** -0.5),
        's5_b_im': nrm(22, (N_AB, S5_GROUPS, S5_STATE, S5_GROUP), (2 * S5_GROUP) ** -0.5),
        's5_c_re': nrm(23, (N_AB, S5_GROUPS, S5_GROUP, S5_STATE), 0.5),
        's5_c_im': nrm(24, (N_AB, S5_GROUPS, S5_GROUP, S5_STATE), 0.5),
        's5_d': nrm(25, (N_AB, S5_WIDTH), 0.5),
        's5_w_glu': nrm(26, (N_AB, S5_WIDTH, S5_WIDTH), S5_WIDTH ** -0.5),
        's5_b_glu': nrm(27, (N_AB, S5_WIDTH), 0.02),
        'w_out_ab': nrm(28, (N_AB, MLA_WIDTH + S5_WIDTH, D_MODEL), 0.5 * (MLA_WIDTH + S5_WIDTH) ** -0.5),
        'norm_c': gain(29, (N_C, D_MODEL)),
        'w_in_c': nrm(30, (N_C, D_MODEL, 3 * CONV_CH), D_MODEL ** -0.5),
        'conv_w': nrm(31, (N_C, CONV_WIDTH, CONV_CH), CONV_WIDTH ** -0.5),
        'conv_b': nrm(32, (N_C, CONV_CH), 0.02),
        'ln_g': gain(33, (N_C, CONV_CH)),
        'ln_b': nrm(34, (N_C, CONV_CH), 0.02),
        'w_out_c': nrm(35, (N_C, CONV_CH, D_MODEL), 0.5 * CONV_CH ** -0.5),
    }


def reference(x_prompt, x_sample, cache_mla_latent, cache_mla_krope, state_s5_re, state_s5_im, state_conv,
              norm_ab, w_in_ab, g_q_lat, w_uq, g_kv_lat, w_uk, w_uv, g_q_nope, g_q_rope, g_k_nope, g_k_rope,
              s5_lam_re, s5_lam_im, s5_log_dt, s5_b_re, s5_b_im, s5_c_re, s5_c_im, s5_d, s5_w_glu, s5_b_glu,
              w_out_ab, norm_c, w_in_c, conv_w, conv_b, ln_g, ln_b, w_out_c):
    pos_p = jnp.arange(x_prompt.shape[1], dtype=jnp.int32)
    pos_s = cache_mla_latent.shape[2] + jnp.arange(x_sample.shape[1], dtype=jnp.int32)
    yp, ys = x_prompt, x_sample
    lat_p, kr_p, re_p, im_p, conv_p = [], [], [], [], []
    lat_s, kr_s, re_s, im_s, conv_s = [], [], [], [], []
    for layer in range(DEPTH):
        i = layer // 2
        if layer % 2 == 0:
            prm = (norm_ab[i], w_in_ab[i], g_q_lat[i], w_uq[i], g_kv_lat[i], w_uk[i], w_uv[i],
                   g_q_nope[i], g_q_rope[i], g_k_nope[i], g_k_rope[i],
                   s5_lam_re[i], s5_lam_im[i], s5_log_dt[i], s5_b_re[i], s5_b_im[i], s5_c_re[i], s5_c_im[i],
                   s5_d[i], s5_w_glu[i], s5_b_glu[i], w_out_ab[i])
            yp, c_p, r_p, sr_p, si_p = ab_layer(yp, pos_p, None, None, None, None, prm)
            ys, c_s, r_s, sr_s, si_s = ab_layer(ys, pos_s, cache_mla_latent[i], cache_mla_krope[i],
                                                state_s5_re[i], state_s5_im[i], prm)
            lat_p.append(c_p); kr_p.append(r_p); re_p.append(sr_p); im_p.append(si_p)
            lat_s.append(c_s); kr_s.append(r_s); re_s.append(sr_s); im_s.append(si_s)
        else:
            prm = (norm_c[i], w_in_c[i], conv_w[i], conv_b[i], ln_g[i], ln_b[i], w_out_c[i])
            yp, cv_p = conv_layer(yp, None, prm)
            ys, cv_s = conv_layer(ys, state_conv[i], prm)
            conv_p.append(cv_p); conv_s.append(cv_s)
    new_lat_p = jnp.stack(lat_p)
    new_kr_p = jnp.stack(kr_p)
    new_re_p = jnp.stack(re_p)
    new_im_p = jnp.stack(im_p)
    new_conv_p = jnp.stack(conv_p)
    new_lat_s = jnp.stack(lat_s)
    new_kr_s = jnp.stack(kr_s)
    new_re_s = jnp.stack(re_s)
    new_im_s = jnp.stack(im_s)
    new_conv_s = jnp.stack(conv_s)
    return (yp, ys, new_lat_p, new_kr_p, new_re_p, new_im_p, new_conv_p,
            new_lat_s, new_kr_s, new_re_s, new_im_s, new_conv_s)
```

```python
import math
import numpy as np
import concourse.bass as bass
import concourse.mybir as mybir
from concourse.bass_utils import run_bass_kernel_spmd

F32 = mybir.dt.float32
BF16 = mybir.dt.bfloat16
U8 = mybir.dt.uint8
ALU = mybir.AluOpType
AF = mybir.ActivationFunctionType
AX = mybir.AxisListType

D = 1024
QL, KVL, RD, MW, SW = 384, 256, 32, 512, 512
NH, ND, VD = 8, 64, 64
HD = ND + RD
CW = 31
EPS = 1e-6
ATTN_SCALE = float(HD ** -0.5)
MAGIC = 12582912.0
TWO_PI = 2.0 * math.pi

COMPUTE = ("pe", "act", "dve", "pool")


class V:
    def __init__(self, ap, res):
        self.ap = ap
        self.res = res if isinstance(res, tuple) else (res,)

    def __getitem__(self, k):
        return V(self.ap[k], self.res)

    def r(self, pat, **kw):
        return V(self.ap.rearrange(pat, **kw), self.res)

    def bc(self, shape):
        return V(self.ap.broadcast_to(list(shape)), self.res)

    def us(self, axis):
        return V(self.ap.unsqueeze(axis), self.res)

    def named(self, *res):
        return V(self.ap, tuple(res))

    @property
    def shape(self):
        return self.ap.shape


class Sched:
    def __init__(self, nc):
        self.nc = nc
        self.q = {e: [] for e in COMPUTE + ("sp",)}
        self.cnt = {}
        self.sems = {}
        self.known = {e: {} for e in COMPUTE + ("sp",)}
        self.lastw = {}
        self.readers = {}
        self.n_inst = 0

    def _sem(self, key):
        if key not in self.sems:
            self.sems[key] = self.nc.alloc_semaphore("s_" + key)
            self.cnt[key] = 0
        return self.sems[key]

    def _deps(self, reads, writes):
        deps = {}

        def add(tok):
            if tok is None:
                return
            k, v = tok
            if deps.get(k, 0) < v:
                deps[k] = v
        for r in reads:
            add(self.lastw.get(r))
        for w in writes:
            add(self.lastw.get(w))
            for k, v in self.readers.get(w, {}).items():
                add((k, v))
        return deps

    def _emit_waits(self, engine, deps):
        kn = self.known[engine]
        for k, v in deps.items():
            if k == engine and engine == "pe":
                continue
            if kn.get(k, 0) >= v:
                continue
            kn[k] = v
            sem = self.sems[k]
            self.q[engine].append(lambda eng, sem=sem, v=v: eng.wait_ge(sem, v))

    def _commit(self, tok, reads, writes):
        k, v = tok
        for r in reads:
            d = self.readers.setdefault(r, {})
            if d.get(k, 0) < v:
                d[k] = v
        for w in writes:
            self.lastw[w] = tok
            self.readers[w] = {}

    def op(self, engine, fn, reads=(), writes=(), inc=True):
        deps = self._deps(reads, writes)
        self._emit_waits(engine, deps)
        sem = self._sem(engine)
        if inc:
            self.cnt[engine] += 1
            n = self.cnt[engine]
            self.q[engine].append(lambda eng, fn=fn, sem=sem: fn(eng).then_inc(sem, 1))
        else:
            n = self.cnt[engine] + 1
            self.q[engine].append(lambda eng, fn=fn: fn(eng))
        self._commit((engine, n), reads, writes)
        self.n_inst += 1

    def dma(self, stream, fn, reads=(), writes=(), queue="sp"):
        deps = self._deps(reads, writes)
        key = "d" + queue[0] + "_" + stream
        sem = self._sem(key)
        if self.cnt[key] > 0:
            deps[key] = max(deps.get(key, 0), self.cnt[key])
        self._emit_waits(queue, deps)
        self.cnt[key] += 16
        v = self.cnt[key]
        self.q[queue].append(lambda eng, fn=fn, sem=sem: fn(eng).then_inc(sem, 16))
        self._commit((key, v), reads, writes)
        self.n_inst += 1

    def barrier(self):
        deps = {k: self.cnt[k] for k in self.sems if self.cnt[k] > 0}
        for e in COMPUTE + ("sp",):
            self._emit_waits(e, dict(deps))

    def finish(self):
        deps = {k: self.cnt[k] for k in self.sems if self.cnt[k] > 0}
        self._emit_waits("sp", deps)

    def run(self):
        nc = self.nc
        with nc.Block() as block:
            @block.sync
            def _(eng):
                for f in self.q["sp"]:
                    f(eng)

            @block.tensor
            def _(eng):
                for f in self.q["pe"]:
                    f(eng)

            @block.scalar
            def _(eng):
                for f in self.q["act"]:
                    f(eng)

            @block.vector
            def _(eng):
                for f in self.q["dve"]:
                    f(eng)

            @block.gpsimd
            def _(eng):
                for f in self.q["pool"]:
                    f(eng)


def _res(*vs):
    out = []
    for v in vs:
        if v is None or isinstance(v, (int, float)):
            continue
        out.extend(v.res)
    return tuple(out)


def _ap(v):
    return v.ap if isinstance(v, V) else v


class K:
    def __init__(self, NP, T, NS, PAST, TS=64, NB1=256, NQ=512, NB3=512):
        self.NP, self.T, self.NS, self.PAST, self.TS = NP, T, NS, PAST, TS
        self.NB1, self.NQ, self.NB3 = NB1, NQ, NB3
        self.nc = nc = bass.Bass("TRN2", target_bir_lowering=False)
        self.S = Sched(nc)
        self.seqs = [("p", i, T, 0) for i in range(NP)] + [("s", i, TS, PAST) for i in range(NS)]
        self.arena = nc.alloc_sbuf_tensor("arena", [128, 204 * 1024], U8)
        self.off = 0
        self.uid = 0
        self.pb = [nc.alloc_psum_tensor("pb%d" % i, [128, 512], F32) for i in range(8)]
        self.din = {}
        self.dout = {}
        self.dscr = {}

    def sb(self, name, shape, dt, parts=128):
        nbytes = int(np.prod(shape[1:])) * (4 if dt == F32 else 2)
        nbytes = (nbytes + 63) // 64 * 64
        assert self.off + nbytes <= 204 * 1024, (name, self.off, nbytes)
        ap = self.arena[0:shape[0], self.off:self.off + nbytes].bitcast(dt)
        used = int(np.prod(shape[1:]))
        ap = ap[:, 0:used]
        if len(shape) == 3:
            ap = ap.rearrange("p (a b) -> p a b", a=shape[1])
        elif len(shape) == 4:
            ap = ap.rearrange("p (a b c) -> p a b c", a=shape[1], b=shape[2])
        elif len(shape) == 5:
            ap = ap.rearrange("p (a b c d) -> p a b c d", a=shape[1], b=shape[2], c=shape[3])
        self.off += nbytes
        self.uid += 1
        return V(ap, "%s#%d" % (name, self.uid))

    def ps(self, bank, shape, dt=F32, res=None):
        t = self.pb[bank]
        ap = t[0:shape[0], :]
        if dt == BF16:
            ap = ap.bitcast(BF16)
        used = int(np.prod(shape[1:]))
        ap = ap[:, 0:used]
        if len(shape) == 3:
            ap = ap.rearrange("p (a b) -> p a b", a=shape[1])
        elif len(shape) == 4:
            ap = ap.rearrange("p (a b c) -> p a b c", a=shape[1], b=shape[2])
        return V(ap, res or ("pb%d" % bank))

    def inp(self, name, shape, dt=F32):
        ap = self.nc.dram_tensor(name, list(shape), dt, kind="ExternalInput").ap()
        self.din[name] = V(ap, "in_" + name)
        return self.din[name]

    def outp(self, name, shape):
        ap = self.nc.dram_tensor(name, list(shape), F32, kind="ExternalOutput").ap()
        self.dout[name] = V(ap, "out_" + name)
        return self.dout[name]

    def scr(self, name, shape, dt):
        ap = self.nc.dram_tensor(name, list(shape), dt, kind="Internal").ap()
        self.dscr[name] = V(ap, "scr_" + name)
        return self.dscr[name]

    def act(self, out, in_, func, bias=None, scale=None, accum=None):
        kw = {}
        if bias is not None:
            kw["bias"] = _ap(bias)
        if scale is not None:
            kw["scale"] = _ap(scale)
        if accum is not None:
            kw["accum_out"] = accum.ap
        self.S.op("act", lambda e: e.activation(out=out.ap, in_=in_.ap, func=func, **kw),
                  reads=_res(in_, bias, scale), writes=_res(out, accum))

    def ts(self, out, in0, s1, s2, op0, op1=None, eng="dve"):
        kw = {}
        if op1 is not None:
            kw["op1"] = op1
        self.S.op(eng, lambda e: e.tensor_scalar(out=out.ap, in0=in0.ap, scalar1=_ap(s1), scalar2=_ap(s2), op0=op0, **kw),
                  reads=_res(in0, s1, s2), writes=_res(out))

    def tt(self, out, in0, in1, op, eng="dve"):
        self.S.op(eng, lambda e: e.tensor_tensor(out=out.ap, in0=in0.ap, in1=in1.ap, op=op),
                  reads=_res(in0, in1), writes=_res(out))

    def stt(self, out, in0, scalar, in1, op0, op1, eng="dve"):
        self.S.op(eng, lambda e: e.scalar_tensor_tensor(out=out.ap, in0=in0.ap, scalar=_ap(scalar), in1=in1.ap, op0=op0, op1=op1),
                  reads=_res(in0, scalar, in1), writes=_res(out))

    def cp(self, out, in_, eng="dve"):
        if eng == "act":
            self.S.op("act", lambda e: e.copy(out=out.ap, in_=in_.ap), reads=_res(in_), writes=_res(out))
        else:
            self.S.op(eng, lambda e: e.tensor_copy(out=out.ap, in_=in_.ap), reads=_res(in_), writes=_res(out))

    def recip(self, out, in_):
        self.S.op("dve", lambda e: e.reciprocal(out=out.ap, in_=in_.ap), reads=_res(in_), writes=_res(out))

    def red(self, out, in_, eng="dve"):
        self.S.op(eng, lambda e: e.tensor_reduce(out=out.ap, in_=in_.ap, axis=AX.X, op=ALU.add),
                  reads=_res(in_), writes=_res(out))

    def memset(self, out, val, eng="pool"):
        self.S.op(eng, lambda e: e.memset(out.ap, val), writes=_res(out))

    def mm(self, out, lhsT, rhs, start, stop, last=None):
        if last is None:
            last = stop
        self.S.op("pe", lambda e: e.matmul(out.ap, lhsT=lhsT.ap, rhs=rhs.ap, start=start, stop=stop),
                  reads=_res(lhsT, rhs), writes=_res(out), inc=last)

    def tr(self, out, in_, ident, last=True):
        self.S.op("pe", lambda e: e.transpose(out=out.ap, in_=in_.ap, identity=ident.ap),
                  reads=_res(in_, ident), writes=_res(out), inc=last)

    def ld(self, stream, out, in_, slow=False):
        kw = {"allow_slow_non_contiguous": True} if slow else {}
        extra = ("miscq",) if stream == "misc" else ()
        self.S.dma(stream, lambda e: e.dma_start(out=out.ap, in_=in_.ap, **kw), reads=_res(in_), writes=_res(out) + extra, queue="sp")

    def st(self, stream, out, in_, slow=False):
        kw = {"allow_slow_non_contiguous": True} if slow else {}
        self.S.dma(stream, lambda e: e.dma_start(out=out.ap, in_=in_.ap, **kw), reads=_res(in_), writes=_res(out), queue=getattr(self, "st_queue", "pool"))

    def rstd_from_ss(self, rstd, ss, n):
        self.ts(rstd, ss, 1.0 / n, EPS, ALU.mult, ALU.add)
        self.act(rstd, rstd, AF.Sqrt)
        self.recip(rstd, rstd)

    def declare_io(self):
        NP, T, NS, PAST, TS = self.NP, self.T, self.NS, self.PAST, self.TS
        i = self.inp
        i("xp", [NP, T, D]); i("xs", [NS, TS, D])
        i("clat", [NS, PAST, KVL]); i("ckr", [NS, PAST, RD])
        i("sre", [NS, 32, 64]); i("sim", [NS, 32, 64]); i("sconv", [NS, CW - 1, D])
        i("norm_ab", [D]); i("w_in_ab", [D, 2208]); i("g_q_lat", [QL]); i("w_uq", [QL, NH * HD])
        i("g_kv_lat", [KVL]); i("w_uk", [KVL, NH * ND]); i("w_uv", [KVL, NH * VD])
        i("g_q_nope", [ND]); i("g_q_rope", [RD]); i("g_k_nope", [ND]); i("g_k_rope", [RD])
        i("lam_re", [32, 64]); i("lam_im", [32, 64]); i("log_dt", [32])
        i("b_re", [32, 64, 16]); i("b_im", [32, 64, 16]); i("c_re", [32, 16, 64]); i("c_im", [32, 16, 64])
        i("s5_d", [SW]); i("w_glu", [SW, SW]); i("b_glu", [SW]); i("w_out_ab", [D, D])
        i("norm_c", [D]); i("w_in_c", [D, 3 * D]); i("conv_w", [CW, D]); i("conv_b", [D])
        i("ln_g", [D]); i("ln_b", [D]); i("w_out_c", [D, D])
        i("c_ident", [128, 128]); i("c_mask", [128, 128])
        i("c_cos_p", [T, 32]); i("c_sin_p", [T, 32]); i("c_cos_s", [TS, 32]); i("c_sin_s", [TS, 32])
        o = self.outp
        o("yp", [NP, T, D]); o("ys", [NS, TS, D])
        o("lat_p", [NP, T, KVL]); o("kr_p", [NP, T, RD]); o("re_p", [NP, 32, 64]); o("im_p", [NP, 32, 64])
        o("conv_p", [NP, CW - 1, D])
        o("lat_s", [NS, TS, KVL]); o("kr_s", [NS, TS, RD]); o("re_s", [NS, 32, 64]); o("im_s", [NS, 32, 64])
        o("conv_s", [NS, CW - 1, D])
        for si, (kind, idx, Tq, P) in enumerate(self.seqs):
            self.scr("QT%d" % si, [HD, NH, Tq], BF16)
            self.scr("KT%d" % si, [HD, NH, P + Tq], BF16)
            self.scr("VS%d" % si, [P + Tq, NH * VD], BF16)
            self.scr("GM%d" % si, [128, 4, Tq], BF16)
            self.scr("MS%d" % si, [128, 4, Tq], BF16)
            self.scr("X1%d" % si, [Tq, D], F32)
        self.scr("WC", [D, 3 * D], BF16)

    def seq_io(self, si):
        kind, idx, Tq, P = self.seqs[si]
        if kind == "p":
            return dict(x=self.din["xp"][idx], y=self.dout["yp"][idx], lat=self.dout["lat_p"][idx], kr=self.dout["kr_p"][idx],
                        re=self.dout["re_p"][idx], im=self.dout["im_p"][idx], conv=self.dout["conv_p"][idx],
                        cos=self.din["c_cos_p"], sin=self.din["c_sin_p"])
        return dict(x=self.din["xs"][idx], y=self.dout["ys"][idx], lat=self.dout["lat_s"][idx], kr=self.dout["kr_s"][idx],
                    re=self.dout["re_s"][idx], im=self.dout["im_s"][idx], conv=self.dout["conv_s"][idx],
                    cos=self.din["c_cos_s"], sin=self.din["c_sin_s"],
                    clat=self.din["clat"][idx], ckr=self.din["ckr"][idx], sre=self.din["sre"][idx], sim=self.din["sim"][idx],
                    sconv=self.din["sconv"][idx])

    def load_weight(self, dst, src, Kd, c0, c1, gain=None, stage=None):
        nk = Kd // 128
        n = c1 - c0
        srcv = src.r("(k p) n -> p k n", p=128)
        for k in range(nk):
            stg = stage[self._stg % 2]
            self._stg += 1
            self.ld("wstg%d" % (self._stg % 2), stg[:, 0:n], srcv[:, k, c0:c1])
            eng = "dve" if (self._stg % 2) else "pool"
            if gain is not None:
                self.ts(dst[:, k, :], stg[:, 0:n], gain[:, k:k + 1], None, ALU.mult, eng=eng)
            else:
                self.cp(dst[:, k, :], stg[:, 0:n], eng=eng)

    def load_vec_pk(self, name, src, Kd):
        v = self.sb(name, [128, Kd // 128], F32)
        self.ld("misc", v, src.r("(k p) -> p k", p=128), slow=True)
        return v

    def load_vec_bc(self, name, src, n, parts=128):
        v = self.sb(name, [parts, n], F32)
        self.ld("misc", v, V(src.ap.partition_broadcast(parts), src.res))
        return v

    def build(self):
        self.declare_io()
        self._stg = 0
        self.phase_const()
        self.phase1()
        self.st_queue = "pool"
        self.S.barrier()
        self.phase1b()
        self.S.barrier()
        if getattr(self, "STOP", 9) >= 2:
            self.phase2()
            self.S.barrier()
        if getattr(self, "STOP", 9) >= 3:
            self.phase3()
        self.S.finish()
        self.S.run()
        return self.nc

    def phase_const(self):
        self.identF = self.sb("identF", [128, 128], F32)
        self.ld("misc", self.identF, self.din["c_ident"])
        self.identB = self.sb("identB", [128, 128], BF16)
        self.cp(self.identB, self.identF)
        self.onesB = self.sb("onesB", [128, 128], BF16)
        self.memset(self.onesB, 1.0)
        self.onesF = self.sb("onesF", [128, 128], F32)
        self.memset(self.onesF, 1.0)
        self.base_off = self.off

    def s5_setup(self):
        d = self.din
        mark = None
        self.BkT = self.sb("BkT", [128, 32, 2, 64], BF16)
        self.Cq = self.sb("Cq", [128, 32, 128], BF16)
        self.T0 = self.sb("T0", [128, 32, 128], BF16)
        self.lamA = self.sb("lamA", [64, 2, 32], F32)
        self.lamAn = self.sb("lamAn", [64, 32], F32)
        self.LPr = self.sb("LPr", [64, 32, 8], F32)
        self.LPpm = self.sb("LPpm", [64, 2, 32, 8], F32)
        self.s5_persist_end = self.off
        if hasattr(self, "_p1_weights"):
            self._p1_weights()
        keep = self.off
        P = 64
        lre = self.sb("lre", [P, 32], F32); lim = self.sb("lim", [P, 32], F32)
        self.ld("misc", lre, d["lam_re"].r("g p -> p g"), slow=True)
        self.ld("misc", lim, d["lam_im"].r("g p -> p g"), slow=True)
        dt = self.load_vec_bc("dt", d["log_dt"], 32, parts=P)
        self.act(dt, dt, AF.Exp)
        ar = self.sb("ar", [P, 32], F32); ai = self.sb("ai", [P, 32], F32)
        self.tt(ar, lre, dt, ALU.mult); self.tt(ai, lim, dt, ALU.mult)
        NPW = 16
        pwr = self.sb("pwr", [P, 32, NPW], F32); pwi = self.sb("pwi", [P, 32, NPW], F32)
        mag = self.sb("mag", [P, 32], F32); ang = self.sb("ang", [P, 32], F32); kk = self.sb("kk", [P, 32], F32)
        sn = self.sb("sn", [P, 32], F32); cs = self.sb("cs", [P, 32], F32)
        for mi in range(NPW):
            m = float(mi - 7)
            self.act(mag, ar, AF.Exp, scale=m)
            for (dst, shift) in ((sn, 0.0), (cs, math.pi / 2)):
                self.ts(ang, ai, m, shift, ALU.mult, ALU.add)
                self.ts(kk, ang, 1.0 / TWO_PI, MAGIC, ALU.mult, ALU.add)
                self.ts(kk, kk, -MAGIC, -TWO_PI, ALU.add, ALU.mult)
                self.tt(ang, ang, kk, ALU.add)
                self.act(dst, ang, AF.Sin)
            self.tt(pwr[:, :, mi], mag, cs, ALU.mult)
            self.tt(pwi[:, :, mi], mag, sn, ALU.mult)
        for i in range(8):
            m = 8.0 * (i + 1)
            self.act(mag, ar, AF.Exp, scale=m)
            for (dst, shift) in ((sn, 0.0), (cs, math.pi / 2)):
                self.ts(ang, ai, m, shift, ALU.mult, ALU.add)
                self.ts(kk, ang, 1.0 / TWO_PI, MAGIC, ALU.mult, ALU.add)
                self.ts(kk, kk, -MAGIC, -TWO_PI, ALU.add, ALU.mult)
                self.tt(ang, ang, kk, ALU.add)
                self.act(dst, ang, AF.Sin)
            self.tt(self.LPr[:, :, i], mag, cs, ALU.mult)
            self.tt(self.LPpm[:, 1, :, i], mag, sn, ALU.mult)
            self.ts(self.LPpm[:, 0, :, i], self.LPpm[:, 1, :, i], -1.0, None, ALU.mult)
        self.cp(self.lamA[:, 0, :], pwr[:, :, 15]); self.cp(self.lamA[:, 1, :], pwi[:, :, 15])
        self.ts(self.lamAn, pwi[:, :, 15], -1.0, None, ALU.mult)
        nr = self.sb("nr", [P, 32], F32); den = self.sb("den", [P, 32], F32); t1 = self.sb("t1", [P, 32], F32); t2 = self.sb("t2", [P, 32], F32)
        kre = self.sb("kre", [P, 32], F32); kim = self.sb("kim", [P, 32], F32)
        self.ts(nr, pwr[:, :, 8], -1.0, None, ALU.add)
        ni = pwi[:, :, 8]
        self.tt(den, lre, lre, ALU.mult); self.tt(t1, lim, lim, ALU.mult); self.tt(den, den, t1, ALU.add); self.recip(den, den)
        self.tt(t1, nr, lre, ALU.mult); self.tt(t2, ni, lim, ALU.mult); self.tt(t1, t1, t2, ALU.add); self.tt(kre, t1, den, ALU.mult)
        self.tt(t1, ni, lre, ALU.mult); self.tt(t2, nr, lim, ALU.mult); self.tt(t1, t1, t2, ALU.subtract); self.tt(kim, t1, den, ALU.mult)
        bre = self.sb("bre", [P, 32, 16], F32); bim = self.sb("bim", [P, 32, 16], F32)
        self.ld("misc", bre, d["b_re"].r("g p n -> p g n")); self.ld("misc", bim, d["b_im"].r("g p n -> p g n"))
        bbr = self.sb("bbr", [P, 32, 16], F32); bbi = self.sb("bbi", [P, 32, 16], F32); tb = self.sb("tb", [P, 32, 16], F32)
        kreb = kre.us(2).bc([P, 32, 16]); kimb = kim.us(2).bc([P, 32, 16])
        self.tt(bbr, bre, kreb, ALU.mult); self.tt(tb, bim, kimb, ALU.mult); self.tt(bbr, bbr, tb, ALU.subtract)
        self.tt(bbi, bim, kreb, ALU.mult); self.tt(tb, bre, kimb, ALU.mult); self.tt(bbi, bbi, tb, ALU.add)
        cre = self.sb("cre", [P, 32, 16], F32); cim = self.sb("cim", [P, 32, 16], F32)
        self.ld("misc", cre, d["c_re"].r("g n p -> p g n"), slow=True); self.ld("misc", cim, d["c_im"].r("g n p -> p g n"), slow=True)
        Bkr = self.sb("Bkr", [P, 32, 8, 16], F32); Bki = self.sb("Bki", [P, 32, 8, 16], F32); tB = self.sb("tB", [P, 32, 16], F32)
        for s in range(8):
            pr = pwr[:, :, 14 - s].us(2).bc([P, 32, 16]); pi_ = pwi[:, :, 14 - s].us(2).bc([P, 32, 16])
            self.tt(Bkr[:, :, s, :], bbr, pr, ALU.mult); self.tt(tB, bbi, pi_, ALU.mult); self.tt(Bkr[:, :, s, :], Bkr[:, :, s, :], tB, ALU.subtract)
            self.tt(Bki[:, :, s, :], bbr, pi_, ALU.mult); self.tt(tB, bbi, pr, ALU.mult); self.tt(Bki[:, :, s, :], Bki[:, :, s, :], tB, ALU.add)
        CLr = self.sb("CLr", [P, 32, 8, 16], F32); CLi = self.sb("CLi", [P, 32, 8, 16], F32); tC = self.sb("tC", [P, 32, 16], F32)
        tCr = self.sb("tCr", [P, 32, 16], F32); tCi = self.sb("tCi", [P, 32, 16], F32)
        Cq4r = self.Cq[0:64].r("p g (s n) -> p g s n", s=8); Cq4i = self.Cq[64:128].r("p g (s n) -> p g s n", s=8)
        for mi in range(NPW):
            pr = pwr[:, :, mi].us(2).bc([P, 32, 16]); pi_ = pwi[:, :, mi].us(2).bc([P, 32, 16])
            dr = CLr[:, :, mi, :] if mi < 8 else tCr
            di = CLi[:, :, mi, :] if mi < 8 else tCi
            self.tt(dr, cre, pr, ALU.mult); self.tt(tC, cim, pi_, ALU.mult); self.tt(dr, dr, tC, ALU.subtract)
            self.tt(di, cre, pi_, ALU.mult); self.tt(tC, cim, pr, ALU.mult); self.tt(di, di, tC, ALU.add)
            self.ts(di, di, -1.0, None, ALU.mult)
            if mi >= 8:
                self.cp(Cq4r[:, :, mi - 8, :], tCr)
                self.cp(Cq4i[:, :, mi - 8, :], tCi)
        maskS = self.sb("maskS", [128, 128], F32)
        self.ld("misc", maskS, d["c_mask"])
        dsel = self.sb("dsel", [128, 32], F32)
        for s in range(8):
            self.ld("misc", dsel[s * 16:(s + 1) * 16, :], d["s5_d"].r("(g n) -> n g", n=16), slow=True)
        diagd = self.sb("diagd", [128, 128], F32)
        for g in range(32):
            bk = g % 2
            pT = self.ps(bk, [128, 2, 64], F32)
            self.tr(pT[:, 0, :], Bkr[:, g, :, :].r("p s n -> p (s n)"), self.identF[0:64, 0:64], last=False)
            self.tr(pT[:, 1, :], Bki[:, g, :, :].r("p s n -> p (s n)"), self.identF[0:64, 0:64], last=True)
            self.cp(self.BkT[:, g, :, :], pT, eng="act")
            p0 = self.ps(2 + bk, [128, 128], F32)
            self.mm(p0, Bkr[:, g, :, :].r("p s n -> p (s n)"), CLr[:, g, 0:8, :].r("p s n -> p (s n)"), True, False)
            self.mm(p0, Bki[:, g, :, :].r("p s n -> p (s n)"), CLi[:, g, 0:8, :].r("p s n -> p (s n)"), False, True)
            self.ts(diagd, self.identF, dsel[:, g:g + 1], None, ALU.mult, eng="pool")
            t0f = self.sb("t0f", [128, 128], F32) if g == 0 else t0f
            self.tt(t0f, p0, maskS, ALU.mult)
            self.tt(self.T0[:, g, :], t0f, diagd, ALU.add)
        self.s5_tmp_end = self.off
        return keep

    def phase1(self):
        d = self.din
        NB = self.NB1
        W_ = {}

        def p1_weights():
            stage = [self.sb("wstgA", [128, 1024], F32), self.sb("wstgB", [128, 1024], F32)]
            W_["stage"] = stage
            g_norm = self.load_vec_pk("g_norm", d["norm_ab"], D)
            g_ql = self.load_vec_pk("g_ql", d["g_q_lat"], QL)
            self.ts(g_ql, g_ql, math.sqrt(QL), None, ALU.mult)
            for nm, shape in (("wA", [128, 8, 672]), ("wGm", [128, 8, 512]), ("wU", [128, 8, 512]), ("wGs", [128, 8, 512]),
                              ("wuq", [128, 3, 768]), ("wuk", [128, 2, 512]), ("wuv", [128, 2, 512]), ("wglu", [128, 4, 512])):
                W_[nm] = self.sb(nm, shape, BF16)
            self.load_weight(W_["wA"], d["w_in_ab"], D, 0, 672, g_norm, stage)
            self.load_weight(W_["wGm"], d["w_in_ab"], D, 672, 1184, g_norm, stage)
            self.load_weight(W_["wU"], d["w_in_ab"], D, 1184, 1696, g_norm, stage)
            self.load_weight(W_["wGs"], d["w_in_ab"], D, 1696, 2208, g_norm, stage)
            self.load_weight(W_["wuq"], d["w_uq"], QL, 0, 768, g_ql, stage)
            self.load_weight(W_["wuk"], d["w_uk"], KVL, 0, 512, None, stage)
            self.load_weight(W_["wuv"], d["w_uv"], KVL, 0, 512, None, stage)
            self.load_weight(W_["wglu"], d["w_glu"], SW, 0, 512, None, stage)
        self._p1_weights = p1_weights
        keep = self.s5_setup()
        self.S.barrier()
        self.off = keep
        wA, wGm, wU, wGs, wuq, wuk, wuv, wglu = (W_[k] for k in ("wA", "wGm", "wU", "wGs", "wuq", "wuk", "wuv", "wglu"))
        xt = W_["stage"]
        if False:
            g_norm = None
        bglu = self.load_vec_pk("bglu", d["b_glu"], SW)
        gkv = self.load_vec_bc("gkv", d["g_kv_lat"], KVL)
        gkr = self.load_vec_bc("gkr", d["g_k_rope"], RD)
        gq = self.sb("gq", [128, HD], F32)
        self.ld("misc", gq[:, 0:ND], V(d["g_q_nope"].ap.partition_broadcast(128), d["g_q_nope"].res))
        self.ld("misc", gq[:, ND:HD], V(d["g_q_rope"].ap.partition_broadcast(128), d["g_q_rope"].res))
        gkn = self.load_vec_bc("gkn", d["g_k_nope"], ND)
        self.ts(gkv, gkv, math.sqrt(KVL), None, ALU.mult)
        self.ts(gkr, gkr, math.sqrt(RD), None, ALU.mult)
        self.ts(gq[:, 0:ND], gq[:, 0:ND], math.sqrt(ND), None, ALU.mult)
        self.ts(gq[:, ND:HD], gq[:, ND:HD], math.sqrt(RD), None, ALU.mult)
        nepsA = self.sb("nepsA", [128, 8], F32)
        self.memset(nepsA, KVL * EPS)
        self.memset(nepsA[:, 1:2], QL * EPS)
        self.memset(nepsA[:, 4:5], RD * EPS)
        nepsQ = self.sb("nepsQ", [128, 16], F32)
        self.memset(nepsQ[:, 0:8], ND * EPS)
        self.memset(nepsQ[:, 8:16], RD * EPS)

        def two(name, shape, dt, parts=128):
            return [self.sb(name + str(i), [parts] + shape, dt) for i in range(2)]
        TB = dict(
            hb=two("hb", [D], BF16), ss=two("ss", [8], F32), rs=two("rs", [8], F32), ssq=two("ssq", [16], F32), rsq=two("rsq", [16], F32),
            ssk=two("ssk", [8], F32), rsk=two("rsk", [8], F32), qlb=two("qlb", [QL], BF16), qlT=two("qlT", [3, 128], BF16),
            latf=two("latf", [KVL], F32), latb=two("latb", [KVL], BF16), latT=two("latT", [2, 128], BF16),
            krf=two("krf", [RD], F32), krn=two("krn", [RD], F32), cosT=two("cosT", [32], F32), sinT=two("sinT", [32], F32),
            r1=two("r1", [NH, 32], F32), r2=two("r2", [NH, 32], F32), r1k=two("r1k", [1, 32], F32), r2k=two("r2k", [1, 32], F32), qn=two("qn", [NH, HD], F32), sqq=two("sqq", [NH, HD], BF16),
            Qs=two("Qs", [NH, HD], BF16), Ks=two("Ks", [NH, HD], BF16), knf=two("knf", [NH, ND], F32), sqk=two("sqk", [NH, ND], BF16),
            vb=two("vb", [512], BF16))
        hTs = [self.sb("hT%d" % i, [128, 8, NB], BF16) for i in range(2)]
        KTc = [self.sb("KTc%d" % i, [HD, NH, 128], BF16) for i in range(2)]
        QTb = self.sb("QTb", [HD, NH, NB], BF16)
        KTb = self.sb("KTb", [HD, NH, NB], BF16)
        GMt = self.sb("GMt", [128, 4, NB], BF16)
        gsT = [self.sb("gsT%d" % i, [128, 4, NB], BF16) for i in range(3)]
        J = NB // 8
        F1 = getattr(self, "F1", True)
        X2s = self.sb("X2s", [64, 16, 8, 16], BF16) if F1 else self.sb("X2s", [J, 32, 8, 16], BF16)
        Usel = [self.sb("Usel%d" % i, [128, 32, J], BF16) for i in range(3)]
        Sb = [self.sb("Sb%d" % i, [64, 2, 32, J], F32) for i in range(2)]
        Sp = [self.sb("Sp%d" % i, [128, 32, J], BF16) for i in range(2)]
        carry = self.sb("carry", [64, 2, 32], F32)
        sq1 = self.sb("sq1", [64, 2, 32, 4], F32); sq2 = self.sb("sq2", [64, 2, 32, 4], F32)
        sl1 = self.sb("sl1", [64, 2, 32, 8], F32); sl2 = self.sb("sl2", [64, 2, 32, 8], F32)
        Z2 = self.sb("Z2", [64, 8, 16, 16], BF16) if F1 else self.sb("Z2", [J, 8, 32, 16], BF16)
        yz = self.sb("yz", [J, 4, 128], F32); yz2 = self.sb("yz2", [J, 4, 128], F32)
        zT = self.sb("zT", [128, 4, NB], BF16)
        sig = self.sb("sig", [128, NB], F32)
        MSt = self.sb("MSt", [128, 4, NB], BF16)
        self.p1_end = self.off

        lamre_b = self.lamA[:, 0, :].us(1).bc([64, 2, 32])
        lam_pm = self.sb("lam_pm", [64, 2, 32], F32)
        self.cp(lam_pm[:, 0, :], self.lamAn); self.cp(lam_pm[:, 1, :], self.lamA[:, 1, :])

        st_ = dict(xi=0, tc=0)

        def tile_front(io, TT, tk0, ti, VSd, P, bk2, hT):
            b0, b1 = bk2
            xi = st_["xi"]; st_["xi"] += 1
            tp = st_["tc"] % 2; st_["tc"] += 1
            B = {k: v[tp] for k, v in TB.items()}
            x = xt[xi % 2]
            ss, rs = B["ss"], B["rs"]
            self.ld("xt%d" % (xi % 2), x[0:TT, :], io["x"][tk0:tk0 + TT, :])
            self.ld("cos%d" % tp, B["cosT"][0:TT, :], io["cos"][tk0:tk0 + TT, :])
            self.ld("sin%d" % tp, B["sinT"][0:TT, :], io["sin"][tk0:tk0 + TT, :])
            self.act(B["hb"][0:TT, :], x[0:TT, :], AF.Square, accum=ss[0:TT, 0:1])
            self.rstd_from_ss(rs[0:TT, 0:1], ss[0:TT, 0:1], D)
            self.ts(B["hb"][0:TT, :], x[0:TT, :], rs[0:TT, 0:1], None, ALU.mult)
            yield
            pT = self.ps(b0, [128, 8, TT], BF16)
            for k in range(8):
                self.tr(pT[:, k, :], B["hb"][0:TT, k * 128:(k + 1) * 128], self.identB[0:TT, 0:TT], last=(k == 7))
            self.cp(hT[:, :, ti * TT:(ti + 1) * TT], pT, eng="act")
            yield
            pA = self.ps(b0, [TT, 512], F32); pB = self.ps(b1, [TT, 160], F32)
            for k in range(8):
                self.mm(pA, hT[:, k, ti * TT:(ti + 1) * TT], wA[:, k, 0:512], k == 0, k == 7)
            for k in range(8):
                self.mm(pB, hT[:, k, ti * TT:(ti + 1) * TT], wA[:, k, 512:672], k == 0, k == 7)
            yield
            trash = B["sqq"][0:TT].r("p h d -> p (h d)")
            self.act(trash[:, 0:QL], pA[:, 0:QL], AF.Square, accum=ss[0:TT, 1:2])
            self.act(trash[:, 0:128], pA[:, QL:512], AF.Square, accum=ss[0:TT, 2:3])
            self.act(trash[:, 0:128], pB[:, 0:128], AF.Square, accum=ss[0:TT, 3:4])
            self.act(trash[:, 0:RD], pB[:, 128:160], AF.Square, accum=ss[0:TT, 4:5])
            self.tt(ss[0:TT, 2:3], ss[0:TT, 2:3], ss[0:TT, 3:4], ALU.add)
            self.tt(rs[0:TT, 1:5], ss[0:TT, 1:5], nepsA[0:TT, 1:5], ALU.add)
            self.act(rs[0:TT, 1:5], rs[0:TT, 1:5], AF.Sqrt)
            self.recip(rs[0:TT, 1:5], rs[0:TT, 1:5])
            yield
            self.ts(B["qlb"][0:TT, :], pA[:, 0:QL], rs[0:TT, 1:2], None, ALU.mult)
            lf = B["latf"]
            self.stt(lf[0:TT, 0:128], pA[:, QL:512], rs[0:TT, 2:3], gkv[0:TT, 0:128], ALU.mult, ALU.mult)
            self.stt(lf[0:TT, 128:256], pB[:, 0:128], rs[0:TT, 2:3], gkv[0:TT, 128:256], ALU.mult, ALU.mult)
            self.st("latf%d" % tp, io["lat"][tk0:tk0 + TT, :], lf[0:TT, :])
            self.stt(B["krn"][0:TT, :], pB[:, 128:160], rs[0:TT, 4:5], gkr[0:TT, :], ALU.mult, ALU.mult)
            yield
            def q_chain():
                pQT = self.ps(b0, [128, 3, TT], BF16)
                for k in range(3):
                    self.tr(pQT[:, k, :], B["qlb"][0:TT, k * 128:(k + 1) * 128], self.identB[0:TT, 0:TT], last=(k == 2))
                self.cp(B["qlT"][:, :, 0:TT], pQT, eng="act")
                yield
                qn = B["qn"]
                qf = qn[0:TT].r("p h d -> p (h d)")
                pQ1 = self.ps(b0, [TT, 512], F32)
                for k in range(3):
                    self.mm(pQ1, B["qlT"][:, k, 0:TT], wuq[:, k, 0:512], k == 0, k == 2)
                self.cp(qf[:, 0:512], pQ1, eng="act")
                yield
                pQ2 = self.ps(b0, [TT, 256], F32)
                for k in range(3):
                    self.mm(pQ2, B["qlT"][:, k, 0:TT], wuq[:, k, 512:768], k == 0, k == 2)
                self.cp(qf[:, 512:768], pQ2, eng="act")
                yield
                sqv = B["sqq"][0:TT]
                self.act(sqv.r("p h d -> p (h d)"), qf, AF.Square)
                self.red(B["ssq"][0:TT, 0:8], sqv[:, :, 0:ND])
                self.red(B["ssq"][0:TT, 8:16], sqv[:, :, ND:HD])
                yield
                self.tt(B["rsq"][0:TT, :], B["ssq"][0:TT, :], nepsQ[0:TT, :], ALU.add)
                self.act(B["rsq"][0:TT, :], B["rsq"][0:TT, :], AF.Sqrt)
                self.recip(B["rsq"][0:TT, :], B["rsq"][0:TT, :])
                yield
                self.tt(qn[0:TT, :, 0:ND], qn[0:TT, :, 0:ND], B["rsq"][0:TT, 0:8].us(2).bc([TT, NH, ND]), ALU.mult)
                self.tt(qn[0:TT, :, ND:HD], qn[0:TT, :, ND:HD], B["rsq"][0:TT, 8:16].us(2).bc([TT, NH, RD]), ALU.mult)
                self.tt(qn[0:TT], qn[0:TT], gq[0:TT].us(1).bc([TT, NH, HD]), ALU.mult)
                yield
                self.cp(B["Qs"][0:TT, :, 0:ND], qn[0:TT, :, 0:ND], eng="act")
                self.rope(B["Qs"][0:TT, :, ND:HD], qn[0:TT, :, ND:HD], B["cosT"], B["sinT"], B["r1"], B["r2"], TT, NH)
                yield
                pq = self.ps(b0, [HD, NH, TT], BF16)
                for h in range(NH):
                    self.tr(pq[:, h, :], B["Qs"][0:TT, h, :], self.identB[0:TT, 0:TT], last=(h == NH - 1))
                self.cp(QTb[:, :, ti * TT:(ti + 1) * TT], pq, eng="act")

            def k_chain():
                kf = B["krf"]
                self.rope(kf[0:TT, :].r("p (o r) -> p o r", o=1), B["krn"][0:TT, :].r("p (o r) -> p o r", o=1), B["cosT"], B["sinT"], B["r1k"], B["r2k"], TT, 1)
                self.st("krf%d" % tp, io["kr"][tk0:tk0 + TT, :], kf[0:TT, :])
                self.cp(B["Ks"][0:TT, :, ND:HD], kf[0:TT, :].us(1).bc([TT, NH, RD]))
                yield
                yield from self.kv_from_lat(lf, TT, B, wuk, wuv, gkn, KTb, ti * TT, (b1, b1, b1, b1))
                self.st("vb%d" % tp, VSd[P + tk0:P + tk0 + TT, :], B["vb"][0:TT, :])

            gens = [q_chain(), k_chain()]
            alive = [True, True]
            while any(alive):
                for gi in range(2):
                    if alive[gi]:
                        try:
                            next(gens[gi])
                        except StopIteration:
                            alive[gi] = False
                yield

        def cache_tile(io, kt_, KTd, VSd, tp, banks):
            B = {k: v[tp] for k, v in TB.items()}
            lf = B["latf"]; kf = B["krf"]
            self.ld("latf%d" % tp, lf, io["clat"][kt_ * 128:(kt_ + 1) * 128, :])
            self.ld("krf%d" % tp, kf, io["ckr"][kt_ * 128:(kt_ + 1) * 128, :])
            self.cp(B["Ks"][:, :, ND:HD], kf.us(1).bc([128, NH, RD]))
            kb = KTc[tp]
            yield
            yield from self.kv_from_lat(lf, 128, B, wuk, wuv, gkn, kb, 0, banks)
            self.st("KTc%d" % tp, KTd[:, :, kt_ * 128:(kt_ + 1) * 128], kb[:, :, 0:128])
            self.st("vb%d" % tp, VSd[kt_ * 128:(kt_ + 1) * 128, :], B["vb"])


        def stageA_pre(blk):
            si, b = blk["si"], blk["b"]
            kind, idx, Tq, P = self.seqs[si]
            io = self.seq_io(si)
            KTd, VSd = self.dscr["KT" + str(si)], self.dscr["VS" + str(si)]
            nb = blk["nb"]; TT = min(128, nb); ntile = nb // TT
            t0 = b * nb
            if False:
                yield
            blk["tile_gens"] = [tile_front(io, TT, t0 + ti * TT, ti, VSd, P, ((0, 1), (2, 3))[ti % 2], hTs[blk["r"] % 2]) for ti in range(ntile)]

        def tiles_done(blk):
            si, b, nb = blk["si"], blk["b"], blk["nb"]
            kind, idx, Tq, P = self.seqs[si]
            t0 = b * nb
            QTd, KTd = self.dscr["QT" + str(si)], self.dscr["KT" + str(si)]
            self.st("QTb", QTd[:, :, t0:t0 + nb], QTb[:, :, 0:nb])
            self.st("KTb", KTd[:, :, P + t0:P + t0 + nb], KTb[:, :, 0:nb])

        def stageA2(blk):
            si, b, slot3, slot2 = blk["si"], blk["b"], blk["r"] % 3, blk["r"] % 2
            GMd = self.dscr["GM" + str(si)]
            nb = blk["nb"]; Jb = nb // 8; t0 = b * nb
            hT = hTs[slot2]
            for (wG, dst) in ((wGm, GMt), (wGs, gsT[slot3])):
                for m in range(4):
                    pg = self.ps(4 + (m % 2), [128, nb], F32)
                    for k in range(8):
                        self.mm(pg, wG[:, k, m * 128:(m + 1) * 128], hT[:, k, 0:nb], k == 0, k == 7)
                    self.act(dst[:, m, 0:nb], pg, AF.Silu)
                    yield
            self.st("GMt", GMd[:, :, t0:t0 + nb], GMt[:, :, 0:nb])
            for s_ in range(8):
                pu = self.ps(4 + (s_ % 2), [Jb, 512], F32)
                for k in range(8):
                    self.mm(pu, hT[:, k, s_:nb:8], wU[:, k, :], k == 0, k == 7)
                e_ = "act" if s_ % 2 else "dve"
                if F1:
                    self.cp(X2s[0:Jb, :, s_, :], pu[:, 0:256].r("j (g n) -> j g n", g=16), eng=e_)
                    self.cp(X2s[32:32 + Jb, :, s_, :], pu[:, 256:512].r("j (g n) -> j g n", g=16), eng=e_)
                else:
                    self.cp(X2s[0:Jb, :, s_, :], pu.r("j (g n) -> j g n", g=32), eng=e_)
                yield
            for g4 in range(8):
                pus = self.ps(4 + (g4 % 2), [128, 4, Jb], BF16)
                for gg in range(4):
                    g = g4 * 4 + gg
                    p0 = (g // 16) * 32
                    if F1:
                        self.tr(pus[:, gg, :], X2s[p0:p0 + Jb, g % 16, :, :].r("j s n -> j (s n)"), self.identB[p0:p0 + Jb, p0:p0 + Jb], last=(gg == 3))
                    else:
                        self.tr(pus[:, gg, :], X2s[0:Jb, g, :, :].r("j s n -> j (s n)"), self.identB[0:Jb, 0:Jb], last=(gg == 3))
                self.cp(Usel[slot3][:, g4 * 4:(g4 + 1) * 4, 0:Jb], pus, eng=("act" if g4 % 2 else "dve"))
                yield
            for g8 in range(4):
                pw = self.ps(4 + (g8 % 2), [64, 2, 8, Jb], F32)
                for gg in range(8):
                    g = g8 * 8 + gg
                    for c in range(2):
                        self.mm(pw[:, c, gg, :], self.BkT[:, g, c, :], Usel[slot3][:, g, 0:Jb], True, True, last=(gg == 7 and c == 1))
                self.cp(Sb[slot2][:, :, g8 * 8:(g8 + 1) * 8, 0:Jb], pw, eng=("act" if g8 % 2 else "dve"))
                yield

        def stageB(blk):
            si, b, slot2 = blk["si"], blk["b"], blk["r"] % 2
            kind, idx, Tq, P = self.seqs[si]
            io = self.seq_io(si)
            Jb = blk["nb"] // 8
            S_ = Sb[slot2]; Spb = Sp[slot2]
            if b == 0:
                if P > 0:
                    self.ld("carry", carry[:, 0, :], io["sre"].r("g p -> p g"), slow=True)
                    self.ld("carry", carry[:, 1, :], io["sim"].r("g p -> p g"), slow=True)
                else:
                    self.memset(carry, 0.0)
            self.cp(Spb[0:64, :, 0], carry[:, 0, :], eng="pool")
            self.cp(Spb[64:128, :, 0], carry[:, 1, :], eng="pool")
            L = 8; Q = Jb // L
            S5 = S_[:, :, :, 0:Jb].r("p c g (q l) -> p c g q l", l=L)
            a1 = sq1[:, :, :, 0:Q]; a2 = sq2[:, :, :, 0:Q]
            lre_q = self.lamA[:, 0, :].us(1).us(3).bc([64, 2, 32, Q])
            for i in range(1, L):
                prev = S5[:, :, :, :, i - 1]; cur = S5[:, :, :, :, i]
                self.tt(a1, prev, lre_q, ALU.mult, eng="pool")
                self.tt(a2[:, 0], prev[:, 1], lam_pm[:, 0, :].us(2).bc([64, 32, Q]), ALU.mult, eng="pool")
                self.tt(a2[:, 1], prev[:, 0], lam_pm[:, 1, :].us(2).bc([64, 32, Q]), ALU.mult, eng="pool")
                self.tt(a1, a1, a2, ALU.add, eng="pool")
                self.tt(cur, cur, a1, ALU.add, eng="pool")
                yield
            LPr_b = self.LPr.us(1).bc([64, 2, 32, L])
            for q in range(Q):
                C = carry if q == 0 else S5[:, :, :, q - 1, L - 1]
                self.tt(sl1, LPr_b, C.us(3).bc([64, 2, 32, L]), ALU.mult, eng="pool")
                self.tt(sl2[:, 0], self.LPpm[:, 0], C[:, 1].us(2).bc([64, 32, L]), ALU.mult, eng="pool")
                self.tt(sl2[:, 1], self.LPpm[:, 1], C[:, 0].us(2).bc([64, 32, L]), ALU.mult, eng="pool")
                self.tt(sl1, sl1, sl2, ALU.add, eng="pool")
                self.tt(S5[:, :, :, q, :], S5[:, :, :, q, :], sl1, ALU.add, eng="pool")
                yield
            self.cp(carry, S_[:, :, :, Jb - 1], eng="pool")
            if Jb > 1:
                self.cp(Spb[0:64, :, 1:Jb], S_[:, 0, :, 0:Jb - 1], eng="pool")
                self.cp(Spb[64:128, :, 1:Jb], S_[:, 1, :, 0:Jb - 1], eng="pool")
            if b == Tq // blk["nb"] - 1:
                self.st("fin", io["re"].r("g p -> p g"), carry[:, 0, :], slow=True)
                self.st("fin", io["im"].r("g p -> p g"), carry[:, 1, :], slow=True)
            yield

        def stageC(blk):
            si, b, slot3, slot2 = blk["si"], blk["b"], blk["r"] % 3, blk["r"] % 2
            MSd = self.dscr["MS" + str(si)]
            nb = blk["nb"]; Jb = nb // 8; t0 = b * nb
            U_ = Usel[slot3]; Spb = Sp[slot2]; gs_ = gsT[slot3]
            for g4 in range(8):
                py = self.ps(6 + (g4 % 2), [Jb, 4, 128], F32)
                for gg in range(4):
                    g = g4 * 4 + gg
                    self.mm(py[:, gg, :], U_[:, g, 0:Jb], self.T0[:, g, :], True, False)
                    self.mm(py[:, gg, :], Spb[:, g, 0:Jb], self.Cq[:, g, :], False, True, last=(gg == 3))
                self.cp(yz[0:Jb], py, eng="act")
                self.tt(yz2[0:Jb], yz[0:Jb], yz[0:Jb], ALU.mult)
                self.ts(yz2[0:Jb], yz2[0:Jb], 0.044715, 1.0, ALU.mult, ALU.add)
                self.tt(yz2[0:Jb], yz2[0:Jb], yz[0:Jb], ALU.mult)
                self.act(yz2[0:Jb], yz2[0:Jb], AF.Sigmoid, scale=2.0 * math.sqrt(2.0 / math.pi))
                zp0 = (g4 // 4) * 32; zg0 = (g4 % 4) * 4
                if not F1:
                    zp0 = 0; zg0 = g4 * 4
                self.tt(Z2[zp0:zp0 + Jb, :, zg0:zg0 + 4, :].r("j s g n -> j g s n"), yz[0:Jb].r("j g (s n) -> j g s n", s=8),
                        yz2[0:Jb].r("j g (s n) -> j g s n", s=8), ALU.mult)
                yield
            if getattr(self, 'STOPC', 9) < 2:
                return
            for s in range(8):
                if F1:
                    pzA = self.ps(6, [128, 2, Jb], BF16); pzB = self.ps(7, [128, 2, Jb], BF16)
                    for m in range(2):
                        self.tr(pzA[:, m, :], Z2[0:Jb, s, m * 8:(m + 1) * 8, :].r("j g n -> j (g n)"), self.identB[0:Jb, 0:Jb], last=(m == 1))
                    for m in range(2):
                        self.tr(pzB[:, m, :], Z2[32:32 + Jb, s, m * 8:(m + 1) * 8, :].r("j g n -> j (g n)"), self.identB[32:32 + Jb, 32:32 + Jb], last=(m == 1))
                    self.cp(zT[:, 0:2, s:nb:8], pzA, eng="act")
                    self.cp(zT[:, 2:4, s:nb:8], pzB, eng="dve")
                else:
                    pz = self.ps(6 + (s % 2), [128, 4, Jb], BF16)
                    for m in range(4):
                        self.tr(pz[:, m, :], Z2[0:Jb, s, m * 8:(m + 1) * 8, :].r("j g n -> j (g n)"), self.identB[0:Jb, 0:Jb], last=(m == 3))
                    self.cp(zT[:, :, s:nb:8], pz, eng=("act" if s % 2 else "dve"))
                yield
            if getattr(self, 'STOPC', 9) < 3:
                return
            for m in range(4):
                pg = self.ps(6 + (m % 2), [128, nb], F32)
                for k in range(4):
                    self.mm(pg, wglu[:, k, m * 128:(m + 1) * 128], zT[:, k, 0:nb], k == 0, k == 3)
                self.act(sig[:, 0:nb], pg, AF.Sigmoid, bias=bglu[:, m:m + 1])
                self.tt(sig[:, 0:nb], sig[:, 0:nb], zT[:, m, 0:nb], ALU.mult)
                self.tt(MSt[:, m, 0:nb], sig[:, 0:nb], gs_[:, m, 0:nb], ALU.mult)
                yield
            self.st("MSt", MSd[:, :, t0:t0 + nb], MSt[:, :, 0:nb])

        blocks = []
        for si, (kind, idx, Tq, P) in enumerate(self.seqs):
            nb = min(NB, Tq)
            for b in range(Tq // nb):
                blocks.append(dict(si=si, b=b, nb=nb, r=len(blocks)))

        def chain(*gens):
            for g in gens:
                if g is not None:
                    yield from g

        nblk = len(blocks)

        def step(g):
            try:
                return next(g), True
            except StopIteration:
                return None, False

        def get(fn, i):
            return fn(blocks[i]) if 0 <= i < nblk else iter(())

        SEQ_DRV = getattr(self, "SEQ_DRV", False)
        self.st_queue = "sp"
        for t in range(min(nblk + 3, getattr(self, "MAXR", 10 ** 9))):
            if SEQ_DRV:
                skip = getattr(self, "SKIP", "") if t == getattr(self, "MAXR", 10 ** 9) - 1 else ""
                for nm_, g_ in (("C", get(stageC, t - 3)), ("B", get(stageB, t - 2)), ("A2", get(stageA2, t - 1)), ("P", get(stageA_pre, t))):
                    if nm_ in skip.split(","):
                        continue
                    for _ in g_:
                        pass
                if t < nblk and "T" not in skip.split(","):
                    for tg in blocks[t]["tile_gens"]:
                        for _ in tg:
                            pass
                    tiles_done(blocks[t])
                continue
            pre = get(stageA_pre, t)
            others = [get(stageA2, t - 1), get(stageB, t - 2), get(stageC, t - 3)]
            o_alive = [True, True, True]
            p_alive = True
            while p_alive:
                _, p_alive = step(pre)
                for i in range(3):
                    if o_alive[i]:
                        _, o_alive[i] = step(others[i])
            tiles = list(blocks[t]["tile_gens"]) if t < nblk else []
            t_alive = [True] * len(tiles)
            while any(t_alive) or any(o_alive):
                for i, tg in enumerate(tiles):
                    if t_alive[i]:
                        _, t_alive[i] = step(tg)
                for i in range(3):
                    if o_alive[i]:
                        _, o_alive[i] = step(others[i])
            if t < nblk:
                tiles_done(blocks[t])

    def _end_phase1_marker(self):
        pass

    def rope(self, out, in_, cosT, sinT, r1, r2, TT, nh):
        cb = cosT[0:TT, :].us(1).bc([TT, nh, 32])
        a = r1[0:TT, 0:nh, :]; b = r2[0:TT, 0:nh, :]
        self.tt(a, in_, cb, ALU.mult)
        self.tt(b[:, :, 0:16], in_[:, :, 16:32], sinT[0:TT, 0:16].us(1).bc([TT, nh, 16]), ALU.mult)
        self.tt(b[:, :, 16:32], in_[:, :, 0:16], sinT[0:TT, 16:32].us(1).bc([TT, nh, 16]), ALU.mult)
        self.tt(out, a, b, ALU.add)

    def kv_from_lat(self, lf, TT, B, wuk, wuv, gkn, KTdst, col0, banks):
        bl, bk_, bv, bkT = banks
        latb, latT, knf, Ks = B["latb"], B["latT"], B["knf"], B["Ks"]
        self.cp(latb[0:TT, :], lf[0:TT, :])
        pl = self.ps(bl, [128, 2, TT], BF16)
        for k in range(2):
            self.tr(pl[:, k, :], latb[0:TT, k * 128:(k + 1) * 128], self.identB[0:TT, 0:TT], last=(k == 1))
        self.cp(latT[:, :, 0:TT], pl, eng="act")
        yield
        pk = self.ps(bk_, [TT, 512], F32)
        for k in range(2):
            self.mm(pk, latT[:, k, 0:TT], wuk[:, k, :], k == 0, k == 1)
        kf = knf[0:TT].r("p h d -> p (h d)")
        self.cp(kf, pk, eng="act")
        sqv = B["sqk"][0:TT]
        self.act(sqv.r("p h d -> p (h d)"), pk, AF.Square)
        yield
        pv = self.ps(bv, [TT, 512], F32)
        for k in range(2):
            self.mm(pv, latT[:, k, 0:TT], wuv[:, k, :], k == 0, k == 1)
        self.cp(B["vb"][0:TT, :], pv, eng="act")
        yield
        self.red(B["ssk"][0:TT, :], sqv)
        self.rstd_from_ss(B["rsk"][0:TT, :], B["ssk"][0:TT, :], ND)
        yield
        self.tt(knf[0:TT], knf[0:TT], B["rsk"][0:TT, :].us(2).bc([TT, NH, ND]), ALU.mult)
        self.tt(Ks[0:TT, :, 0:ND], knf[0:TT], gkn[0:TT].us(1).bc([TT, NH, ND]), ALU.mult)
        yield
        pkT = self.ps(bkT, [HD, NH, TT], BF16)
        for h in range(NH):
            self.tr(pkT[:, h, :], Ks[0:TT, h, :], self.identB[0:TT, 0:TT], last=(h == NH - 1))
        self.cp(KTdst[:, :, col0:col0 + TT], pkT, eng="act")

    def phase1b(self):
        d = self.din
        work = [(si, kt_) for si, (kind, idx, Tq, P) in enumerate(self.seqs) for kt_ in range(P // 128)]
        if not work:
            return
        self.off = self.base_off
        stage = [self.sb("wstgA1b", [128, 1024], F32), self.sb("wstgB1b", [128, 1024], F32)]
        wuk = self.sb("wuk1b", [128, 2, 512], BF16); self.load_weight(wuk, d["w_uk"], KVL, 0, 512, None, stage)
        wuv = self.sb("wuv1b", [128, 2, 512], BF16); self.load_weight(wuv, d["w_uv"], KVL, 0, 512, None, stage)
        gkn = self.load_vec_bc("gkn1b", d["g_k_nope"], ND)
        NT = 8
        slots = []
        for i in range(NT):
            slots.append(dict(
                latf=self.sb("c_latf%d" % i, [128, KVL], F32), krf=self.sb("c_krf%d" % i, [128, RD], F32),
                latb=self.sb("c_latb%d" % i, [128, KVL], BF16), latT=self.sb("c_latT%d" % i, [128, 2, 128], BF16),
                knf=self.sb("c_knf%d" % i, [128, NH, ND], F32), sqk=self.sb("c_sqk%d" % i, [128, NH, ND], BF16),
                Ks=self.sb("c_Ks%d" % i, [128, NH, HD], BF16), vb=self.sb("c_vb%d" % i, [128, 512], BF16),
                ssk=self.sb("c_ssk%d" % i, [128, 8], F32), rsk=self.sb("c_rsk%d" % i, [128, 8], F32),
                KTc=self.sb("c_KTc%d" % i, [HD, NH, 128], BF16)))

        def cache_tile(si, kt_, th):
            B = slots[th]
            io = self.seq_io(si)
            KTd, VSd = self.dscr["KT" + str(si)], self.dscr["VS" + str(si)]
            lf = B["latf"]; kf = B["krf"]
            self.ld("c_latf%d" % th, lf, io["clat"][kt_ * 128:(kt_ + 1) * 128, :])
            self.ld("c_krf%d" % th, kf, io["ckr"][kt_ * 128:(kt_ + 1) * 128, :])
            self.cp(B["Ks"][:, :, ND:HD], kf.us(1).bc([128, NH, RD]))
            yield
            yield from self.kv_from_lat(lf, 128, B, wuk, wuv, gkn, B["KTc"], 0, (th, th, th, th))
            self.st("c_KTc%d" % th, KTd[:, :, kt_ * 128:(kt_ + 1) * 128], B["KTc"])
            self.st("c_vb%d" % th, VSd[kt_ * 128:(kt_ + 1) * 128, :], B["vb"])

        threads = [None] * NT
        nxt = 0
        while nxt < len(work) or any(t is not None for t in threads):
            for th in range(NT):
                if threads[th] is None and nxt < len(work):
                    threads[th] = cache_tile(work[nxt][0], work[nxt][1], th)
                    nxt += 1
                if threads[th] is not None:
                    try:
                        next(threads[th])
                    except StopIteration:
                        threads[th] = None

    def phase2(self):
        d = self.din
        self.off = self.base_off
        NQ = self.NQ
        xt = [self.sb("x2t%d" % i, [128, D], F32) for i in range(2)]
        wo = self.sb("wo", [128, 8, D], BF16)
        self.load_weight(wo, d["w_out_ab"], D, 0, D, None, xt)
        Kmax = max(P + Tq for (_, _, Tq, P) in self.seqs)
        nktmax = (Kmax + 127) // 128
        KTs = self.sb("KTs", [HD, NH, Kmax], BF16)
        Vp = self.sb("Vp", [128, nktmax, NH, 65], BF16)
        self.memset(Vp[:, :, :, 64:65], 1.0)
        nchmax = (Kmax + 511) // 512
        KTc_ = [KTs[:, :, c * 512:min(Kmax, (c + 1) * 512)].named("KTs_c%d" % c) for c in range(nchmax)]
        Vpc_ = [Vp[:, c * 4:min(nktmax, (c + 1) * 4), :, :].named("Vp_c%d" % c) for c in range(nchmax)]
        self.S.barrier()
        QTq = [self.sb("QTq%d" % i, [HD, NH, NQ], BF16) for i in range(2)]
        GMq = [self.sb("GMq%d" % i, [128, 4, NQ], BF16) for i in range(2)]
        mixs = [self.sb("mix%d" % i, [128, 8, NQ], BF16) for i in range(2)]
        NPT = 4
        PT = [self.sb("PT%d" % i, [128, NQ], BF16) for i in range(NPT)]
        rsum = [self.sb("rsum%d" % i, [1, NQ], F32) for i in range(2)]
        bcs = [self.sb("bcs%d" % i, [64, NQ], F32) for i in range(2)]
        stage = [self.sb("wstgA2", [128, 1024], F32), self.sb("wstgB2", [128, 1024], F32)]
        g_c = self.load_vec_pk("g_c", d["norm_c"], D)
        WC = self.dscr["WC"]
        wtmp = [self.sb("wtmp%d" % i, [128, 1024], BF16) for i in range(2)]
        wi = 0
        srcv = d["w_in_c"].r("(k p) n -> p k n", p=128)
        WCv = WC.r("(k p) n -> p k n", p=128)
        for k in range(8):
            for c in range(3):
                stg = stage[wi % 2]; wt = wtmp[wi % 2]
                self.ld("wstg%d" % (wi % 2), stg, srcv[:, k, c * 1024:(c + 1) * 1024])
                self.ts(wt, stg, g_c[:, k:k + 1], None, ALU.mult, eng="dve")
                self.st("wtmp%d" % (wi % 2), WCv[:, k, c * 1024:(c + 1) * 1024], wt)
                wi += 1
        xi = 0
        qbc = 0
        for si, (kind, idx, Tq, P) in enumerate(self.seqs):
            io = self.seq_io(si)
            QTd, KTd, VSd, GMd, MSd, X1d = (self.dscr[n + str(si)] for n in ("QT", "KT", "VS", "GM", "MS", "X1"))
            Ktot = P + Tq
            nkt = (Ktot + 127) // 128
            nch = (Ktot + 511) // 512
            for c in range(nch):
                c0k = c * 512; c1k = min(Ktot, c0k + 512)
                self.ld("KTs_c%d" % c, KTc_[c][:, :, 0:c1k - c0k], KTd[:, :, c0k:c1k])
                for kt_ in range(c * 4, min(nkt, c * 4 + 4)):
                    kk = min(128, Ktot - kt_ * 128)
                    self.ld("Vp_c%d" % c, Vpc_[c][0:kk, kt_ - c * 4, :, 0:64], VSd[kt_ * 128:kt_ * 128 + kk, :].r("k (h d) -> k h d", h=NH))
            nq = min(NQ, Tq)
            TT = min(128, nq)
            for qb in range(Tq // nq):
                q0 = qb * nq
                sl = qbc % 2; qbc += 1
                Qq, Gq, mix = QTq[sl], GMq[sl], mixs[sl]
                self.ld("QTq%d" % sl, Qq[:, :, 0:nq], QTd[:, :, q0:q0 + nq])
                self.ld("GMq%d" % sl, Gq[:, :, 0:nq], GMd[:, :, q0:q0 + nq])
                self.ld("mixms%d" % sl, mix[:, 4:8, 0:nq], MSd[:, :, q0:q0 + nq])
                if kind == "p":
                    tiles = [(kt_, 128, 0, False) for kt_ in range(q0 // 128)]
                    tiles += [(q0 // 128 + m, 128, 128 * m, True) for m in range(nq // 128)]
                else:
                    tiles = [(kt_, min(128, Ktot - kt_ * 128), 0, False) for kt_ in range(nkt)]
                ntl = len(tiles)
                if kind == "s" and NH * nq <= NQ:
                    pOall = self.ps(4, [65, NH, nq], F32)
                    win = []
                    for i in range(ntl + 2):
                        if i < ntl:
                            kt_, kk, c0, diag = tiles[i]
                            pS = self.ps(i % NPT, [128, NH, nq], F32)
                            for h in range(NH):
                                self.mm(pS[0:kk, h, :], KTc_[kt_ // 4][:, h, (kt_ % 4) * 128:(kt_ % 4) * 128 + kk], Qq[:, h, 0:nq], True, True, last=(h == NH - 1))
                            pt = PT[i % NPT]
                            self.act(pt[0:kk, 0:NH * nq], pS[0:kk].r("k h q -> k (h q)"), AF.Exp, scale=ATTN_SCALE)
                            win.append((i, pt))
                        if i >= 2:
                            j, pt = win.pop(0)
                            kt_, kk, c0, diag = tiles[j]
                            for h in range(NH):
                                self.mm(pOall[:, h, :], Vpc_[kt_ // 4][0:kk, kt_ % 4, h, :], pt[0:kk, h * nq:(h + 1) * nq], j == 0 and h == 0, j == ntl - 1 and h == NH - 1, last=(h == NH - 1))
                    self.recip(rsum[0][:, 0:NH * nq], pOall[64:65].r("o h q -> o (h q)"))
                    pBc = self.ps(6, [64, NH * nq], F32)
                    self.mm(pBc, self.onesF[0:1, 0:64], rsum[0][:, 0:NH * nq], True, True)
                    self.cp(bcs[0][:, 0:NH * nq], pBc, eng="act")
                    for h in range(NH):
                        hs = h % 2
                        dst = mix[hs * 64:hs * 64 + 64, h // 2, 0:nq]
                        self.tt(dst, pOall[0:64, h, :], bcs[0][:, h * nq:(h + 1) * nq], ALU.mult)
                        self.tt(dst, dst, Gq[hs * 64:hs * 64 + 64, h // 2, 0:nq], ALU.mult, eng="pool")
                    tiles = []
                    ntl = 0
                items = [(h, ti) for h in range(NH) for ti in range(ntl)]
                pending = []
                LOOK = 3
                window = []
                for i, item in enumerate(items + [None] * LOOK):
                    cur = None
                    if item is not None:
                        h, ti = item
                        kt_, kk, c0, diag = tiles[ti]
                        pS = self.ps(i % NPT, [128, nq], F32)
                        pt = PT[i % NPT]
                        self.mm(pS[0:kk, c0:nq], KTc_[kt_ // 4][:, h, (kt_ % 4) * 128:(kt_ % 4) * 128 + kk], Qq[:, h, c0:nq], True, True)
                        self.act(pt[0:kk, c0:nq], pS[0:kk, c0:nq], AF.Exp, scale=ATTN_SCALE)
                        if diag:
                            self.memset(pt[64:128, c0:c0 + 64], 0.0)
                        cur = (h, ti, pt)
                    window.append(cur)
                    prev = window.pop(0) if len(window) > LOOK else None
                    if prev is not None:
                        h, ti, pt = prev
                        kt_, kk, c0, diag = tiles[ti]
                        pO = self.ps(4 + (h % 2), [65, nq], F32)
                        self.mm(pO[:, c0:nq], Vpc_[kt_ // 4][0:kk, kt_ % 4, h, :], pt[0:kk, c0:nq], ti == 0, ti == ntl - 1)
                        if ti == ntl - 1:
                            hs = h % 2
                            self.recip(rsum[hs][:, 0:nq], pO[64:65, :])

                            def fin(h=h, hs=hs, pO=pO):
                                pBc = self.ps(6 + hs, [64, nq], F32)
                                self.mm(pBc, self.onesF[0:1, 0:64], rsum[hs][:, 0:nq], True, True)
                                self.cp(bcs[hs][:, 0:nq], pBc, eng="act")
                                dst = mix[hs * 64:hs * 64 + 64, h // 2, 0:nq]
                                self.tt(dst, pO[0:64, :], bcs[hs][:, 0:nq], ALU.mult)
                                self.tt(dst, dst, Gq[hs * 64:hs * 64 + 64, h // 2, 0:nq], ALU.mult, eng="pool")
                            pending.append((i + 2, fin))
                    while pending and (pending[0][0] <= i or i == len(items) + LOOK - 1):
                        pending.pop(0)[1]()
                for ti in range(nq // TT):
                    x = xt[xi % 2]; xi += 1
                    self.ld("x2t%d" % (xi % 2), x[0:TT, :], io["x"][q0 + ti * TT:q0 + (ti + 1) * TT, :])
                    for half in range(2):
                        po = self.ps(6 + half, [TT, 512], F32)
                        for k in range(8):
                            self.mm(po, mix[:, k, ti * TT:(ti + 1) * TT], wo[:, k, half * 512:(half + 1) * 512], k == 0, k == 7)
                        self.tt(x[0:TT, half * 512:(half + 1) * 512], x[0:TT, half * 512:(half + 1) * 512], po, ALU.add)
                    self.st("x2t%d" % (xi % 2), X1d[q0 + ti * TT:q0 + (ti + 1) * TT, :], x[0:TT, :])

    def phase3(self):
        d = self.din
        self.off = self.base_off
        NB = self.NB3
        xt = [self.sb("x3t%d" % i, [128, D], F32) for i in range(2)]
        xr = [self.sb("x3r%d" % i, [128, D], F32) for i in range(2)]
        stage = xt
        WC = self.dscr["WC"]
        WCv = WC.r("(k p) n -> p k n", p=128)
        woc = self.sb("woc", [128, 8, D], BF16)
        self.load_weight(woc, d["w_out_c"], D, 0, D, None, stage)
        cb = self.load_vec_pk("cb", d["conv_b"], D)
        lng = self.load_vec_pk("lng", d["ln_g"], D)
        lnb = self.load_vec_pk("lnb", d["ln_b"], D)
        cw = self.sb("cw", [128, 8, CW], F32)
        for k in range(8):
            self.ld("misc", cw[:, k, :], d["conv_w"][:, k * 128:(k + 1) * 128].r("t p -> p t"), slow=True)
        diag = self.sb("diag", [128, 8, CW, 128], BF16)
        for k in range(8):
            for t in range(CW):
                if t % 2:
                    self.ts(diag[:, k, t, :], self.identF, cw[:, k, t:t + 1], None, ALU.mult)
                else:
                    self.act(diag[:, k, t, :], self.identF, AF.Copy, scale=cw[:, k, t:t + 1])
        wch = [self.sb("wch%d" % i, [128, 8, 512], BF16) for i in range(2)]
        H = CW - 1
        junk = self.sb("junk3", [128, D], BF16)
        hb = self.sb("hb3", [128, D], BF16)
        hT = self.sb("hT3", [128, 8, NB], BF16)
        ss = self.sb("ss3", [128, 2], F32); rs = self.sb("rs3", [128, 2], F32)
        vT = self.sb("vT", [128, 8, H + NB], BF16)
        vtails = [self.sb("vtail%d" % i, [128, 8, 32], F32) for i in range(2)]
        gTs = [self.sb("gT%d" % i, [128, 8, NB], BF16) for i in range(2)]
        sgm = [self.sb("sgm%d" % i, [128, NB], F32) for i in range(2)]
        vf = [self.sb("vf%d" % i, [128, NB], F32) for i in range(2)]
        ybf = self.sb("ybf", [128, 8, NB], BF16)
        sqb = self.sb("sqb", [128, 8, NB], BF16)
        mean = self.sb("mean", [128, NB], F32); var = self.sb("var", [128, NB], F32); msq = self.sb("msq", [128, NB], F32)
        tmp = [self.sb("tmp3%d" % i, [128, NB], F32) for i in range(2)]
        halo = self.sb("halo", [128, 8, H], BF16)
        mT = self.sb("mT", [128, 8, NB], BF16)
        tailo = self.sb("tailo", [32, D], F32)
        scv = self.sb("scv", [32, D], F32)
        st_ = dict(wci=0, xi=0, xri=0)

        blocks = []
        for si, (kind, idx, Tq, P) in enumerate(self.seqs):
            nb = min(NB, Tq)
            for b in range(Tq // nb):
                blocks.append(dict(si=si, b=b, nb=nb, r=len(blocks), last=(b == Tq // nb - 1)))

        def wchunk(c0):
            wci = st_["wci"]; st_["wci"] += 1
            w = wch[wci % 2]
            self.ld("wch%d" % (wci % 2), w, WCv[:, :, c0:c0 + 512])
            return w

        def front(blk):
            si, b, nb, r = blk["si"], blk["b"], blk["nb"], blk["r"]
            kind, idx, Tq, P = self.seqs[si]
            io = self.seq_io(si)
            X1d = self.dscr["X1" + str(si)]
            TT = min(128, nb); ntile = nb // TT; t0 = b * nb
            gT = gTs[r % 2]
            if b == 0:
                if kind == "p":
                    self.memset(vT[:, :, 0:H], 0.0)
                else:
                    self.ld("scv", scv[0:H, :], io["sconv"])
                    for k in range(8):
                        pc = self.ps(7, [128, H], F32)
                        self.tr(pc, scv[0:H, k * 128:(k + 1) * 128], self.identF[0:H, 0:H])
                        self.cp(vT[:, k, 0:H], pc, eng="act")
            else:
                self.cp(vT[:, :, 0:H], halo)
            for ti in range(ntile):
                xi = st_["xi"]; st_["xi"] += 1
                x = xt[xi % 2]
                self.ld("x3t%d" % (xi % 2), x[0:TT, :], X1d[t0 + ti * TT:t0 + (ti + 1) * TT, :])
                self.act(junk[0:TT, :], x[0:TT, :], AF.Square, accum=ss[0:TT, 0:1])
                self.rstd_from_ss(rs[0:TT, 0:1], ss[0:TT, 0:1], D)
                self.ts(hb[0:TT, :], x[0:TT, :], rs[0:TT, 0:1], None, ALU.mult)
                pT = self.ps(0, [128, 8, TT], BF16)
                for k in range(8):
                    self.tr(pT[:, k, :], hb[0:TT, k * 128:(k + 1) * 128], self.identB[0:TT, 0:TT], last=(k == 7))
                self.cp(hT[:, :, ti * TT:(ti + 1) * TT], pT, eng="act")
            for half in range(2):
                wa = wchunk(half * 512)
                wb_ = wchunk(1024 + half * 512)
                for mm_ in range(4):
                    m = half * 4 + mm_
                    pa = self.ps(1 + 2 * (m % 2), [128, nb], F32); pb_ = self.ps(2 + 2 * (m % 2), [128, nb], F32)
                    for k in range(8):
                        self.mm(pa, wa[:, k, mm_ * 128:(mm_ + 1) * 128], hT[:, k, 0:nb], k == 0, k == 7)
                    for k in range(8):
                        self.mm(pb_, wb_[:, k, mm_ * 128:(mm_ + 1) * 128], hT[:, k, 0:nb], k == 0, k == 7)
                    sg = sgm[m % 2]; v_ = vf[m % 2]
                    self.act(sg[:, 0:nb], pb_, AF.Sigmoid)
                    self.tt(v_[:, 0:nb], pa, sg[:, 0:nb], ALU.mult)
                    self.cp(vT[:, m, H:H + nb], v_[:, 0:nb], eng="pool")
                    if blk["last"]:
                        self.cp(vtails[r % 2][:, m, :], v_[:, nb - 32:nb], eng="pool")
            for half in range(2):
                wg = wchunk(2048 + half * 512)
                for mm_ in range(4):
                    m = half * 4 + mm_
                    pg = self.ps(5 + (m % 2), [128, nb], F32)
                    for k in range(8):
                        self.mm(pg, wg[:, k, mm_ * 128:(mm_ + 1) * 128], hT[:, k, 0:nb], k == 0, k == 7)
                    self.act(gT[:, m, 0:nb], pg, AF.Silu)

        def conv_stats(blk):
            nb = blk["nb"]
            for m in range(8):
                pc = self.ps(5 + (m % 2), [128, nb], F32)
                for t in range(CW):
                    self.mm(pc, diag[:, m, t, :], vT[:, m, t:t + nb], t == 0, t == CW - 1)
                self.act(ybf[:, m, 0:nb], pc, AF.Identity, bias=cb[:, m:m + 1])
                self.act(sqb[:, m, 0:nb], pc, AF.Square, bias=cb[:, m:m + 1])
            self.cp(halo, vT[:, :, nb:nb + H], eng="pool")
            pst = self.ps(7, [128, nb], F32); pst2 = self.ps(0, [128, nb], F32)
            for m in range(8):
                self.mm(pst, self.onesB, ybf[:, m, 0:nb], m == 0, m == 7)
            for m in range(8):
                self.mm(pst2, self.onesB, sqb[:, m, 0:nb], m == 0, m == 7)
            self.ts(mean[:, 0:nb], pst, 1.0 / D, None, ALU.mult)
            self.tt(msq[:, 0:nb], mean[:, 0:nb], mean[:, 0:nb], ALU.mult)
            self.stt(var[:, 0:nb], pst2, 1.0 / D, msq[:, 0:nb], ALU.mult, ALU.subtract)
            self.ts(var[:, 0:nb], var[:, 0:nb], EPS, None, ALU.add)
            self.act(var[:, 0:nb], var[:, 0:nb], AF.Sqrt)
            self.recip(var[:, 0:nb], var[:, 0:nb])

        def back(blk):
            si, b, nb, r = blk["si"], blk["b"], blk["nb"], blk["r"]
            io = self.seq_io(si)
            X1d = self.dscr["X1" + str(si)]
            TT = min(128, nb); ntile = nb // TT; t0 = b * nb
            gT = gTs[r % 2]
            for m in range(8):
                tm = tmp[m % 2]
                self.tt(tm[:, 0:nb], ybf[:, m, 0:nb], mean[:, 0:nb], ALU.subtract)
                self.tt(tm[:, 0:nb], tm[:, 0:nb], var[:, 0:nb], ALU.mult, eng="pool")
                self.act(tm[:, 0:nb], tm[:, 0:nb], AF.Silu, bias=lnb[:, m:m + 1], scale=lng[:, m:m + 1])
                self.tt(mT[:, m, 0:nb], tm[:, 0:nb], gT[:, m, 0:nb], ALU.mult)

        def outproj(blk):
            si, b, nb, r = blk["si"], blk["b"], blk["nb"], blk["r"]
            io = self.seq_io(si)
            X1d = self.dscr["X1" + str(si)]
            TT = min(128, nb); ntile = nb // TT; t0 = b * nb
            for ti in range(ntile):
                xri = st_["xri"]; st_["xri"] += 1
                x = xr[xri % 2]
                self.ld("x3r%d" % (xri % 2), x[0:TT, :], X1d[t0 + ti * TT:t0 + (ti + 1) * TT, :])
                for half in range(2):
                    po = self.ps(1 + half, [TT, 512], F32)
                    for k in range(8):
                        self.mm(po, mT[:, k, ti * TT:(ti + 1) * TT], woc[:, k, half * 512:(half + 1) * 512], k == 0, k == 7)
                    self.tt(x[0:TT, half * 512:(half + 1) * 512], x[0:TT, half * 512:(half + 1) * 512], po, ALU.add)
                self.st("x3r%d" % (xri % 2), io["y"][t0 + ti * TT:t0 + (ti + 1) * TT, :], x[0:TT, :])
            if blk["last"]:
                for k in range(8):
                    pc = self.ps(7, [32, 128], F32)
                    self.tr(pc, vtails[r % 2][:, k, :], self.identF)
                    self.cp(tailo[:, k * 128:(k + 1) * 128], pc, eng="act")
                self.st("tailo", io["conv"], tailo[2:32, :])

        nblk = len(blocks)
        front(blocks[0])
        for r in range(nblk):
            conv_stats(blocks[r])
            back(blocks[r])
            if r + 1 < nblk:
                front(blocks[r + 1])
            outproj(blocks[r])


def host_consts(T, PAST, TS):
    half = RD // 2
    inv = (10000.0 ** (-np.arange(half, dtype=np.float32) / half)).astype(np.float32)
    def tab(pos):
        ang = pos.astype(np.float32)[:, None] * inv[None, :]
        c_ = np.cos(ang).astype(np.float32); s_ = np.sin(ang).astype(np.float32)
        return np.concatenate([c_, c_], axis=1), np.concatenate([-s_, s_], axis=1)
    cp_, sp_ = tab(np.arange(T))
    cs_, ss_ = tab(PAST + np.arange(TS))
    ident = np.eye(128, dtype=np.float32)
    s_idx = np.arange(128) // 16
    mask = (s_idx[None, :] >= s_idx[:, None]).astype(np.float32)
    return dict(c_ident=ident, c_mask=mask, c_cos_p=cp_, c_sin_p=sp_, c_cos_s=cs_, c_sin_s=ss_)


WEIGHT_MAP = dict(norm_ab="norm_ab", w_in_ab="w_in_ab", g_q_lat="g_q_lat", w_uq="w_uq", g_kv_lat="g_kv_lat",
                  w_uk="w_uk", w_uv="w_uv", g_q_nope="g_q_nope", g_q_rope="g_q_rope", g_k_nope="g_k_nope",
                  g_k_rope="g_k_rope", lam_re="s5_lam_re", lam_im="s5_lam_im", log_dt="s5_log_dt",
                  b_re="s5_b_re", b_im="s5_b_im", c_re="s5_c_re", c_im="s5_c_im", s5_d="s5_d", w_glu="s5_w_glu",
                  b_glu="s5_b_glu", w_out_ab="w_out_ab", norm_c="norm_c", w_in_c="w_in_c", conv_w="conv_w",
                  conv_b="conv_b", ln_g="ln_g", ln_b="ln_b", w_out_c="w_out_c")


def make_in_maps(inputs, n_cores, NP, NS, T, PAST, TS):
    f = lambda a: np.ascontiguousarray(np.asarray(a, dtype=np.float32))
    consts = host_consts(T, PAST, TS)
    w = {}
    for k, src in WEIGHT_MAP.items():
        a = f(inputs[src])[0]
        if k in ("w_uk", "w_uv"):
            a = a.reshape(a.shape[0], -1)
        w[k] = np.ascontiguousarray(a)
    maps = []
    for c in range(n_cores):
        m = dict(w)
        m.update(consts)
        m["xp"] = f(inputs["x_prompt"][c * NP:(c + 1) * NP])
        m["xs"] = f(inputs["x_sample"][c * NS:(c + 1) * NS])
        m["clat"] = f(inputs["cache_mla_latent"][0, c * NS:(c + 1) * NS])
        m["ckr"] = f(inputs["cache_mla_krope"][0, c * NS:(c + 1) * NS])
        m["sre"] = f(inputs["state_s5_re"][0, c * NS:(c + 1) * NS])
        m["sim"] = f(inputs["state_s5_im"][0, c * NS:(c + 1) * NS])
        m["sconv"] = f(inputs["state_conv"][0, c * NS:(c + 1) * NS])
        maps.append(m)
    return maps


def assemble(results):
    cat = lambda k: np.concatenate([r[k] for r in results], axis=0)
    yp, ys = cat("yp"), cat("ys")
    st = lambda k: cat(k)[None]
    return (yp, ys, st("lat_p"), st("kr_p"), st("re_p"), st("im_p"), st("conv_p"),
            st("lat_s"), st("kr_s"), st("re_s"), st("im_s"), st("conv_s"))


def kernel(**inputs):
    n = 8
    B, T = inputs["x_prompt"].shape[0], inputs["x_prompt"].shape[1]
    BS, TS = inputs["x_sample"].shape[0], inputs["x_sample"].shape[1]
    PAST = inputs["cache_mla_latent"].shape[2]
    NP, NS = B // n, BS // n
    kb = K(NP, T, NS, PAST, TS)
    nc = kb.build()
    maps = make_in_maps(inputs, n, NP, NS, T, PAST, TS)
    res = run_bass_kernel_spmd(nc, maps, core_ids=list(range(n)))
    outs = assemble(res.results)
    return tuple(np.asarray(o, dtype=np.float32) for o in outs)
```

```python
import math
import numpy as np
import concourse.bass as bass
import concourse.mybir as mybir
from concourse.bass_utils import run_bass_kernel_spmd

F32 = mybir.dt.float32
BF16 = mybir.dt.bfloat16
U8 = mybir.dt.uint8
ALU = mybir.AluOpType
AF = mybir.ActivationFunctionType
AX = mybir.AxisListType

D = 1024
QL, KVL, RD, MW, SW = 384, 256, 32, 512, 512
NH, ND, VD = 8, 64, 64
HD = ND + RD
CW = 31
EPS = 1e-6
ATTN_SCALE = float(HD ** -0.5)
MAGIC = 12582912.0
TWO_PI = 2.0 * math.pi

COMPUTE = ("pe", "act", "dve", "pool")


class V:
    def __init__(self, ap, res):
        self.ap = ap
        self.res = res if isinstance(res, tuple) else (res,)

    def __getitem__(self, k):
        return V(self.ap[k], self.res)

    def r(self, pat, **kw):
        return V(self.ap.rearrange(pat, **kw), self.res)

    def bc(self, shape):
        return V(self.ap.broadcast_to(list(shape)), self.res)

    def us(self, axis):
        return V(self.ap.unsqueeze(axis), self.res)

    def named(self, *res):
        return V(self.ap, tuple(res))

    @property
    def shape(self):
        return self.ap.shape


class Sched:
    def __init__(self, nc):
        self.nc = nc
        self.q = {e: [] for e in COMPUTE + ("sp",)}
        self.cnt = {}
        self.sems = {}
        self.known = {e: {} for e in COMPUTE + ("sp",)}
        self.lastw = {}
        self.readers = {}
        self.n_inst = 0

    def _sem(self, key):
        if key not in self.sems:
            self.sems[key] = self.nc.alloc_semaphore("s_" + key)
            self.cnt[key] = 0
        return self.sems[key]

    def _deps(self, reads, writes):
        deps = {}

        def add(tok):
            if tok is None:
                return
            k, v = tok
            if deps.get(k, 0) < v:
                deps[k] = v
        for r in reads:
            add(self.lastw.get(r))
        for w in writes:
            add(self.lastw.get(w))
            for k, v in self.readers.get(w, {}).items():
                add((k, v))
        return deps

    def _emit_waits(self, engine, deps):
        kn = self.known[engine]
        for k, v in deps.items():
            if k == engine and engine == "pe":
                continue
            if kn.get(k, 0) >= v:
                continue
            kn[k] = v
            sem = self.sems[k]
            self.q[engine].append(lambda eng, sem=sem, v=v: eng.wait_ge(sem, v))

    def _commit(self, tok, reads, writes):
        k, v = tok
        for r in reads:
            d = self.readers.setdefault(r, {})
            if d.get(k, 0) < v:
                d[k] = v
        for w in writes:
            self.lastw[w] = tok
            self.readers[w] = {}

    def op(self, engine, fn, reads=(), writes=(), inc=True):
        deps = self._deps(reads, writes)
        self._emit_waits(engine, deps)
        sem = self._sem(engine)
        if inc:
            self.cnt[engine] += 1
            n = self.cnt[engine]
            self.q[engine].append(lambda eng, fn=fn, sem=sem: fn(eng).then_inc(sem, 1))
        else:
            n = self.cnt[engine] + 1
            self.q[engine].append(lambda eng, fn=fn: fn(eng))
        self._commit((engine, n), reads, writes)
        self.n_inst += 1

    def dma(self, stream, fn, reads=(), writes=(), queue="sp"):
        deps = self._deps(reads, writes)
        key = "d" + queue[0] + "_" + stream
        sem = self._sem(key)
        if self.cnt[key] > 0:
            deps[key] = max(deps.get(key, 0), self.cnt[key])
        self._emit_waits(queue, deps)
        self.cnt[key] += 16
        v = self.cnt[key]
        self.q[queue].append(lambda eng, fn=fn, sem=sem: fn(eng).then_inc(sem, 16))
        self._commit((key, v), reads, writes)
        self.n_inst += 1

    def barrier(self):
        deps = {k: self.cnt[k] for k in self.sems if self.cnt[k] > 0}
        for e in COMPUTE + ("sp",):
            self._emit_waits(e, dict(deps))

    def finish(self):
        deps = {k: self.cnt[k] for k in self.sems if self.cnt[k] > 0}
        self._emit_waits("sp", deps)

    def run(self):
        nc = self.nc
        with nc.Block() as block:
            @block.sync
            def _(eng):
                for f in self.q["sp"]:
                    f(eng)

            @block.tensor
            def _(eng):
                for f in self.q["pe"]:
                    f(eng)

            @block.scalar
            def _(eng):
                for f in self.q["act"]:
                    f(eng)

            @block.vector
            def _(eng):
                for f in self.q["dve"]:
                    f(eng)

            @block.gpsimd
            def _(eng):
                for f in self.q["pool"]:
                    f(eng)


def _res(*vs):
    out = []
    for v in vs:
        if v is None or isinstance(v, (int, float)):
            continue
        out.extend(v.res)
    return tuple(out)


def _ap(v):
    return v.ap if isinstance(v, V) else v


class K:
    def __init__(self, NP, T, NS, PAST, TS=64, NB1=256, NQ=512, NB3=512):
        self.NP, self.T, self.NS, self.PAST, self.TS = NP, T, NS, PAST, TS
        self.NB1, self.NQ, self.NB3 = NB1, NQ, NB3
        self.nc = nc = bass.Bass("TRN2", target_bir_lowering=False)
        self.S = Sched(nc)
        self.seqs = [("p", i, T, 0) for i in range(NP)] + [("s", i, TS, PAST) for i in range(NS)]
        self.arena = nc.alloc_sbuf_tensor("arena", [128, 204 * 1024], U8)
        self.off = 0
        self.uid = 0
        self.pb = [nc.alloc_psum_tensor("pb%d" % i, [128, 512], F32) for i in range(8)]
        self.din = {}
        self.dout = {}
        self.dscr = {}

    def sb(self, name, shape, dt, parts=128):
        nbytes = int(np.prod(shape[1:])) * (4 if dt == F32 else 2)
        nbytes = (nbytes + 63) // 64 * 64
        assert self.off + nbytes <= 204 * 1024, (name, self.off, nbytes)
        ap = self.arena[0:shape[0], self.off:self.off + nbytes].bitcast(dt)
        used = int(np.prod(shape[1:]))
        ap = ap[:, 0:used]
        if len(shape) == 3:
            ap = ap.rearrange("p (a b) -> p a b", a=shape[1])
        elif len(shape) == 4:
            ap = ap.rearrange("p (a b c) -> p a b c", a=shape[1], b=shape[2])
        elif len(shape) == 5:
            ap = ap.rearrange("p (a b c d) -> p a b c d", a=shape[1], b=shape[2], c=shape[3])
        self.off += nbytes
        self.uid += 1
        return V(ap, "%s#%d" % (name, self.uid))

    def ps(self, bank, shape, dt=F32, res=None):
        t = self.pb[bank]
        ap = t[0:shape[0], :]
        if dt == BF16:
            ap = ap.bitcast(BF16)
        used = int(np.prod(shape[1:]))
        ap = ap[:, 0:used]
        if len(shape) == 3:
            ap = ap.rearrange("p (a b) -> p a b", a=shape[1])
        elif len(shape) == 4:
            ap = ap.rearrange("p (a b c) -> p a b c", a=shape[1], b=shape[2])
        return V(ap, res or ("pb%d" % bank))

    def inp(self, name, shape, dt=F32):
        ap = self.nc.dram_tensor(name, list(shape), dt, kind="ExternalInput").ap()
        self.din[name] = V(ap, "in_" + name)
        return self.din[name]

    def outp(self, name, shape):
        ap = self.nc.dram_tensor(name, list(shape), F32, kind="ExternalOutput").ap()
        self.dout[name] = V(ap, "out_" + name)
        return self.dout[name]

    def scr(self, name, shape, dt):
        ap = self.nc.dram_tensor(name, list(shape), dt, kind="Internal").ap()
        self.dscr[name] = V(ap, "scr_" + name)
        return self.dscr[name]

    def act(self, out, in_, func, bias=None, scale=None, accum=None):
        kw = {}
        if bias is not None:
            kw["bias"] = _ap(bias)
        if scale is not None:
            kw["scale"] = _ap(scale)
        if accum is not None:
            kw["accum_out"] = accum.ap
        self.S.op("act", lambda e: e.activation(out=out.ap, in_=in_.ap, func=func, **kw),
                  reads=_res(in_, bias, scale), writes=_res(out, accum))

    def ts(self, out, in0, s1, s2, op0, op1=None, eng="dve"):
        kw = {}
        if op1 is not None:
            kw["op1"] = op1
        self.S.op(eng, lambda e: e.tensor_scalar(out=out.ap, in0=in0.ap, scalar1=_ap(s1), scalar2=_ap(s2), op0=op0, **kw),
                  reads=_res(in0, s1, s2), writes=_res(out))

    def tt(self, out, in0, in1, op, eng="dve"):
        self.S.op(eng, lambda e: e.tensor_tensor(out=out.ap, in0=in0.ap, in1=in1.ap, op=op),
                  reads=_res(in0, in1), writes=_res(out))

    def stt(self, out, in0, scalar, in1, op0, op1, eng="dve"):
        self.S.op(eng, lambda e: e.scalar_tensor_tensor(out=out.ap, in0=in0.ap, scalar=_ap(scalar), in1=in1.ap, op0=op0, op1=op1),
                  reads=_res(in0, scalar, in1), writes=_res(out))

    def cp(self, out, in_, eng="dve"):
        if eng == "act":
            self.S.op("act", lambda e: e.copy(out=out.ap, in_=in_.ap), reads=_res(in_), writes=_res(out))
        else:
            self.S.op(eng, lambda e: e.tensor_copy(out=out.ap, in_=in_.ap), reads=_res(in_), writes=_res(out))

    def recip(self, out, in_):
        self.S.op("dve", lambda e: e.reciprocal(out=out.ap, in_=in_.ap), reads=_res(in_), writes=_res(out))

    def red(self, out, in_, eng="dve"):
        self.S.op(eng, lambda e: e.tensor_reduce(out=out.ap, in_=in_.ap, axis=AX.X, op=ALU.add),
                  reads=_res(in_), writes=_res(out))

    def memset(self, out, val, eng="pool"):
        self.S.op(eng, lambda e: e.memset(out.ap, val), writes=_res(out))

    def mm(self, out, lhsT, rhs, start, stop, last=None):
        if last is None:
            last = stop
        self.S.op("pe", lambda e: e.matmul(out.ap, lhsT=lhsT.ap, rhs=rhs.ap, start=start, stop=stop),
                  reads=_res(lhsT, rhs), writes=_res(out), inc=last)

    def tr(self, out, in_, ident, last=True):
        self.S.op("pe", lambda e: e.transpose(out=out.ap, in_=in_.ap, identity=ident.ap),
                  reads=_res(in_, ident), writes=_res(out), inc=last)

    def ld(self, stream, out, in_, slow=False):
        kw = {"allow_slow_non_contiguous": True} if slow else {}
        extra = ("miscq",) if stream == "misc" else ()
        self.S.dma(stream, lambda e: e.dma_start(out=out.ap, in_=in_.ap, **kw), reads=_res(in_), writes=_res(out) + extra, queue="sp")

    def st(self, stream, out, in_, slow=False):
        kw = {"allow_slow_non_contiguous": True} if slow else {}
        self.S.dma(stream, lambda e: e.dma_start(out=out.ap, in_=in_.ap, **kw), reads=_res(in_), writes=_res(out), queue=getattr(self, "st_queue", "pool"))

    def rstd_from_ss(self, rstd, ss, n):
        self.ts(rstd, ss, 1.0 / n, EPS, ALU.mult, ALU.add)
        self.act(rstd, rstd, AF.Sqrt)
        self.recip(rstd, rstd)

    def declare_io(self):
        NP, T, NS, PAST, TS = self.NP, self.T, self.NS, self.PAST, self.TS
        i = self.inp
        i("xp", [NP, T, D]); i("xs", [NS, TS, D])
        i("clat", [NS, PAST, KVL]); i("ckr", [NS, PAST, RD])
        i("sre", [NS, 32, 64]); i("sim", [NS, 32, 64]); i("sconv", [NS, CW - 1, D])
        i("norm_ab", [D]); i("w_in_ab", [D, 2208]); i("g_q_lat", [QL]); i("w_uq", [QL, NH * HD])
        i("g_kv_lat", [KVL]); i("w_uk", [KVL, NH * ND]); i("w_uv", [KVL, NH * VD])
        i("g_q_nope", [ND]); i("g_q_rope", [RD]); i("g_k_nope", [ND]); i("g_k_rope", [RD])
        i("lam_re", [32, 64]); i("lam_im", [32, 64]); i("log_dt", [32])
        i("b_re", [32, 64, 16]); i("b_im", [32, 64, 16]); i("c_re", [32, 16, 64]); i("c_im", [32, 16, 64])
        i("s5_d", [SW]); i("w_glu", [SW, SW]); i("b_glu", [SW]); i("w_out_ab", [D, D])
        i("norm_c", [D]); i("w_in_c", [D, 3 * D]); i("conv_w", [CW, D]); i("conv_b", [D])
        i("ln_g", [D]); i("ln_b", [D]); i("w_out_c", [D, D])
        i("c_ident", [128, 128]); i("c_mask", [128, 128])
        i("c_cos_p", [T, 32]); i("c_sin_p", [T, 32]); i("c_cos_s", [TS, 32]); i("c_sin_s", [TS, 32])
        o = self.outp
        o("yp", [NP, T, D]); o("ys", [NS, TS, D])
        o("lat_p", [NP, T, KVL]); o("kr_p", [NP, T, RD]); o("re_p", [NP, 32, 64]); o("im_p", [NP, 32, 64])
        o("conv_p", [NP, CW - 1, D])
        o("lat_s", [NS, TS, KVL]); o("kr_s", [NS, TS, RD]); o("re_s", [NS, 32, 64]); o("im_s", [NS, 32, 64])
        o("conv_s", [NS, CW - 1, D])
        for si, (kind, idx, Tq, P) in enumerate(self.seqs):
            self.scr("QT%d" % si, [HD, NH, Tq], BF16)
            self.scr("KT%d" % si, [HD, NH, P + Tq], BF16)
            self.scr("VS%d" % si, [P + Tq, NH * VD], BF16)
            self.scr("GM%d" % si, [128, 4, Tq], BF16)
            self.scr("MS%d" % si, [128, 4, Tq], BF16)
            self.scr("X1%d" % si, [Tq, D], F32)
        self.scr("WC", [D, 3 * D], BF16)

    def seq_io(self, si):
        kind, idx, Tq, P = self.seqs[si]
        if kind == "p":
            return dict(x=self.din["xp"][idx], y=self.dout["yp"][idx], lat=self.dout["lat_p"][idx], kr=self.dout["kr_p"][idx],
                        re=self.dout["re_p"][idx], im=self.dout["im_p"][idx], conv=self.dout["conv_p"][idx],
                        cos=self.din["c_cos_p"], sin=self.din["c_sin_p"])
        return dict(x=self.din["xs"][idx], y=self.dout["ys"][idx], lat=self.dout["lat_s"][idx], kr=self.dout["kr_s"][idx],
                    re=self.dout["re_s"][idx], im=self.dout["im_s"][idx], conv=self.dout["conv_s"][idx],
                    cos=self.din["c_cos_s"], sin=self.din["c_sin_s"],
                    clat=self.din["clat"][idx], ckr=self.din["ckr"][idx], sre=self.din["sre"][idx], sim=self.din["sim"][idx],
                    sconv=self.din["sconv"][idx])

    def load_weight(self, dst, src, Kd, c0, c1, gain=None, stage=None):
        nk = Kd // 128
        n = c1 - c0
        srcv = src.r("(k p) n -> p k n", p=128)
        for k in range(nk):
            stg = stage[self._stg % 2]
            self._stg += 1
            self.ld("wstg%d" % (self._stg % 2), stg[:, 0:n], srcv[:, k, c0:c1])
            eng = "dve" if (self._stg % 2) else "pool"
            if gain is not None:
                self.ts(dst[:, k, :], stg[:, 0:n], gain[:, k:k + 1], None, ALU.mult, eng=eng)
            else:
                self.cp(dst[:, k, :], stg[:, 0:n], eng=eng)

    def load_vec_pk(self, name, src, Kd):
        v = self.sb(name, [128, Kd // 128], F32)
        self.ld("misc", v, src.r("(k p) -> p k", p=128), slow=True)
        return v

    def load_vec_bc(self, name, src, n, parts=128):
        v = self.sb(name, [parts, n], F32)
        self.ld("misc", v, V(src.ap.partition_broadcast(parts), src.res))
        return v

    def build(self):
        self.declare_io()
        self._stg = 0
        self.phase_const()
        self.phase1()
        self.st_queue = "pool"
        self.S.barrier()
        self.phase1b()
        self.S.barrier()
        if getattr(self, "STOP", 9) >= 2:
            self.phase2()
            self.S.barrier()
        if getattr(self, "STOP", 9) >= 3:
            self.phase3()
        self.S.finish()
        self.S.run()
        return self.nc

    def phase_const(self):
        self.identF = self.sb("identF", [128, 128], F32)
        self.ld("misc", self.identF, self.din["c_ident"])
        self.identB = self.sb("identB", [128, 128], BF16)
        self.cp(self.identB, self.identF)
        self.onesB = self.sb("onesB", [128, 128], BF16)
        self.memset(self.onesB, 1.0)
        self.onesF = self.sb("onesF", [128, 128], F32)
        self.memset(self.onesF, 1.0)
        self.base_off = self.off

    def s5_setup(self):
        d = self.din
        mark = None
        self.BkT = self.sb("BkT", [128, 32, 2, 64], BF16)
        self.Cq = self.sb("Cq", [128, 32, 128], BF16)
        self.T0 = self.sb("T0", [128, 32, 128], BF16)
        self.lamA = self.sb("lamA", [64, 2, 32], F32)
        self.lamAn = self.sb("lamAn", [64, 32], F32)
        self.LPr = self.sb("LPr", [64, 32, 8], F32)
        self.LPpm = self.sb("LPpm", [64, 2, 32, 8], F32)
        self.s5_persist_end = self.off
        if hasattr(self, "_p1_weights"):
            self._p1_weights()
        keep = self.off
        P = 64
        lre = self.sb("lre", [P, 32], F32); lim = self.sb("lim", [P, 32], F32)
        self.ld("misc", lre, d["lam_re"].r("g p -> p g"), slow=True)
        self.ld("misc", lim, d["lam_im"].r("g p -> p g"), slow=True)
        dt = self.load_vec_bc("dt", d["log_dt"], 32, parts=P)
        self.act(dt, dt, AF.Exp)
        ar = self.sb("ar", [P, 32], F32); ai = self.sb("ai", [P, 32], F32)
        self.tt(ar, lre, dt, ALU.mult); self.tt(ai, lim, dt, ALU.mult)
        NPW = 16
        pwr = self.sb("pwr", [P, 32, NPW], F32); pwi = self.sb("pwi", [P, 32, NPW], F32)
        mag = self.sb("mag", [P, 32], F32); ang = self.sb("ang", [P, 32], F32); kk = self.sb("kk", [P, 32], F32)
        sn = self.sb("sn", [P, 32], F32); cs = self.sb("cs", [P, 32], F32)
        for mi in range(NPW):
            m = float(mi - 7)
            self.act(mag, ar, AF.Exp, scale=m)
            for (dst, shift) in ((sn, 0.0), (cs, math.pi / 2)):
                self.ts(ang, ai, m, shift, ALU.mult, ALU.add)
                self.ts(kk, ang, 1.0 / TWO_PI, MAGIC, ALU.mult, ALU.add)
                self.ts(kk, kk, -MAGIC, -TWO_PI, ALU.add, ALU.mult)
                self.tt(ang, ang, kk, ALU.add)
                self.act(dst, ang, AF.Sin)
            self.tt(pwr[:, :, mi], mag, cs, ALU.mult)
            self.tt(pwi[:, :, mi], mag, sn, ALU.mult)
        for i in range(8):
            m = 8.0 * (i + 1)
            self.act(mag, ar, AF.Exp, scale=m)
            for (dst, shift) in ((sn, 0.0), (cs, math.pi / 2)):
                self.ts(ang, ai, m, shift, ALU.mult, ALU.add)
                self.ts(kk, ang, 1.0 / TWO_PI, MAGIC, ALU.mult, ALU.add)
                self.ts(kk, kk, -MAGIC, -TWO_PI, ALU.add, ALU.mult)
                self.tt(ang, ang, kk, ALU.add)
                self.act(dst, ang, AF.Sin)
            self.tt(self.LPr[:, :, i], mag, cs, ALU.mult)
            self.tt(self.LPpm[:, 1, :, i], mag, sn, ALU.mult)
            self.ts(self.LPpm[:, 0, :, i], self.LPpm[:, 1, :, i], -1.0, None, ALU.mult)
        self.cp(self.lamA[:, 0, :], pwr[:, :, 15]); self.cp(self.lamA[:, 1, :], pwi[:, :, 15])
        self.ts(self.lamAn, pwi[:, :, 15], -1.0, None, ALU.mult)
        nr = self.sb("nr", [P, 32], F32); den = self.sb("den", [P, 32], F32); t1 = self.sb("t1", [P, 32], F32); t2 = self.sb("t2", [P, 32], F32)
        kre = self.sb("kre", [P, 32], F32); kim = self.sb("kim", [P, 32], F32)
        self.ts(nr, pwr[:, :, 8], -1.0, None, ALU.add)
        ni = pwi[:, :, 8]
        self.tt(den, lre, lre, ALU.mult); self.tt(t1, lim, lim, ALU.mult); self.tt(den, den, t1, ALU.add); self.recip(den, den)
        self.tt(t1, nr, lre, ALU.mult); self.tt(t2, ni, lim, ALU.mult); self.tt(t1, t1, t2, ALU.add); self.tt(kre, t1, den, ALU.mult)
        self.tt(t1, ni, lre, ALU.mult); self.tt(t2, nr, lim, ALU.mult); self.tt(t1, t1, t2, ALU.subtract); self.tt(kim, t1, den, ALU.mult)
        bre = self.sb("bre", [P, 32, 16], F32); bim = self.sb("bim", [P, 32, 16], F32)
        self.ld("misc", bre, d["b_re"].r("g p n -> p g n")); self.ld("misc", bim, d["b_im"].r("g p n -> p g n"))
        bbr = self.sb("bbr", [P, 32, 16], F32); bbi = self.sb("bbi", [P, 32, 16], F32); tb = self.sb("tb", [P, 32, 16], F32)
        kreb = kre.us(2).bc([P, 32, 16]); kimb = kim.us(2).bc([P, 32, 16])
        self.tt(bbr, bre, kreb, ALU.mult); self.tt(tb, bim, kimb, ALU.mult); self.tt(bbr, bbr, tb, ALU.subtract)
        self.tt(bbi, bim, kreb, ALU.mult); self.tt(tb, bre, kimb, ALU.mult); self.tt(bbi, bbi, tb, ALU.add)
        cre = self.sb("cre", [P, 32, 16], F32); cim = self.sb("cim", [P, 32, 16], F32)
        self.ld("misc", cre, d["c_re"].r("g n p -> p g n"), slow=True); self.ld("misc", cim, d["c_im"].r("g n p -> p g n"), slow=True)
        Bkr = self.sb("Bkr", [P, 32, 8, 16], F32); Bki = self.sb("Bki", [P, 32, 8, 16], F32); tB = self.sb("tB", [P, 32, 16], F32)
        for s in range(8):
            pr = pwr[:, :, 14 - s].us(2).bc([P, 32, 16]); pi_ = pwi[:, :, 14 - s].us(2).bc([P, 32, 16])
            self.tt(Bkr[:, :, s, :], bbr, pr, ALU.mult); self.tt(tB, bbi, pi_, ALU.mult); self.tt(Bkr[:, :, s, :], Bkr[:, :, s, :], tB, ALU.subtract)
            self.tt(Bki[:, :, s, :], bbr, pi_, ALU.mult); self.tt(tB, bbi, pr, ALU.mult); self.tt(Bki[:, :, s, :], Bki[:, :, s, :], tB, ALU.add)
        CLr = self.sb("CLr", [P, 32, 8, 16], F32); CLi = self.sb("CLi", [P, 32, 8, 16], F32); tC = self.sb("tC", [P, 32, 16], F32)
        tCr = self.sb("tCr", [P, 32, 16], F32); tCi = self.sb("tCi", [P, 32, 16], F32)
        Cq4r = self.Cq[0:64].r("p g (s n) -> p g s n", s=8); Cq4i = self.Cq[64:128].r("p g (s n) -> p g s n", s=8)
        for mi in range(NPW):
            pr = pwr[:, :, mi].us(2).bc([P, 32, 16]); pi_ = pwi[:, :, mi].us(2).bc([P, 32, 16])
            dr = CLr[:, :, mi, :] if mi < 8 else tCr
            di = CLi[:, :, mi, :] if mi < 8 else tCi
            self.tt(dr, cre, pr, ALU.mult); self.tt(tC, cim, pi_, ALU.mult); self.tt(dr, dr, tC, ALU.subtract)
            self.tt(di, cre, pi_, ALU.mult); self.tt(tC, cim, pr, ALU.mult); self.tt(di, di, tC, ALU.add)
            self.ts(di, di, -1.0, None, ALU.mult)
            if mi >= 8:
                self.cp(Cq4r[:, :, mi - 8, :], tCr)
                self.cp(Cq4i[:, :, mi - 8, :], tCi)
        maskS = self.sb("maskS", [128, 128], F32)
        self.ld("misc", maskS, d["c_mask"])
        dsel = self.sb("dsel", [128, 32], F32)
        for s in range(8):
            self.ld("misc", dsel[s * 16:(s + 1) * 16, :], d["s5_d"].r("(g n) -> n g", n=16), slow=True)
        diagd = self.sb("diagd", [128, 128], F32)
        for g in range(32):
            bk = g % 2
            pT = self.ps(bk, [128, 2, 64], F32)
            self.tr(pT[:, 0, :], Bkr[:, g, :, :].r("p s n -> p (s n)"), self.identF[0:64, 0:64], last=False)
            self.tr(pT[:, 1, :], Bki[:, g, :, :].r("p s n -> p (s n)"), self.identF[0:64, 0:64], last=True)
            self.cp(self.BkT[:, g, :, :], pT, eng="act")
            p0 = self.ps(2 + bk, [128, 128], F32)
            self.mm(p0, Bkr[:, g, :, :].r("p s n -> p (s n)"), CLr[:, g, 0:8, :].r("p s n -> p (s n)"), True, False)
            self.mm(p0, Bki[:, g, :, :].r("p s n -> p (s n)"), CLi[:, g, 0:8, :].r("p s n -> p (s n)"), False, True)
            self.ts(diagd, self.identF, dsel[:, g:g + 1], None, ALU.mult, eng="pool")
            t0f = self.sb("t0f", [128, 128], F32) if g == 0 else t0f
            self.tt(t0f, p0, maskS, ALU.mult)
            self.tt(self.T0[:, g, :], t0f, diagd, ALU.add)
        self.s5_tmp_end = self.off
        return keep

    def phase1(self):
        d = self.din
        NB = self.NB1
        W_ = {}

        def p1_weights():
            stage = [self.sb("wstgA", [128, 1024], F32), self.sb("wstgB", [128, 1024], F32)]
            W_["stage"] = stage
            g_norm = self.load_vec_pk("g_norm", d["norm_ab"], D)
            g_ql = self.load_vec_pk("g_ql", d["g_q_lat"], QL)
            self.ts(g_ql, g_ql, math.sqrt(QL), None, ALU.mult)
            for nm, shape in (("wA", [128, 8, 672]), ("wGm", [128, 8, 512]), ("wU", [128, 8, 512]), ("wGs", [128, 8, 512]),
                              ("wuq", [128, 3, 768]), ("wuk", [128, 2, 512]), ("wuv", [128, 2, 512]), ("wglu", [128, 4, 512])):
                W_[nm] = self.sb(nm, shape, BF16)
            self.load_weight(W_["wA"], d["w_in_ab"], D, 0, 672, g_norm, stage)
            self.load_weight(W_["wGm"], d["w_in_ab"], D, 672, 1184, g_norm, stage)
            self.load_weight(W_["wU"], d["w_in_ab"], D, 1184, 1696, g_norm, stage)
            self.load_weight(W_["wGs"], d["w_in_ab"], D, 1696, 2208, g_norm, stage)
            self.load_weight(W_["wuq"], d["w_uq"], QL, 0, 768, g_ql, stage)
            self.load_weight(W_["wuk"], d["w_uk"], KVL, 0, 512, None, stage)
            self.load_weight(W_["wuv"], d["w_uv"], KVL, 0, 512, None, stage)
            self.load_weight(W_["wglu"], d["w_glu"], SW, 0, 512, None, stage)
        self._p1_weights = p1_weights
        keep = self.s5_setup()
        self.S.barrier()
        self.off = keep
        wA, wGm, wU, wGs, wuq, wuk, wuv, wglu = (W_[k] for k in ("wA", "wGm", "wU", "wGs", "wuq", "wuk", "wuv", "wglu"))
        xt = W_["stage"]
        if False:
            g_norm = None
        bglu = self.load_vec_pk("bglu", d["b_glu"], SW)
        gkv = self.load_vec_bc("gkv", d["g_kv_lat"], KVL)
        gkr = self.load_vec_bc("gkr", d["g_k_rope"], RD)
        gq = self.sb("gq", [128, HD], F32)
        self.ld("misc", gq[:, 0:ND], V(d["g_q_nope"].ap.partition_broadcast(128), d["g_q_nope"].res))
        self.ld("misc", gq[:, ND:HD], V(d["g_q_rope"].ap.partition_broadcast(128), d["g_q_rope"].res))
        gkn = self.load_vec_bc("gkn", d["g_k_nope"], ND)
        self.ts(gkv, gkv, math.sqrt(KVL), None, ALU.mult)
        self.ts(gkr, gkr, math.sqrt(RD), None, ALU.mult)
        self.ts(gq[:, 0:ND], gq[:, 0:ND], math.sqrt(ND), None, ALU.mult)
        self.ts(gq[:, ND:HD], gq[:, ND:HD], math.sqrt(RD), None, ALU.mult)
        nepsA = self.sb("nepsA", [128, 8], F32)
        self.memset(nepsA, KVL * EPS)
        self.memset(nepsA[:, 1:2], QL * EPS)
        self.memset(nepsA[:, 4:5], RD * EPS)
        nepsQ = self.sb("nepsQ", [128, 16], F32)
        self.memset(nepsQ[:, 0:8], ND * EPS)
        self.memset(nepsQ[:, 8:16], RD * EPS)

        def two(name, shape, dt, parts=128):
            return [self.sb(name + str(i), [parts] + shape, dt) for i in range(2)]
        TB = dict(
            hb=two("hb", [D], BF16), ss=two("ss", [8], F32), rs=two("rs", [8], F32), ssq=two("ssq", [16], F32), rsq=two("rsq", [16], F32),
            ssk=two("ssk", [8], F32), rsk=two("rsk", [8], F32), qlb=two("qlb", [QL], BF16), qlT=two("qlT", [3, 128], BF16),
            latf=two("latf", [KVL], F32), latb=two("latb", [KVL], BF16), latT=two("latT", [2, 128], BF16),
            krf=two("krf", [RD], F32), krn=two("krn", [RD], F32), cosT=two("cosT", [32], F32), sinT=two("sinT", [32], F32),
            r1=two("r1", [NH, 32], F32), r2=two("r2", [NH, 32], F32), r1k=two("r1k", [1, 32], F32), r2k=two("r2k", [1, 32], F32), qn=two("qn", [NH, HD], F32), sqq=two("sqq", [NH, HD], BF16),
            Qs=two("Qs", [NH, HD], BF16), Ks=two("Ks", [NH, HD], BF16), knf=two("knf", [NH, ND], F32), sqk=two("sqk", [NH, ND], BF16),
            vb=two("vb", [512], BF16))
        hTs = [self.sb("hT%d" % i, [128, 8, NB], BF16) for i in range(2)]
        KTc = [self.sb("KTc%d" % i, [HD, NH, 128], BF16) for i in range(2)]
        QTb = self.sb("QTb", [HD, NH, NB], BF16)
        KTb = self.sb("KTb", [HD, NH, NB], BF16)
        GMt = self.sb("GMt", [128, 4, NB], BF16)
        gsT = [self.sb("gsT%d" % i, [128, 4, NB], BF16) for i in range(3)]
        J = NB // 8
        F1 = getattr(self, "F1", True)
        X2s = self.sb("X2s", [64, 16, 8, 16], BF16) if F1 else self.sb("X2s", [J, 32, 8, 16], BF16)
        Usel = [self.sb("Usel%d" % i, [128, 32, J], BF16) for i in range(3)]
        Sb = [self.sb("Sb%d" % i, [64, 2, 32, J], F32) for i in range(2)]
        Sp = [self.sb("Sp%d" % i, [128, 32, J], BF16) for i in range(2)]
        carry = self.sb("carry", [64, 2, 32], F32)
        sq1 = self.sb("sq1", [64, 2, 32, 4], F32); sq2 = self.sb("sq2", [64, 2, 32, 4], F32)
        sl1 = self.sb("sl1", [64, 2, 32, 8], F32); sl2 = self.sb("sl2", [64, 2, 32, 8], F32)
        Z2 = self.sb("Z2", [64, 8, 16, 16], BF16) if F1 else self.sb("Z2", [J, 8, 32, 16], BF16)
        yz = self.sb("yz", [J, 4, 128], F32); yz2 = self.sb("yz2", [J, 4, 128], F32)
        zT = self.sb("zT", [128, 4, NB], BF16)
        sig = self.sb("sig", [128, NB], F32)
        MSt = self.sb("MSt", [128, 4, NB], BF16)
        self.p1_end = self.off

        lamre_b = self.lamA[:, 0, :].us(1).bc([64, 2, 32])
        lam_pm = self.sb("lam_pm", [64, 2, 32], F32)
        self.cp(lam_pm[:, 0, :], self.lamAn); self.cp(lam_pm[:, 1, :], self.lamA[:, 1, :])

        st_ = dict(xi=0, tc=0)

        def tile_front(io, TT, tk0, ti, VSd, P, bk2, hT):
            b0, b1 = bk2
            xi = st_["xi"]; st_["xi"] += 1
            tp = st_["tc"] % 2; st_["tc"] += 1
            B = {k: v[tp] for k, v in TB.items()}
            x = xt[xi % 2]
            ss, rs = B["ss"], B["rs"]
            self.ld("xt%d" % (xi % 2), x[0:TT, :], io["x"][tk0:tk0 + TT, :])
            self.ld("cos%d" % tp, B["cosT"][0:TT, :], io["cos"][tk0:tk0 + TT, :])
            self.ld("sin%d" % tp, B["sinT"][0:TT, :], io["sin"][tk0:tk0 + TT, :])
            self.act(B["hb"][0:TT, :], x[0:TT, :], AF.Square, accum=ss[0:TT, 0:1])
            self.rstd_from_ss(rs[0:TT, 0:1], ss[0:TT, 0:1], D)
            self.ts(B["hb"][0:TT, :], x[0:TT, :], rs[0:TT, 0:1], None, ALU.mult)
            yield
            pT = self.ps(b0, [128, 8, TT], BF16)
            for k in range(8):
                self.tr(pT[:, k, :], B["hb"][0:TT, k * 128:(k + 1) * 128], self.identB[0:TT, 0:TT], last=(k == 7))
            self.cp(hT[:, :, ti * TT:(ti + 1) * TT], pT, eng="act")
            yield
            pA = self.ps(b0, [TT, 512], F32); pB = self.ps(b1, [TT, 160], F32)
            for k in range(8):
                self.mm(pA, hT[:, k, ti * TT:(ti + 1) * TT], wA[:, k, 0:512], k == 0, k == 7)
            for k in range(8):
                self.mm(pB, hT[:, k, ti * TT:(ti + 1) * TT], wA[:, k, 512:672], k == 0, k == 7)
            yield
            trash = B["sqq"][0:TT].r("p h d -> p (h d)")
            self.act(trash[:, 0:QL], pA[:, 0:QL], AF.Square, accum=ss[0:TT, 1:2])
            self.act(trash[:, 0:128], pA[:, QL:512], AF.Square, accum=ss[0:TT, 2:3])
            self.act(trash[:, 0:128], pB[:, 0:128], AF.Square, accum=ss[0:TT, 3:4])
            self.act(trash[:, 0:RD], pB[:, 128:160], AF.Square, accum=ss[0:TT, 4:5])
            self.tt(ss[0:TT, 2:3], ss[0:TT, 2:3], ss[0:TT, 3:4], ALU.add)
            self.tt(rs[0:TT, 1:5], ss[0:TT, 1:5], nepsA[0:TT, 1:5], ALU.add)
            self.act(rs[0:TT, 1:5], rs[0:TT, 1:5], AF.Sqrt)
            self.recip(rs[0:TT, 1:5], rs[0:TT, 1:5])
            yield
            self.ts(B["qlb"][0:TT, :], pA[:, 0:QL], rs[0:TT, 1:2], None, ALU.mult)
            lf = B["latf"]
            self.stt(lf[0:TT, 0:128], pA[:, QL:512], rs[0:TT, 2:3], gkv[0:TT, 0:128], ALU.mult, ALU.mult)
            self.stt(lf[0:TT, 128:256], pB[:, 0:128], rs[0:TT, 2:3], gkv[0:TT, 128:256], ALU.mult, ALU.mult)
            self.st("latf%d" % tp, io["lat"][tk0:tk0 + TT, :], lf[0:TT, :])
            self.stt(B["krn"][0:TT, :], pB[:, 128:160], rs[0:TT, 4:5], gkr[0:TT, :], ALU.mult, ALU.mult)
            yield
            def q_chain():
                pQT = self.ps(b0, [128, 3, TT], BF16)
                for k in range(3):
                    self.tr(pQT[:, k, :], B["qlb"][0:TT, k * 128:(k + 1) * 128], self.identB[0:TT, 0:TT], last=(k == 2))
                self.cp(B["qlT"][:, :, 0:TT], pQT, eng="act")
                yield
                qn = B["qn"]
                qf = qn[0:TT].r("p h d -> p (h d)")
                pQ1 = self.ps(b0, [TT, 512], F32)
                for k in range(3):
                    self.mm(pQ1, B["qlT"][:, k, 0:TT], wuq[:, k, 0:512], k == 0, k == 2)
                self.cp(qf[:, 0:512], pQ1, eng="act")
                yield
                pQ2 = self.ps(b0, [TT, 256], F32)
                for k in range(3):
                    self.mm(pQ2, B["qlT"][:, k, 0:TT], wuq[:, k, 512:768], k == 0, k == 2)
                self.cp(qf[:, 512:768], pQ2, eng="act")
                yield
                sqv = B["sqq"][0:TT]
                self.act(sqv.r("p h d -> p (h d)"), qf, AF.Square)
                self.red(B["ssq"][0:TT, 0:8], sqv[:, :, 0:ND])
                self.red(B["ssq"][0:TT, 8:16], sqv[:, :, ND:HD])
                yield
                self.tt(B["rsq"][0:TT, :], B["ssq"][0:TT, :], nepsQ[0:TT, :], ALU.add)
                self.act(B["rsq"][0:TT, :], B["rsq"][0:TT, :], AF.Sqrt)
                self.recip(B["rsq"][0:TT, :], B["rsq"][0:TT, :])
                yield
                self.tt(qn[0:TT, :, 0:ND], qn[0:TT, :, 0:ND], B["rsq"][0:TT, 0:8].us(2).bc([TT, NH, ND]), ALU.mult)
                self.tt(qn[0:TT, :, ND:HD], qn[0:TT, :, ND:HD], B["rsq"][0:TT, 8:16].us(2).bc([TT, NH, RD]), ALU.mult)
                self.tt(qn[0:TT], qn[0:TT], gq[0:TT].us(1).bc([TT, NH, HD]), ALU.mult)
                yield
                self.cp(B["Qs"][0:TT, :, 0:ND], qn[0:TT, :, 0:ND], eng="act")
                self.rope(B["Qs"][0:TT, :, ND:HD], qn[0:TT, :, ND:HD], B["cosT"], B["sinT"], B["r1"], B["r2"], TT, NH)
                yield
                pq = self.ps(b0, [HD, NH, TT], BF16)
                for h in range(NH):
                    self.tr(pq[:, h, :], B["Qs"][0:TT, h, :], self.identB[0:TT, 0:TT], last=(h == NH - 1))
                self.cp(QTb[:, :, ti * TT:(ti + 1) * TT], pq, eng="act")

            def k_chain():
                kf = B["krf"]
                self.rope(kf[0:TT, :].r("p (o r) -> p o r", o=1), B["krn"][0:TT, :].r("p (o r) -> p o r", o=1), B["cosT"], B["sinT"], B["r1k"], B["r2k"], TT, 1)
                self.st("krf%d" % tp, io["kr"][tk0:tk0 + TT, :], kf[0:TT, :])
                self.cp(B["Ks"][0:TT, :, ND:HD], kf[0:TT, :].us(1).bc([TT, NH, RD]))
                yield
                yield from self.kv_from_lat(lf, TT, B, wuk, wuv, gkn, KTb, ti * TT, (b1, b1, b1, b1))
                self.st("vb%d" % tp, VSd[P + tk0:P + tk0 + TT, :], B["vb"][0:TT, :])

            gens = [q_chain(), k_chain()]
            alive = [True, True]
            while any(alive):
                for gi in range(2):
                    if alive[gi]:
                        try:
                            next(gens[gi])
                        except StopIteration:
                            alive[gi] = False
                yield

        def cache_tile(io, kt_, KTd, VSd, tp, banks):
            B = {k: v[tp] for k, v in TB.items()}
            lf = B["latf"]; kf = B["krf"]
            self.ld("latf%d" % tp, lf, io["clat"][kt_ * 128:(kt_ + 1) * 128, :])
            self.ld("krf%d" % tp, kf, io["ckr"][kt_ * 128:(kt_ + 1) * 128, :])
            self.cp(B["Ks"][:, :, ND:HD], kf.us(1).bc([128, NH, RD]))
            kb = KTc[tp]
            yield
            yield from self.kv_from_lat(lf, 128, B, wuk, wuv, gkn, kb, 0, banks)
            self.st("KTc%d" % tp, KTd[:, :, kt_ * 128:(kt_ + 1) * 128], kb[:, :, 0:128])
            self.st("vb%d" % tp, VSd[kt_ * 128:(kt_ + 1) * 128, :], B["vb"])


        def stageA_pre(blk):
            si, b = blk["si"], blk["b"]
            kind, idx, Tq, P = self.seqs[si]
            io = self.seq_io(si)
            KTd, VSd = self.dscr["KT" + str(si)], self.dscr["VS" + str(si)]
            nb = blk["nb"]; TT = min(128, nb); ntile = nb // TT
            t0 = b * nb
            if False:
                yield
            blk["tile_gens"] = [tile_front(io, TT, t0 + ti * TT, ti, VSd, P, ((0, 1), (2, 3))[ti % 2], hTs[blk["r"] % 2]) for ti in range(ntile)]

        def tiles_done(blk):
            si, b, nb = blk["si"], blk["b"], blk["nb"]
            kind, idx, Tq, P = self.seqs[si]
            t0 = b * nb
            QTd, KTd = self.dscr["QT" + str(si)], self.dscr["KT" + str(si)]
            self.st("QTb", QTd[:, :, t0:t0 + nb], QTb[:, :, 0:nb])
            self.st("KTb", KTd[:, :, P + t0:P + t0 + nb], KTb[:, :, 0:nb])

        def stageA2(blk):
            si, b, slot3, slot2 = blk["si"], blk["b"], blk["r"] % 3, blk["r"] % 2
            GMd = self.dscr["GM" + str(si)]
            nb = blk["nb"]; Jb = nb // 8; t0 = b * nb
            hT = hTs[slot2]
            for (wG, dst) in ((wGm, GMt), (wGs, gsT[slot3])):
                for m in range(4):
                    pg = self.ps(4 + (m % 2), [128, nb], F32)
                    for k in range(8):
                        self.mm(pg, wG[:, k, m * 128:(m + 1) * 128], hT[:, k, 0:nb], k == 0, k == 7)
                    self.act(dst[:, m, 0:nb], pg, AF.Silu)
                    yield
            self.st("GMt", GMd[:, :, t0:t0 + nb], GMt[:, :, 0:nb])
            for s_ in range(8):
                pu = self.ps(4 + (s_ % 2), [Jb, 512], F32)
                for k in range(8):
                    self.mm(pu, hT[:, k, s_:nb:8], wU[:, k, :], k == 0, k == 7)
                e_ = "act" if s_ % 2 else "dve"
                if F1:
                    self.cp(X2s[0:Jb, :, s_, :], pu[:, 0:256].r("j (g n) -> j g n", g=16), eng=e_)
                    self.cp(X2s[32:32 + Jb, :, s_, :], pu[:, 256:512].r("j (g n) -> j g n", g=16), eng=e_)
                else:
                    self.cp(X2s[0:Jb, :, s_, :], pu.r("j (g n) -> j g n", g=32), eng=e_)
                yield
            for g4 in range(8):
                pus = self.ps(4 + (g4 % 2), [128, 4, Jb], BF16)
                for gg in range(4):
                    g = g4 * 4 + gg
                    p0 = (g // 16) * 32
                    if F1:
                        self.tr(pus[:, gg, :], X2s[p0:p0 + Jb, g % 16, :, :].r("j s n -> j (s n)"), self.identB[p0:p0 + Jb, p0:p0 + Jb], last=(gg == 3))
                    else:
                        self.tr(pus[:, gg, :], X2s[0:Jb, g, :, :].r("j s n -> j (s n)"), self.identB[0:Jb, 0:Jb], last=(gg == 3))
                self.cp(Usel[slot3][:, g4 * 4:(g4 + 1) * 4, 0:Jb], pus, eng=("act" if g4 % 2 else "dve"))
                yield
            for g8 in range(4):
                pw = self.ps(4 + (g8 % 2), [64, 2, 8, Jb], F32)
                for gg in range(8):
                    g = g8 * 8 + gg
                    for c in range(2):
                        self.mm(pw[:, c, gg, :], self.BkT[:, g, c, :], Usel[slot3][:, g, 0:Jb], True, True, last=(gg == 7 and c == 1))
                self.cp(Sb[slot2][:, :, g8 * 8:(g8 + 1) * 8, 0:Jb], pw, eng=("act" if g8 % 2 else "dve"))
                yield

        def stageB(blk):
            si, b, slot2 = blk["si"], blk["b"], blk["r"] % 2
            kind, idx, Tq, P = self.seqs[si]
            io = self.seq_io(si)
            Jb = blk["nb"] // 8
            S_ = Sb[slot2]; Spb = Sp[slot2]
            if b == 0:
                if P > 0:
                    self.ld("carry", carry[:, 0, :], io["sre"].r("g p -> p g"), slow=True)
                    self.ld("carry", carry[:, 1, :], io["sim"].r("g p -> p g"), slow=True)
                else:
                    self.memset(carry, 0.0)
            self.cp(Spb[0:64, :, 0], carry[:, 0, :], eng="pool")
            self.cp(Spb[64:128, :, 0], carry[:, 1, :], eng="pool")
            L = 8; Q = Jb // L
            S5 = S_[:, :, :, 0:Jb].r("p c g (q l) -> p c g q l", l=L)
            a1 = sq1[:, :, :, 0:Q]; a2 = sq2[:, :, :, 0:Q]
            lre_q = self.lamA[:, 0, :].us(1).us(3).bc([64, 2, 32, Q])
            for i in range(1, L):
                prev = S5[:, :, :, :, i - 1]; cur = S5[:, :, :, :, i]
                self.tt(a1, prev, lre_q, ALU.mult, eng="pool")
                self.tt(a2[:, 0], prev[:, 1], lam_pm[:, 0, :].us(2).bc([64, 32, Q]), ALU.mult, eng="pool")
                self.tt(a2[:, 1], prev[:, 0], lam_pm[:, 1, :].us(2).bc([64, 32, Q]), ALU.mult, eng="pool")
                self.tt(a1, a1, a2, ALU.add, eng="pool")
                self.tt(cur, cur, a1, ALU.add, eng="pool")
                yield
            LPr_b = self.LPr.us(1).bc([64, 2, 32, L])
            for q in range(Q):
                C = carry if q == 0 else S5[:, :, :, q - 1, L - 1]
                self.tt(sl1, LPr_b, C.us(3).bc([64, 2, 32, L]), ALU.mult, eng="pool")
                self.tt(sl2[:, 0], self.LPpm[:, 0], C[:, 1].us(2).bc([64, 32, L]), ALU.mult, eng="pool")
                self.tt(sl2[:, 1], self.LPpm[:, 1], C[:, 0].us(2).bc([64, 32, L]), ALU.mult, eng="pool")
                self.tt(sl1, sl1, sl2, ALU.add, eng="pool")
                self.tt(S5[:, :, :, q, :], S5[:, :, :, q, :], sl1, ALU.add, eng="pool")
                yield
            self.cp(carry, S_[:, :, :, Jb - 1], eng="pool")
            if Jb > 1:
                self.cp(Spb[0:64, :, 1:Jb], S_[:, 0, :, 0:Jb - 1], eng="pool")
                self.cp(Spb[64:128, :, 1:Jb], S_[:, 1, :, 0:Jb - 1], eng="pool")
            if b == Tq // blk["nb"] - 1:
                self.st("fin", io["re"].r("g p -> p g"), carry[:, 0, :], slow=True)
                self.st("fin", io["im"].r("g p -> p g"), carry[:, 1, :], slow=True)
            yield

        def stageC(blk):
            si, b, slot3, slot2 = blk["si"], blk["b"], blk["r"] % 3, blk["r"] % 2
            MSd = self.dscr["MS" + str(si)]
            nb = blk["nb"]; Jb = nb // 8; t0 = b * nb
            U_ = Usel[slot3]; Spb = Sp[slot2]; gs_ = gsT[slot3]
            for g4 in range(8):
                py = self.ps(6 + (g4 % 2), [Jb, 4, 128], F32)
                for gg in range(4):
                    g = g4 * 4 + gg
                    self.mm(py[:, gg, :], U_[:, g, 0:Jb], self.T0[:, g, :], True, False)
                    self.mm(py[:, gg, :], Spb[:, g, 0:Jb], self.Cq[:, g, :], False, True, last=(gg == 3))
                self.cp(yz[0:Jb], py, eng="act")
                self.tt(yz2[0:Jb], yz[0:Jb], yz[0:Jb], ALU.mult)
                self.ts(yz2[0:Jb], yz2[0:Jb], 0.044715, 1.0, ALU.mult, ALU.add)
                self.tt(yz2[0:Jb], yz2[0:Jb], yz[0:Jb], ALU.mult)
                self.act(yz2[0:Jb], yz2[0:Jb], AF.Sigmoid, scale=2.0 * math.sqrt(2.0 / math.pi))
                zp0 = (g4 // 4) * 32; zg0 = (g4 % 4) * 4
                if not F1:
                    zp0 = 0; zg0 = g4 * 4
                self.tt(Z2[zp0:zp0 + Jb, :, zg0:zg0 + 4, :].r("j s g n -> j g s n"), yz[0:Jb].r("j g (s n) -> j g s n", s=8),
                        yz2[0:Jb].r("j g (s n) -> j g s n", s=8), ALU.mult)
                yield
            if getattr(self, 'STOPC', 9) < 2:
                return
            for s in range(8):
                if F1:
                    pzA = self.ps(6, [128, 2, Jb], BF16); pzB = self.ps(7, [128, 2, Jb], BF16)
                    for m in range(2):
                        self.tr(pzA[:, m, :], Z2[0:Jb, s, m * 8:(m + 1) * 8, :].r("j g n -> j (g n)"), self.identB[0:Jb, 0:Jb], last=(m == 1))
                    for m in range(2):
                        self.tr(pzB[:, m, :], Z2[32:32 + Jb, s, m * 8:(m + 1) * 8, :].r("j g n -> j (g n)"), self.identB[32:32 + Jb, 32:32 + Jb], last=(m == 1))
                    self.cp(zT[:, 0:2, s:nb:8], pzA, eng="act")
                    self.cp(zT[:, 2:4, s:nb:8], pzB, eng="dve")
                else:
                    pz = self.ps(6 + (s % 2), [128, 4, Jb], BF16)
                    for m in range(4):
                        self.tr(pz[:, m, :], Z2[0:Jb, s, m * 8:(m + 1) * 8, :].r("j g n -> j (g n)"), self.identB[0:Jb, 0:Jb], last=(m == 3))
                    self.cp(zT[:, :, s:nb:8], pz, eng=("act" if s % 2 else "dve"))
                yield
            if getattr(self, 'STOPC', 9) < 3:
                return
            for m in range(4):
                pg = self.ps(6 + (m % 2), [128, nb], F32)
                for k in range(4):
                    self.mm(pg, wglu[:, k, m * 128:(m + 1) * 128], zT[:, k, 0:nb], k == 0, k == 3)
                self.act(sig[:, 0:nb], pg, AF.Sigmoid, bias=bglu[:, m:m + 1])
                self.tt(sig[:, 0:nb], sig[:, 0:nb], zT[:, m, 0:nb], ALU.mult)
                self.tt(MSt[:, m, 0:nb], sig[:, 0:nb], gs_[:, m, 0:nb], ALU.mult)
                yield
            self.st("MSt", MSd[:, :, t0:t0 + nb], MSt[:, :, 0:nb])

        blocks = []
        for si, (kind, idx, Tq, P) in enumerate(self.seqs):
            nb = min(NB, Tq)
            for b in range(Tq // nb):
                blocks.append(dict(si=si, b=b, nb=nb, r=len(blocks)))

        def chain(*gens):
            for g in gens:
                if g is not None:
                    yield from g

        nblk = len(blocks)

        def step(g):
            try:
                return next(g), True
            except StopIteration:
                return None, False

        def get(fn, i):
            return fn(blocks[i]) if 0 <= i < nblk else iter(())

        SEQ_DRV = getattr(self, "SEQ_DRV", False)
        self.st_queue = "sp"
        for t in range(min(nblk + 3, getattr(self, "MAXR", 10 ** 9))):
            if SEQ_DRV:
                skip = getattr(self, "SKIP", "") if t == getattr(self, "MAXR", 10 ** 9) - 1 else ""
                for nm_, g_ in (("C", get(stageC, t - 3)), ("B", get(stageB, t - 2)), ("A2", get(stageA2, t - 1)), ("P", get(stageA_pre, t))):
                    if nm_ in skip.split(","):
                        continue
                    for _ in g_:
                        pass
                if t < nblk and "T" not in skip.split(","):
                    for tg in blocks[t]["tile_gens"]:
                        for _ in tg:
                            pass
                    tiles_done(blocks[t])
                continue
            pre = get(stageA_pre, t)
            others = [get(stageA2, t - 1), get(stageB, t - 2), get(stageC, t - 3)]
            o_alive = [True, True, True]
            p_alive = True
            while p_alive:
                _, p_alive = step(pre)
                for i in range(3):
                    if o_alive[i]:
                        _, o_alive[i] = step(others[i])
            tiles = list(blocks[t]["tile_gens"]) if t < nblk else []
            t_alive = [True] * len(tiles)
            while any(t_alive) or any(o_alive):
                for i, tg in enumerate(tiles):
                    if t_alive[i]:
                        _, t_alive[i] = step(tg)
                for i in range(3):
                    if o_alive[i]:
                        _, o_alive[i] = step(others[i])
            if t < nblk:
                tiles_done(blocks[t])

    def _end_phase1_marker(self):
        pass

    def rope(self, out, in_, cosT, sinT, r1, r2, TT, nh):
        cb = cosT[0:TT, :].us(1).bc([TT, nh, 32])
        a = r1[0:TT, 0:nh, :]; b = r2[0:TT, 0:nh, :]
        self.tt(a, in_, cb, ALU.mult)
        self.tt(b[:, :, 0:16], in_[:, :, 16:32], sinT[0:TT, 0:16].us(1).bc([TT, nh, 16]), ALU.mult)
        self.tt(b[:, :, 16:32], in_[:, :, 0:16], sinT[0:TT, 16:32].us(1).bc([TT, nh, 16]), ALU.mult)
        self.tt(out, a, b, ALU.add)

    def kv_from_lat(self, lf, TT, B, wuk, wuv, gkn, KTdst, col0, banks):
        bl, bk_, bv, bkT = banks
        latb, latT, knf, Ks = B["latb"], B["latT"], B["knf"], B["Ks"]
        self.cp(latb[0:TT, :], lf[0:TT, :])
        pl = self.ps(bl, [128, 2, TT], BF16)
        for k in range(2):
            self.tr(pl[:, k, :], latb[0:TT, k * 128:(k + 1) * 128], self.identB[0:TT, 0:TT], last=(k == 1))
        self.cp(latT[:, :, 0:TT], pl, eng="act")
        yield
        pk = self.ps(bk_, [TT, 512], F32)
        for k in range(2):
            self.mm(pk, latT[:, k, 0:TT], wuk[:, k, :], k == 0, k == 1)
        kf = knf[0:TT].r("p h d -> p (h d)")
        self.cp(kf, pk, eng="act")
        sqv = B["sqk"][0:TT]
        self.act(sqv.r("p h d -> p (h d)"), pk, AF.Square)
        yield
        pv = self.ps(bv, [TT, 512], F32)
        for k in range(2):
            self.mm(pv, latT[:, k, 0:TT], wuv[:, k, :], k == 0, k == 1)
        self.cp(B["vb"][0:TT, :], pv, eng="act")
        yield
        self.red(B["ssk"][0:TT, :], sqv)
        self.rstd_from_ss(B["rsk"][0:TT, :], B["ssk"][0:TT, :], ND)
        yield
        self.tt(knf[0:TT], knf[0:TT], B["rsk"][0:TT, :].us(2).bc([TT, NH, ND]), ALU.mult)
        self.tt(Ks[0:TT, :, 0:ND], knf[0:TT], gkn[0:TT].us(1).bc([TT, NH, ND]), ALU.mult)
        yield
        pkT = self.ps(bkT, [HD, NH, TT], BF16)
        for h in range(NH):
            self.tr(pkT[:, h, :], Ks[0:TT, h, :], self.identB[0:TT, 0:TT], last=(h == NH - 1))
        self.cp(KTdst[:, :, col0:col0 + TT], pkT, eng="act")

    def phase1b(self):
        d = self.din
        work = [(si, kt_) for si, (kind, idx, Tq, P) in enumerate(self.seqs) for kt_ in range(P // 128)]
        if not work:
            return
        self.off = self.base_off
        stage = [self.sb("wstgA1b", [128, 1024], F32), self.sb("wstgB1b", [128, 1024], F32)]
        wuk = self.sb("wuk1b", [128, 2, 512], BF16); self.load_weight(wuk, d["w_uk"], KVL, 0, 512, None, stage)
        wuv = self.sb("wuv1b", [128, 2, 512], BF16); self.load_weight(wuv, d["w_uv"], KVL, 0, 512, None, stage)
        gkn = self.load_vec_bc("gkn1b", d["g_k_nope"], ND)
        NT = 8
        slots = []
        for i in range(NT):
            slots.append(dict(
                latf=self.sb("c_latf%d" % i, [128, KVL], F32), krf=self.sb("c_krf%d" % i, [128, RD], F32),
                latb=self.sb("c_latb%d" % i, [128, KVL], BF16), latT=self.sb("c_latT%d" % i, [128, 2, 128], BF16),
                knf=self.sb("c_knf%d" % i, [128, NH, ND], F32), sqk=self.sb("c_sqk%d" % i, [128, NH, ND], BF16),
                Ks=self.sb("c_Ks%d" % i, [128, NH, HD], BF16), vb=self.sb("c_vb%d" % i, [128, 512], BF16),
                ssk=self.sb("c_ssk%d" % i, [128, 8], F32), rsk=self.sb("c_rsk%d" % i, [128, 8], F32),
                KTc=self.sb("c_KTc%d" % i, [HD, NH, 128], BF16)))

        def cache_tile(si, kt_, th):
            B = slots[th]
            io = self.seq_io(si)
            KTd, VSd = self.dscr["KT" + str(si)], self.dscr["VS" + str(si)]
            lf = B["latf"]; kf = B["krf"]
            self.ld("c_latf%d" % th, lf, io["clat"][kt_ * 128:(kt_ + 1) * 128, :])
            self.ld("c_krf%d" % th, kf, io["ckr"][kt_ * 128:(kt_ + 1) * 128, :])
            self.cp(B["Ks"][:, :, ND:HD], kf.us(1).bc([128, NH, RD]))
            yield
            yield from self.kv_from_lat(lf, 128, B, wuk, wuv, gkn, B["KTc"], 0, (th, th, th, th))
            self.st("c_KTc%d" % th, KTd[:, :, kt_ * 128:(kt_ + 1) * 128], B["KTc"])
            self.st("c_vb%d" % th, VSd[kt_ * 128:(kt_ + 1) * 128, :], B["vb"])

        threads = [None] * NT
        nxt = 0
        while nxt < len(work) or any(t is not None for t in threads):
            for th in range(NT):
                if threads[th] is None and nxt < len(work):
                    threads[th] = cache_tile(work[nxt][0], work[nxt][1], th)
                    nxt += 1
                if threads[th] is not None:
                    try:
                        next(threads[th])
                    except StopIteration:
                        threads[th] = None

    def phase2(self):
        d = self.din
        self.off = self.base_off
        NQ = self.NQ
        xt = [self.sb("x2t%d" % i, [128, D], F32) for i in range(2)]
        wo = self.sb("wo", [128, 8, D], BF16)
        self.load_weight(wo, d["w_out_ab"], D, 0, D, None, xt)
        Kmax = max(P + Tq for (_, _, Tq, P) in self.seqs)
        nktmax = (Kmax + 127) // 128
        KTs = self.sb("KTs", [HD, NH, Kmax], BF16)
        Vp = self.sb("Vp", [128, nktmax, NH, 65], BF16)
        self.memset(Vp[:, :, :, 64:65], 1.0)
        nchmax = (Kmax + 511) // 512
        KTc_ = [KTs[:, :, c * 512:min(Kmax, (c + 1) * 512)].named("KTs_c%d" % c) for c in range(nchmax)]
        Vpc_ = [Vp[:, c * 4:min(nktmax, (c + 1) * 4), :, :].named("Vp_c%d" % c) for c in range(nchmax)]
        self.S.barrier()
        QTq = [self.sb("QTq%d" % i, [HD, NH, NQ], BF16) for i in range(2)]
        GMq = [self.sb("GMq%d" % i, [128, 4, NQ], BF16) for i in range(2)]
        mixs = [self.sb("mix%d" % i, [128, 8, NQ], BF16) for i in range(2)]
        NPT = 4
        PT = [self.sb("PT%d" % i, [128, NQ], BF16) for i in range(NPT)]
        rsum = [self.sb("rsum%d" % i, [1, NQ], F32) for i in range(2)]
        bcs = [self.sb("bcs%d" % i, [64, NQ], F32) for i in range(2)]
        stage = [self.sb("wstgA2", [128, 1024], F32), self.sb("wstgB2", [128, 1024], F32)]
        g_c = self.load_vec_pk("g_c", d["norm_c"], D)
        WC = self.dscr["WC"]
        wtmp = [self.sb("wtmp%d" % i, [128, 1024], BF16) for i in range(2)]
        wi = 0
        srcv = d["w_in_c"].r("(k p) n -> p k n", p=128)
        WCv = WC.r("(k p) n -> p k n", p=128)
        for k in range(8):
            for c in range(3):
                stg = stage[wi % 2]; wt = wtmp[wi % 2]
                self.ld("wstg%d" % (wi % 2), stg, srcv[:, k, c * 1024:(c + 1) * 1024])
                self.ts(wt, stg, g_c[:, k:k + 1], None, ALU.mult, eng="dve")
                self.st("wtmp%d" % (wi % 2), WCv[:, k, c * 1024:(c + 1) * 1024], wt)
                wi += 1
        xi = 0
        qbc = 0
        qlist = [(si_, qb_) for si_, (_k, _i, Tq_, _P) in enumerate(self.seqs) for qb_ in range(Tq_ // min(NQ, Tq_))]

        def issue_q(gk):
            si_, qb_ = qlist[gk]
            Tq_ = self.seqs[si_][2]
            nq_ = min(NQ, Tq_); q0_ = qb_ * nq_; sl_ = gk % 2
            self.ld("QTq%d" % sl_, QTq[sl_][:, :, 0:nq_], self.dscr["QT%d" % si_][:, :, q0_:q0_ + nq_])
            self.ld("GMq%d" % sl_, GMq[sl_][:, :, 0:nq_], self.dscr["GM%d" % si_][:, :, q0_:q0_ + nq_])
            self.ld("mixms%d" % sl_, mixs[sl_][:, 4:8, 0:nq_], self.dscr["MS%d" % si_][:, :, q0_:q0_ + nq_])

        for si, (kind, idx, Tq, P) in enumerate(self.seqs):
            io = self.seq_io(si)
            QTd, KTd, VSd, GMd, MSd, X1d = (self.dscr[n + str(si)] for n in ("QT", "KT", "VS", "GM", "MS", "X1"))
            Ktot = P + Tq
            nkt = (Ktot + 127) // 128
            nch = (Ktot + 511) // 512
            for c in range(nch):
                c0k = c * 512; c1k = min(Ktot, c0k + 512)
                self.ld("KTs_c%d" % c, KTc_[c][:, :, 0:c1k - c0k], KTd[:, :, c0k:c1k])
                for kt_ in range(c * 4, min(nkt, c * 4 + 4)):
                    kk = min(128, Ktot - kt_ * 128)
                    self.ld("Vp_c%d" % c, Vpc_[c][0:kk, kt_ - c * 4, :, 0:64], VSd[kt_ * 128:kt_ * 128 + kk, :].r("k (h d) -> k h d", h=NH))
            nq = min(NQ, Tq)
            TT = min(128, nq)
            for qb in range(Tq // nq):
                q0 = qb * nq
                sl = qbc % 2
                Qq, Gq, mix = QTq[sl], GMq[sl], mixs[sl]
                if qbc == 0:
                    issue_q(0)
                if qbc + 1 < len(qlist):
                    issue_q(qbc + 1)
                qbc += 1
                if kind == "p":
                    tiles = [(kt_, 128, 0, False) for kt_ in range(q0 // 128)]
                    tiles += [(q0 // 128 + m, 128, 128 * m, True) for m in range(nq // 128)]
                else:
                    tiles = [(kt_, min(128, Ktot - kt_ * 128), 0, False) for kt_ in range(nkt)]
                ntl = len(tiles)
                if kind == "s" and NH * nq <= NQ:
                    pOall = self.ps(4, [65, NH, nq], F32)
                    win = []
                    for i in range(ntl + 2):
                        if i < ntl:
                            kt_, kk, c0, diag = tiles[i]
                            pS = self.ps(i % NPT, [128, NH, nq], F32)
                            for h in range(NH):
                                self.mm(pS[0:kk, h, :], KTc_[kt_ // 4][:, h, (kt_ % 4) * 128:(kt_ % 4) * 128 + kk], Qq[:, h, 0:nq], True, True, last=(h == NH - 1))
                            pt = PT[i % NPT]
                            self.act(pt[0:kk, 0:NH * nq], pS[0:kk].r("k h q -> k (h q)"), AF.Exp, scale=ATTN_SCALE)
                            win.append((i, pt))
                        if i >= 2:
                            j, pt = win.pop(0)
                            kt_, kk, c0, diag = tiles[j]
                            for h in range(NH):
                                self.mm(pOall[:, h, :], Vpc_[kt_ // 4][0:kk, kt_ % 4, h, :], pt[0:kk, h * nq:(h + 1) * nq], j == 0 and h == 0, j == ntl - 1 and h == NH - 1, last=(h == NH - 1))
                    self.recip(rsum[0][:, 0:NH * nq], pOall[64:65].r("o h q -> o (h q)"))
                    pBc = self.ps(6, [64, NH * nq], F32)
                    self.mm(pBc, self.onesF[0:1, 0:64], rsum[0][:, 0:NH * nq], True, True)
                    self.cp(bcs[0][:, 0:NH * nq], pBc, eng="act")
                    for h in range(NH):
                        hs = h % 2
                        dst = mix[hs * 64:hs * 64 + 64, h // 2, 0:nq]
                        self.tt(dst, pOall[0:64, h, :], bcs[0][:, h * nq:(h + 1) * nq], ALU.mult)
                        self.tt(dst, dst, Gq[hs * 64:hs * 64 + 64, h // 2, 0:nq], ALU.mult, eng="pool")
                    tiles = []
                    ntl = 0
                items = [(h, ti) for h in range(NH) for ti in range(ntl)]
                pending = []
                LOOK = 3
                window = []
                for i, item in enumerate(items + [None] * LOOK):
                    cur = None
                    if item is not None:
                        h, ti = item
                        kt_, kk, c0, diag = tiles[ti]
                        pS = self.ps(i % NPT, [128, nq], F32)
                        pt = PT[i % NPT]
                        self.mm(pS[0:kk, c0:nq], KTc_[kt_ // 4][:, h, (kt_ % 4) * 128:(kt_ % 4) * 128 + kk], Qq[:, h, c0:nq], True, True)
                        self.act(pt[0:kk, c0:nq], pS[0:kk, c0:nq], AF.Exp, scale=ATTN_SCALE)
                        if diag:
                            self.memset(pt[64:128, c0:c0 + 64], 0.0)
                        cur = (h, ti, pt)
                    window.append(cur)
                    prev = window.pop(0) if len(window) > LOOK else None
                    if prev is not None:
                        h, ti, pt = prev
                        kt_, kk, c0, diag = tiles[ti]
                        pO = self.ps(4 + (h % 2), [65, nq], F32)
                        self.mm(pO[:, c0:nq], Vpc_[kt_ // 4][0:kk, kt_ % 4, h, :], pt[0:kk, c0:nq], ti == 0, ti == ntl - 1)
                        if ti == ntl - 1:
                            hs = h % 2
                            self.recip(rsum[hs][:, 0:nq], pO[64:65, :])

                            def fin(h=h, hs=hs, pO=pO):
                                pBc = self.ps(6 + hs, [64, nq], F32)
                                self.mm(pBc, self.onesF[0:1, 0:64], rsum[hs][:, 0:nq], True, True)
                                self.cp(bcs[hs][:, 0:nq], pBc, eng="act")
                                dst = mix[hs * 64:hs * 64 + 64, h // 2, 0:nq]
                                self.tt(dst, pO[0:64, :], bcs[hs][:, 0:nq], ALU.mult)
                                self.tt(dst, dst, Gq[hs * 64:hs * 64 + 64, h // 2, 0:nq], ALU.mult, eng="pool")
                            pending.append((i + 2, fin))
                    while pending and (pending[0][0] <= i or i == len(items) + LOOK - 1):
                        pending.pop(0)[1]()
                for ti in range(nq // TT):
                    x = xt[xi % 2]; xi += 1
                    self.ld("x2t%d" % (xi % 2), x[0:TT, :], io["x"][q0 + ti * TT:q0 + (ti + 1) * TT, :])
                    for half in range(2):
                        po = self.ps(6 + half, [TT, 512], F32)
                        for k in range(8):
                            self.mm(po, mix[:, k, ti * TT:(ti + 1) * TT], wo[:, k, half * 512:(half + 1) * 512], k == 0, k == 7)
                        self.tt(x[0:TT, half * 512:(half + 1) * 512], x[0:TT, half * 512:(half + 1) * 512], po, ALU.add)
                    self.st("x2t%d" % (xi % 2), X1d[q0 + ti * TT:q0 + (ti + 1) * TT, :], x[0:TT, :])

    def phase3(self):
        d = self.din
        self.off = self.base_off
        NB = self.NB3
        xt = [self.sb("x3t%d" % i, [128, D], F32) for i in range(2)]
        xr = [self.sb("x3r%d" % i, [128, D], F32) for i in range(2)]
        stage = xt
        WC = self.dscr["WC"]
        WCv = WC.r("(k p) n -> p k n", p=128)
        woc = self.sb("woc", [128, 8, D], BF16)
        self.load_weight(woc, d["w_out_c"], D, 0, D, None, stage)
        cb = self.load_vec_pk("cb", d["conv_b"], D)
        lng = self.load_vec_pk("lng", d["ln_g"], D)
        lnb = self.load_vec_pk("lnb", d["ln_b"], D)
        cw = self.sb("cw", [128, 8, CW], F32)
        for k in range(8):
            self.ld("misc", cw[:, k, :], d["conv_w"][:, k * 128:(k + 1) * 128].r("t p -> p t"), slow=True)
        diag = self.sb("diag", [128, 8, CW, 128], BF16)
        for k in range(8):
            for t in range(CW):
                if t % 2:
                    self.ts(diag[:, k, t, :], self.identF, cw[:, k, t:t + 1], None, ALU.mult)
                else:
                    self.act(diag[:, k, t, :], self.identF, AF.Copy, scale=cw[:, k, t:t + 1])
        wch = [self.sb("wch%d" % i, [128, 8, 512], BF16) for i in range(2)]
        H = CW - 1
        junk = self.sb("junk3", [128, D], BF16)
        hb = self.sb("hb3", [128, D], BF16)
        hT = self.sb("hT3", [128, 8, NB], BF16)
        ss = self.sb("ss3", [128, 2], F32); rs = self.sb("rs3", [128, 2], F32)
        vT = self.sb("vT", [128, 8, H + NB], BF16)
        vtails = [self.sb("vtail%d" % i, [128, 8, 32], F32) for i in range(2)]
        gTs = [self.sb("gT%d" % i, [128, 8, NB], BF16) for i in range(2)]
        sgm = [self.sb("sgm%d" % i, [128, NB], F32) for i in range(2)]
        vf = [self.sb("vf%d" % i, [128, NB], F32) for i in range(2)]
        ybf = self.sb("ybf", [128, 8, NB], BF16)
        sqb = self.sb("sqb", [128, 8, NB], BF16)
        mean = self.sb("mean", [128, NB], F32); var = self.sb("var", [128, NB], F32); msq = self.sb("msq", [128, NB], F32)
        tmp = [self.sb("tmp3%d" % i, [128, NB], F32) for i in range(2)]
        halo = self.sb("halo", [128, 8, H], BF16)
        mT = self.sb("mT", [128, 8, NB], BF16)
        tailo = self.sb("tailo", [32, D], F32)
        scv = self.sb("scv", [32, D], F32)
        st_ = dict(wci=0, xi=0, xri=0)

        blocks = []
        for si, (kind, idx, Tq, P) in enumerate(self.seqs):
            nb = min(NB, Tq)
            for b in range(Tq // nb):
                blocks.append(dict(si=si, b=b, nb=nb, r=len(blocks), last=(b == Tq // nb - 1)))

        def wchunk(c0):
            wci = st_["wci"]; st_["wci"] += 1
            w = wch[wci % 2]
            self.ld("wch%d" % (wci % 2), w, WCv[:, :, c0:c0 + 512])
            return w

        def front(blk):
            si, b, nb, r = blk["si"], blk["b"], blk["nb"], blk["r"]
            kind, idx, Tq, P = self.seqs[si]
            io = self.seq_io(si)
            X1d = self.dscr["X1" + str(si)]
            TT = min(128, nb); ntile = nb // TT; t0 = b * nb
            gT = gTs[r % 2]
            if b == 0:
                if kind == "p":
                    self.memset(vT[:, :, 0:H], 0.0)
                else:
                    self.ld("scv", scv[0:H, :], io["sconv"])
                    for k in range(8):
                        pc = self.ps(7, [128, H], F32)
                        self.tr(pc, scv[0:H, k * 128:(k + 1) * 128], self.identF[0:H, 0:H])
                        self.cp(vT[:, k, 0:H], pc, eng="act")
            else:
                self.cp(vT[:, :, 0:H], halo)
            for ti in range(ntile):
                xi = st_["xi"]; st_["xi"] += 1
                x = xt[xi % 2]
                self.ld("x3t%d" % (xi % 2), x[0:TT, :], X1d[t0 + ti * TT:t0 + (ti + 1) * TT, :])
                self.act(junk[0:TT, :], x[0:TT, :], AF.Square, accum=ss[0:TT, 0:1])
                self.rstd_from_ss(rs[0:TT, 0:1], ss[0:TT, 0:1], D)
                self.ts(hb[0:TT, :], x[0:TT, :], rs[0:TT, 0:1], None, ALU.mult)
                pT = self.ps(0, [128, 8, TT], BF16)
                for k in range(8):
                    self.tr(pT[:, k, :], hb[0:TT, k * 128:(k + 1) * 128], self.identB[0:TT, 0:TT], last=(k == 7))
                self.cp(hT[:, :, ti * TT:(ti + 1) * TT], pT, eng="act")
            for half in range(2):
                wa = wchunk(half * 512)
                wb_ = wchunk(1024 + half * 512)
                for mm_ in range(4):
                    m = half * 4 + mm_
                    pa = self.ps(1 + 2 * (m % 2), [128, nb], F32); pb_ = self.ps(2 + 2 * (m % 2), [128, nb], F32)
                    for k in range(8):
                        self.mm(pa, wa[:, k, mm_ * 128:(mm_ + 1) * 128], hT[:, k, 0:nb], k == 0, k == 7)
                    for k in range(8):
                        self.mm(pb_, wb_[:, k, mm_ * 128:(mm_ + 1) * 128], hT[:, k, 0:nb], k == 0, k == 7)
                    sg = sgm[m % 2]; v_ = vf[m % 2]
                    self.act(sg[:, 0:nb], pb_, AF.Sigmoid)
                    self.tt(v_[:, 0:nb], pa, sg[:, 0:nb], ALU.mult)
                    self.cp(vT[:, m, H:H + nb], v_[:, 0:nb], eng="pool")
                    if blk["last"]:
                        self.cp(vtails[r % 2][:, m, :], v_[:, nb - 32:nb], eng="pool")
            for half in range(2):
                wg = wchunk(2048 + half * 512)
                for mm_ in range(4):
                    m = half * 4 + mm_
                    pg = self.ps(5 + (m % 2), [128, nb], F32)
                    for k in range(8):
                        self.mm(pg, wg[:, k, mm_ * 128:(mm_ + 1) * 128], hT[:, k, 0:nb], k == 0, k == 7)
                    self.act(gT[:, m, 0:nb], pg, AF.Silu)

        def conv_stats(blk):
            nb = blk["nb"]
            for m in range(8):
                pc = self.ps(5 + (m % 2), [128, nb], F32)
                for t in range(CW):
                    self.mm(pc, diag[:, m, t, :], vT[:, m, t:t + nb], t == 0, t == CW - 1)
                self.act(ybf[:, m, 0:nb], pc, AF.Identity, bias=cb[:, m:m + 1])
                self.act(sqb[:, m, 0:nb], pc, AF.Square, bias=cb[:, m:m + 1])
            self.cp(halo, vT[:, :, nb:nb + H], eng="pool")
            pst = self.ps(7, [128, nb], F32); pst2 = self.ps(0, [128, nb], F32)
            for m in range(8):
                self.mm(pst, self.onesB, ybf[:, m, 0:nb], m == 0, m == 7)
            for m in range(8):
                self.mm(pst2, self.onesB, sqb[:, m, 0:nb], m == 0, m == 7)
            self.ts(mean[:, 0:nb], pst, 1.0 / D, None, ALU.mult)
            self.tt(msq[:, 0:nb], mean[:, 0:nb], mean[:, 0:nb], ALU.mult)
            self.stt(var[:, 0:nb], pst2, 1.0 / D, msq[:, 0:nb], ALU.mult, ALU.subtract)
            self.ts(var[:, 0:nb], var[:, 0:nb], EPS, None, ALU.add)
            self.act(var[:, 0:nb], var[:, 0:nb], AF.Sqrt)
            self.recip(var[:, 0:nb], var[:, 0:nb])

        def back(blk):
            si, b, nb, r = blk["si"], blk["b"], blk["nb"], blk["r"]
            io = self.seq_io(si)
            X1d = self.dscr["X1" + str(si)]
            TT = min(128, nb); ntile = nb // TT; t0 = b * nb
            gT = gTs[r % 2]
            for m in range(8):
                tm = tmp[m % 2]
                self.tt(tm[:, 0:nb], ybf[:, m, 0:nb], mean[:, 0:nb], ALU.subtract)
                self.tt(tm[:, 0:nb], tm[:, 0:nb], var[:, 0:nb], ALU.mult, eng="pool")
                self.act(tm[:, 0:nb], tm[:, 0:nb], AF.Silu, bias=lnb[:, m:m + 1], scale=lng[:, m:m + 1])
                self.tt(mT[:, m, 0:nb], tm[:, 0:nb], gT[:, m, 0:nb], ALU.mult)

        def outproj(blk):
            si, b, nb, r = blk["si"], blk["b"], blk["nb"], blk["r"]
            io = self.seq_io(si)
            X1d = self.dscr["X1" + str(si)]
            TT = min(128, nb); ntile = nb // TT; t0 = b * nb
            for ti in range(ntile):
                xri = st_["xri"]; st_["xri"] += 1
                x = xr[xri % 2]
                self.ld("x3r%d" % (xri % 2), x[0:TT, :], X1d[t0 + ti * TT:t0 + (ti + 1) * TT, :])
                for half in range(2):
                    po = self.ps(1 + half, [TT, 512], F32)
                    for k in range(8):
                        self.mm(po, mT[:, k, ti * TT:(ti + 1) * TT], woc[:, k, half * 512:(half + 1) * 512], k == 0, k == 7)
                    self.tt(x[0:TT, half * 512:(half + 1) * 512], x[0:TT, half * 512:(half + 1) * 512], po, ALU.add)
                self.st("x3r%d" % (xri % 2), io["y"][t0 + ti * TT:t0 + (ti + 1) * TT, :], x[0:TT, :])
            if blk["last"]:
                for k in range(8):
                    pc = self.ps(7, [32, 128], F32)
                    self.tr(pc, vtails[r % 2][:, k, :], self.identF)
                    self.cp(tailo[:, k * 128:(k + 1) * 128], pc, eng="act")
                self.st("tailo", io["conv"], tailo[2:32, :])

        nblk = len(blocks)
        front(blocks[0])
        for r in range(nblk):
            conv_stats(blocks[r])
            back(blocks[r])
            if r + 1 < nblk:
                front(blocks[r + 1])
            outproj(blocks[r])


def host_consts(T, PAST, TS):
    half = RD // 2
    inv = (10000.0 ** (-np.arange(half, dtype=np.float32) / half)).astype(np.float32)
    def tab(pos):
        ang = pos.astype(np.float32)[:, None] * inv[None, :]
        c_ = np.cos(ang).astype(np.float32); s_ = np.sin(ang).astype(np.float32)
        return np.concatenate([c_, c_], axis=1), np.concatenate([-s_, s_], axis=1)
    cp_, sp_ = tab(np.arange(T))
    cs_, ss_ = tab(PAST + np.arange(TS))
    ident = np.eye(128, dtype=np.float32)
    s_idx = np.arange(128) // 16
    mask = (s_idx[None, :] >= s_idx[:, None]).astype(np.float32)
    return dict(c_ident=ident, c_mask=mask, c_cos_p=cp_, c_sin_p=sp_, c_cos_s=cs_, c_sin_s=ss_)


WEIGHT_MAP = dict(norm_ab="norm_ab", w_in_ab="w_in_ab", g_q_lat="g_q_lat", w_uq="w_uq", g_kv_lat="g_kv_lat",
                  w_uk="w_uk", w_uv="w_uv", g_q_nope="g_q_nope", g_q_rope="g_q_rope", g_k_nope="g_k_nope",
                  g_k_rope="g_k_rope", lam_re="s5_lam_re", lam_im="s5_lam_im", log_dt="s5_log_dt",
                  b_re="s5_b_re", b_im="s5_b_im", c_re="s5_c_re", c_im="s5_c_im", s5_d="s5_d", w_glu="s5_w_glu",
                  b_glu="s5_b_glu", w_out_ab="w_out_ab", norm_c="norm_c", w_in_c="w_in_c", conv_w="conv_w",
                  conv_b="conv_b", ln_g="ln_g", ln_b="ln_b", w_out_c="w_out_c")


def make_in_maps(inputs, n_cores, NP, NS, T, PAST, TS):
    f = lambda a: np.ascontiguousarray(np.asarray(a, dtype=np.float32))
    consts = host_consts(T, PAST, TS)
    w = {}
    for k, src in WEIGHT_MAP.items():
        a = f(inputs[src])[0]
        if k in ("w_uk", "w_uv"):
            a = a.reshape(a.shape[0], -1)
        w[k] = np.ascontiguousarray(a)
    maps = []
    for c in range(n_cores):
        m = dict(w)
        m.update(consts)
        m["xp"] = f(inputs["x_prompt"][c * NP:(c + 1) * NP])
        m["xs"] = f(inputs["x_sample"][c * NS:(c + 1) * NS])
        m["clat"] = f(inputs["cache_mla_latent"][0, c * NS:(c + 1) * NS])
        m["ckr"] = f(inputs["cache_mla_krope"][0, c * NS:(c + 1) * NS])
        m["sre"] = f(inputs["state_s5_re"][0, c * NS:(c + 1) * NS])
        m["sim"] = f(inputs["state_s5_im"][0, c * NS:(c + 1) * NS])
        m["sconv"] = f(inputs["state_conv"][0, c * NS:(c + 1) * NS])
        maps.append(m)
    return maps


def assemble(results):
    cat = lambda k: np.concatenate([r[k] for r in results], axis=0)
    yp, ys = cat("yp"), cat("ys")
    st = lambda k: cat(k)[None]
    return (yp, ys, st("lat_p"), st("kr_p"), st("re_p"), st("im_p"), st("conv_p"),
            st("lat_s"), st("kr_s"), st("re_s"), st("im_s"), st("conv_s"))


def kernel(**inputs):
    n = 8
    B, T = inputs["x_prompt"].shape[0], inputs["x_prompt"].shape[1]
    BS, TS = inputs["x_sample"].shape[0], inputs["x_sample"].shape[1]
    PAST = inputs["cache_mla_latent"].shape[2]
    NP, NS = B // n, BS // n
    kb = K(NP, T, NS, PAST, TS)
    nc = kb.build()
    maps = make_in_maps(inputs, n, NP, NS, T, PAST, TS)
    res = run_bass_kernel_spmd(nc, maps, core_ids=list(range(n)))
    outs = assemble(res.results)
    return tuple(np.asarray(o, dtype=np.float32) for o in outs)
```

```python
import math
import numpy as np
import concourse.bass as bass
import concourse.mybir as mybir
from concourse.bass_utils import run_bass_kernel_spmd

F32 = mybir.dt.float32
BF16 = mybir.dt.bfloat16
U8 = mybir.dt.uint8
ALU = mybir.AluOpType
AF = mybir.ActivationFunctionType
AX = mybir.AxisListType

D = 1024
QL, KVL, RD, MW, SW = 384, 256, 32, 512, 512
NH, ND, VD = 8, 64, 64
HD = ND + RD
CW = 31
EPS = 1e-6
ATTN_SCALE = float(HD ** -0.5)
MAGIC = 12582912.0
TWO_PI = 2.0 * math.pi

COMPUTE = ("pe", "act", "dve", "pool")


class V:
    def __init__(self, ap, res):
        self.ap = ap
        self.res = res if isinstance(res, tuple) else (res,)

    def __getitem__(self, k):
        return V(self.ap[k], self.res)

    def r(self, pat, **kw):
        return V(self.ap.rearrange(pat, **kw), self.res)

    def bc(self, shape):
        return V(self.ap.broadcast_to(list(shape)), self.res)

    def us(self, axis):
        return V(self.ap.unsqueeze(axis), self.res)

    def named(self, *res):
        return V(self.ap, tuple(res))

    @property
    def shape(self):
        return self.ap.shape


class Sched:
    def __init__(self, nc):
        self.nc = nc
        self.q = {e: [] for e in COMPUTE + ("sp",)}
        self.cnt = {}
        self.sems = {}
        self.known = {e: {} for e in COMPUTE + ("sp",)}
        self.lastw = {}
        self.readers = {}
        self.n_inst = 0

    def _sem(self, key):
        if key not in self.sems:
            self.sems[key] = self.nc.alloc_semaphore("s_" + key)
            self.cnt[key] = 0
        return self.sems[key]

    def _deps(self, reads, writes):
        deps = {}

        def add(tok):
            if tok is None:
                return
            k, v = tok
            if deps.get(k, 0) < v:
                deps[k] = v
        for r in reads:
            add(self.lastw.get(r))
        for w in writes:
            add(self.lastw.get(w))
            for k, v in self.readers.get(w, {}).items():
                add((k, v))
        return deps

    def _emit_waits(self, engine, deps):
        kn = self.known[engine]
        for k, v in deps.items():
            if k == engine and engine == "pe":
                continue
            if kn.get(k, 0) >= v:
                continue
            kn[k] = v
            sem = self.sems[k]
            self.q[engine].append(lambda eng, sem=sem, v=v: eng.wait_ge(sem, v))

    def _commit(self, tok, reads, writes):
        k, v = tok
        for r in reads:
            d = self.readers.setdefault(r, {})
            if d.get(k, 0) < v:
                d[k] = v
        for w in writes:
            self.lastw[w] = tok
            self.readers[w] = {}

    def op(self, engine, fn, reads=(), writes=(), inc=True):
        deps = self._deps(reads, writes)
        self._emit_waits(engine, deps)
        sem = self._sem(engine)
        if inc:
            self.cnt[engine] += 1
            n = self.cnt[engine]
            self.q[engine].append(lambda eng, fn=fn, sem=sem: fn(eng).then_inc(sem, 1))
        else:
            n = self.cnt[engine] + 1
            self.q[engine].append(lambda eng, fn=fn: fn(eng))
        self._commit((engine, n), reads, writes)
        self.n_inst += 1

    def dma(self, stream, fn, reads=(), writes=(), queue="sp"):
        deps = self._deps(reads, writes)
        key = "d" + queue[0] + "_" + stream
        sem = self._sem(key)
        if self.cnt[key] > 0:
            deps[key] = max(deps.get(key, 0), self.cnt[key])
        self._emit_waits(queue, deps)
        self.cnt[key] += 16
        v = self.cnt[key]
        self.q[queue].append(lambda eng, fn=fn, sem=sem: fn(eng).then_inc(sem, 16))
        self._commit((key, v), reads, writes)
        self.n_inst += 1

    def barrier(self):
        deps = {k: self.cnt[k] for k in self.sems if self.cnt[k] > 0}
        for e in COMPUTE + ("sp",):
            self._emit_waits(e, dict(deps))

    def finish(self):
        deps = {k: self.cnt[k] for k in self.sems if self.cnt[k] > 0}
        self._emit_waits("sp", deps)

    def run(self):
        nc = self.nc
        with nc.Block() as block:
            @block.sync
            def _(eng):
                for f in self.q["sp"]:
                    f(eng)

            @block.tensor
            def _(eng):
                for f in self.q["pe"]:
                    f(eng)

            @block.scalar
            def _(eng):
                for f in self.q["act"]:
                    f(eng)

            @block.vector
            def _(eng):
                for f in self.q["dve"]:
                    f(eng)

            @block.gpsimd
            def _(eng):
                for f in self.q["pool"]:
                    f(eng)


def _res(*vs):
    out = []
    for v in vs:
        if v is None or isinstance(v, (int, float)):
            continue
        out.extend(v.res)
    return tuple(out)


def _ap(v):
    return v.ap if isinstance(v, V) else v


class K:
    def __init__(self, NP, T, NS, PAST, TS=64, NB1=256, NQ=512, NB3=512):
        self.NP, self.T, self.NS, self.PAST, self.TS = NP, T, NS, PAST, TS
        self.NB1, self.NQ, self.NB3 = NB1, NQ, NB3
        self.nc = nc = bass.Bass("TRN2", target_bir_lowering=False)
        self.S = Sched(nc)
        self.seqs = [("p", i, T, 0) for i in range(NP)] + [("s", i, TS, PAST) for i in range(NS)]
        self.arena = nc.alloc_sbuf_tensor("arena", [128, 204 * 1024], U8)
        self.off = 0
        self.uid = 0
        self.pb = [nc.alloc_psum_tensor("pb%d" % i, [128, 512], F32) for i in range(8)]
        self.din = {}
        self.dout = {}
        self.dscr = {}

    def sb(self, name, shape, dt, parts=128):
        nbytes = int(np.prod(shape[1:])) * (4 if dt == F32 else 2)
        nbytes = (nbytes + 63) // 64 * 64
        assert self.off + nbytes <= 204 * 1024, (name, self.off, nbytes)
        ap = self.arena[0:shape[0], self.off:self.off + nbytes].bitcast(dt)
        used = int(np.prod(shape[1:]))
        ap = ap[:, 0:used]
        if len(shape) == 3:
            ap = ap.rearrange("p (a b) -> p a b", a=shape[1])
        elif len(shape) == 4:
            ap = ap.rearrange("p (a b c) -> p a b c", a=shape[1], b=shape[2])
        elif len(shape) == 5:
            ap = ap.rearrange("p (a b c d) -> p a b c d", a=shape[1], b=shape[2], c=shape[3])
        self.off += nbytes
        self.uid += 1
        return V(ap, "%s#%d" % (name, self.uid))

    def ps(self, bank, shape, dt=F32, res=None):
        t = self.pb[bank]
        ap = t[0:shape[0], :]
        if dt == BF16:
            ap = ap.bitcast(BF16)
        used = int(np.prod(shape[1:]))
        ap = ap[:, 0:used]
        if len(shape) == 3:
            ap = ap.rearrange("p (a b) -> p a b", a=shape[1])
        elif len(shape) == 4:
            ap = ap.rearrange("p (a b c) -> p a b c", a=shape[1], b=shape[2])
        return V(ap, res or ("pb%d" % bank))

    def inp(self, name, shape, dt=F32):
        ap = self.nc.dram_tensor(name, list(shape), dt, kind="ExternalInput").ap()
        self.din[name] = V(ap, "in_" + name)
        return self.din[name]

    def outp(self, name, shape):
        ap = self.nc.dram_tensor(name, list(shape), F32, kind="ExternalOutput").ap()
        self.dout[name] = V(ap, "out_" + name)
        return self.dout[name]

    def scr(self, name, shape, dt):
        ap = self.nc.dram_tensor(name, list(shape), dt, kind="Internal").ap()
        self.dscr[name] = V(ap, "scr_" + name)
        return self.dscr[name]

    def act(self, out, in_, func, bias=None, scale=None, accum=None):
        kw = {}
        if bias is not None:
            kw["bias"] = _ap(bias)
        if scale is not None:
            kw["scale"] = _ap(scale)
        if accum is not None:
            kw["accum_out"] = accum.ap
        self.S.op("act", lambda e: e.activation(out=out.ap, in_=in_.ap, func=func, **kw),
                  reads=_res(in_, bias, scale), writes=_res(out, accum))

    def ts(self, out, in0, s1, s2, op0, op1=None, eng="dve"):
        kw = {}
        if op1 is not None:
            kw["op1"] = op1
        self.S.op(eng, lambda e: e.tensor_scalar(out=out.ap, in0=in0.ap, scalar1=_ap(s1), scalar2=_ap(s2), op0=op0, **kw),
                  reads=_res(in0, s1, s2), writes=_res(out))

    def tt(self, out, in0, in1, op, eng="dve"):
        self.S.op(eng, lambda e: e.tensor_tensor(out=out.ap, in0=in0.ap, in1=in1.ap, op=op),
                  reads=_res(in0, in1), writes=_res(out))

    def stt(self, out, in0, scalar, in1, op0, op1, eng="dve"):
        self.S.op(eng, lambda e: e.scalar_tensor_tensor(out=out.ap, in0=in0.ap, scalar=_ap(scalar), in1=in1.ap, op0=op0, op1=op1),
                  reads=_res(in0, scalar, in1), writes=_res(out))

    def cp(self, out, in_, eng="dve"):
        if eng == "act":
            self.S.op("act", lambda e: e.copy(out=out.ap, in_=in_.ap), reads=_res(in_), writes=_res(out))
        else:
            self.S.op(eng, lambda e: e.tensor_copy(out=out.ap, in_=in_.ap), reads=_res(in_), writes=_res(out))

    def recip(self, out, in_):
        self.S.op("dve", lambda e: e.reciprocal(out=out.ap, in_=in_.ap), reads=_res(in_), writes=_res(out))

    def red(self, out, in_, eng="dve"):
        self.S.op(eng, lambda e: e.tensor_reduce(out=out.ap, in_=in_.ap, axis=AX.X, op=ALU.add),
                  reads=_res(in_), writes=_res(out))

    def memset(self, out, val, eng="pool"):
        self.S.op(eng, lambda e: e.memset(out.ap, val), writes=_res(out))

    def mm(self, out, lhsT, rhs, start, stop, last=None):
        if last is None:
            last = stop
        self.S.op("pe", lambda e: e.matmul(out.ap, lhsT=lhsT.ap, rhs=rhs.ap, start=start, stop=stop),
                  reads=_res(lhsT, rhs), writes=_res(out), inc=last)

    def tr(self, out, in_, ident, last=True):
        self.S.op("pe", lambda e: e.transpose(out=out.ap, in_=in_.ap, identity=ident.ap),
                  reads=_res(in_, ident), writes=_res(out), inc=last)

    def ld(self, stream, out, in_, slow=False):
        kw = {"allow_slow_non_contiguous": True} if slow else {}
        extra = ("miscq",) if stream == "misc" else ()
        self.S.dma(stream, lambda e: e.dma_start(out=out.ap, in_=in_.ap, **kw), reads=_res(in_), writes=_res(out) + extra, queue="sp")

    def st(self, stream, out, in_, slow=False):
        kw = {"allow_slow_non_contiguous": True} if slow else {}
        self.S.dma(stream, lambda e: e.dma_start(out=out.ap, in_=in_.ap, **kw), reads=_res(in_), writes=_res(out), queue=getattr(self, "st_queue", "pool"))

    def rstd_from_ss(self, rstd, ss, n):
        self.ts(rstd, ss, 1.0 / n, EPS, ALU.mult, ALU.add)
        self.act(rstd, rstd, AF.Sqrt)
        self.recip(rstd, rstd)

    def declare_io(self):
        NP, T, NS, PAST, TS = self.NP, self.T, self.NS, self.PAST, self.TS
        i = self.inp
        i("xp", [NP, T, D]); i("xs", [NS, TS, D])
        i("clat", [NS, PAST, KVL]); i("ckr", [NS, PAST, RD])
        i("sre", [NS, 32, 64]); i("sim", [NS, 32, 64]); i("sconv", [NS, CW - 1, D])
        i("norm_ab", [D]); i("w_in_ab", [D, 2208]); i("g_q_lat", [QL]); i("w_uq", [QL, NH * HD])
        i("g_kv_lat", [KVL]); i("w_uk", [KVL, NH * ND]); i("w_uv", [KVL, NH * VD])
        i("g_q_nope", [ND]); i("g_q_rope", [RD]); i("g_k_nope", [ND]); i("g_k_rope", [RD])
        i("lam_re", [32, 64]); i("lam_im", [32, 64]); i("log_dt", [32])
        i("b_re", [32, 64, 16]); i("b_im", [32, 64, 16]); i("c_re", [32, 16, 64]); i("c_im", [32, 16, 64])
        i("s5_d", [SW]); i("w_glu", [SW, SW]); i("b_glu", [SW]); i("w_out_ab", [D, D])
        i("norm_c", [D]); i("w_in_c", [D, 3 * D]); i("conv_w", [CW, D]); i("conv_b", [D])
        i("ln_g", [D]); i("ln_b", [D]); i("w_out_c", [D, D])
        i("c_ident", [128, 128]); i("c_mask", [128, 128])
        i("c_cos_p", [T, 32]); i("c_sin_p", [T, 32]); i("c_cos_s", [TS, 32]); i("c_sin_s", [TS, 32])
        o = self.outp
        o("yp", [NP, T, D]); o("ys", [NS, TS, D])
        o("lat_p", [NP, T, KVL]); o("kr_p", [NP, T, RD]); o("re_p", [NP, 32, 64]); o("im_p", [NP, 32, 64])
        o("conv_p", [NP, CW - 1, D])
        o("lat_s", [NS, TS, KVL]); o("kr_s", [NS, TS, RD]); o("re_s", [NS, 32, 64]); o("im_s", [NS, 32, 64])
        o("conv_s", [NS, CW - 1, D])
        for si, (kind, idx, Tq, P) in enumerate(self.seqs):
            self.scr("QT%d" % si, [HD, NH, Tq], BF16)
            self.scr("KT%d" % si, [HD, NH, P + Tq], BF16)
            self.scr("VS%d" % si, [P + Tq, NH * VD], BF16)
            self.scr("GM%d" % si, [128, 4, Tq], BF16)
            self.scr("MS%d" % si, [128, 4, Tq], BF16)
            self.scr("X1%d" % si, [Tq, D], F32)
        self.scr("WC", [D, 3 * D], BF16)

    def seq_io(self, si):
        kind, idx, Tq, P = self.seqs[si]
        if kind == "p":
            return dict(x=self.din["xp"][idx], y=self.dout["yp"][idx], lat=self.dout["lat_p"][idx], kr=self.dout["kr_p"][idx],
                        re=self.dout["re_p"][idx], im=self.dout["im_p"][idx], conv=self.dout["conv_p"][idx],
                        cos=self.din["c_cos_p"], sin=self.din["c_sin_p"])
        return dict(x=self.din["xs"][idx], y=self.dout["ys"][idx], lat=self.dout["lat_s"][idx], kr=self.dout["kr_s"][idx],
                    re=self.dout["re_s"][idx], im=self.dout["im_s"][idx], conv=self.dout["conv_s"][idx],
                    cos=self.din["c_cos_s"], sin=self.din["c_sin_s"],
                    clat=self.din["clat"][idx], ckr=self.din["ckr"][idx], sre=self.din["sre"][idx], sim=self.din["sim"][idx],
                    sconv=self.din["sconv"][idx])

    def load_weight(self, dst, src, Kd, c0, c1, gain=None, stage=None):
        nk = Kd // 128
        n = c1 - c0
        srcv = src.r("(k p) n -> p k n", p=128)
        for k in range(nk):
            stg = stage[self._stg % 2]
            self._stg += 1
            self.ld("wstg%d" % (self._stg % 2), stg[:, 0:n], srcv[:, k, c0:c1])
            eng = "dve" if (self._stg % 2) else "pool"
            if gain is not None:
                self.ts(dst[:, k, :], stg[:, 0:n], gain[:, k:k + 1], None, ALU.mult, eng=eng)
            else:
                self.cp(dst[:, k, :], stg[:, 0:n], eng=eng)

    def load_vec_pk(self, name, src, Kd):
        v = self.sb(name, [128, Kd // 128], F32)
        self.ld("misc", v, src.r("(k p) -> p k", p=128), slow=True)
        return v

    def load_vec_bc(self, name, src, n, parts=128):
        v = self.sb(name, [parts, n], F32)
        self.ld("misc", v, V(src.ap.partition_broadcast(parts), src.res))
        return v

    def build(self):
        self.declare_io()
        self._stg = 0
        self.phase_const()
        self.phase1()
        self.st_queue = "pool"
        self.S.barrier()
        self.phase1b()
        self.S.barrier()
        if getattr(self, "STOP", 9) >= 2:
            self.phase2()
            self.S.barrier()
        if getattr(self, "STOP", 9) >= 3:
            self.phase3()
        self.S.finish()
        self.S.run()
        return self.nc

    def phase_const(self):
        self.identF = self.sb("identF", [128, 128], F32)
        self.ld("misc", self.identF, self.din["c_ident"])
        self.identB = self.sb("identB", [128, 128], BF16)
        self.cp(self.identB, self.identF)
        self.onesB = self.sb("onesB", [128, 128], BF16)
        self.memset(self.onesB, 1.0)
        self.onesF = self.sb("onesF", [128, 128], F32)
        self.memset(self.onesF, 1.0)
        self.base_off = self.off

    def s5_setup(self):
        d = self.din
        mark = None
        self.BkT = self.sb("BkT", [128, 32, 2, 64], BF16)
        self.Cq = self.sb("Cq", [128, 32, 128], BF16)
        self.T0 = self.sb("T0", [128, 32, 128], BF16)
        self.lamA = self.sb("lamA", [64, 2, 32], F32)
        self.lamAn = self.sb("lamAn", [64, 32], F32)
        self.LPr = self.sb("LPr", [64, 32, 8], F32)
        self.LPpm = self.sb("LPpm", [64, 2, 32, 8], F32)
        self.s5_persist_end = self.off
        if hasattr(self, "_p1_weights"):
            self._p1_weights()
        keep = self.off
        P = 64
        lre = self.sb("lre", [P, 32], F32); lim = self.sb("lim", [P, 32], F32)
        self.ld("misc", lre, d["lam_re"].r("g p -> p g"), slow=True)
        self.ld("misc", lim, d["lam_im"].r("g p -> p g"), slow=True)
        dt = self.load_vec_bc("dt", d["log_dt"], 32, parts=P)
        self.act(dt, dt, AF.Exp)
        ar = self.sb("ar", [P, 32], F32); ai = self.sb("ai", [P, 32], F32)
        self.tt(ar, lre, dt, ALU.mult); self.tt(ai, lim, dt, ALU.mult)
        NPW = 16
        pwr = self.sb("pwr", [P, 32, NPW], F32); pwi = self.sb("pwi", [P, 32, NPW], F32)
        mag = self.sb("mag", [P, 32], F32); ang = self.sb("ang", [P, 32], F32); kk = self.sb("kk", [P, 32], F32)
        sn = self.sb("sn", [P, 32], F32); cs = self.sb("cs", [P, 32], F32)
        for mi in range(NPW):
            m = float(mi - 7)
            self.act(mag, ar, AF.Exp, scale=m)
            for (dst, shift) in ((sn, 0.0), (cs, math.pi / 2)):
                self.ts(ang, ai, m, shift, ALU.mult, ALU.add)
                self.ts(kk, ang, 1.0 / TWO_PI, MAGIC, ALU.mult, ALU.add)
                self.ts(kk, kk, -MAGIC, -TWO_PI, ALU.add, ALU.mult)
                self.tt(ang, ang, kk, ALU.add)
                self.act(dst, ang, AF.Sin)
            self.tt(pwr[:, :, mi], mag, cs, ALU.mult)
            self.tt(pwi[:, :, mi], mag, sn, ALU.mult)
        for i in range(8):
            m = 8.0 * (i + 1)
            self.act(mag, ar, AF.Exp, scale=m)
            for (dst, shift) in ((sn, 0.0), (cs, math.pi / 2)):
                self.ts(ang, ai, m, shift, ALU.mult, ALU.add)
                self.ts(kk, ang, 1.0 / TWO_PI, MAGIC, ALU.mult, ALU.add)
                self.ts(kk, kk, -MAGIC, -TWO_PI, ALU.add, ALU.mult)
                self.tt(ang, ang, kk, ALU.add)
                self.act(dst, ang, AF.Sin)
            self.tt(self.LPr[:, :, i], mag, cs, ALU.mult)
            self.tt(self.LPpm[:, 1, :, i], mag, sn, ALU.mult)
            self.ts(self.LPpm[:, 0, :, i], self.LPpm[:, 1, :, i], -1.0, None, ALU.mult)
        self.cp(self.lamA[:, 0, :], pwr[:, :, 15]); self.cp(self.lamA[:, 1, :], pwi[:, :, 15])
        self.ts(self.lamAn, pwi[:, :, 15], -1.0, None, ALU.mult)
        nr = self.sb("nr", [P, 32], F32); den = self.sb("den", [P, 32], F32); t1 = self.sb("t1", [P, 32], F32); t2 = self.sb("t2", [P, 32], F32)
        kre = self.sb("kre", [P, 32], F32); kim = self.sb("kim", [P, 32], F32)
        self.ts(nr, pwr[:, :, 8], -1.0, None, ALU.add)
        ni = pwi[:, :, 8]
        self.tt(den, lre, lre, ALU.mult); self.tt(t1, lim, lim, ALU.mult); self.tt(den, den, t1, ALU.add); self.recip(den, den)
        self.tt(t1, nr, lre, ALU.mult); self.tt(t2, ni, lim, ALU.mult); self.tt(t1, t1, t2, ALU.add); self.tt(kre, t1, den, ALU.mult)
        self.tt(t1, ni, lre, ALU.mult); self.tt(t2, nr, lim, ALU.mult); self.tt(t1, t1, t2, ALU.subtract); self.tt(kim, t1, den, ALU.mult)
        bre = self.sb("bre", [P, 32, 16], F32); bim = self.sb("bim", [P, 32, 16], F32)
        self.ld("misc", bre, d["b_re"].r("g p n -> p g n")); self.ld("misc", bim, d["b_im"].r("g p n -> p g n"))
        bbr = self.sb("bbr", [P, 32, 16], F32); bbi = self.sb("bbi", [P, 32, 16], F32); tb = self.sb("tb", [P, 32, 16], F32)
        kreb = kre.us(2).bc([P, 32, 16]); kimb = kim.us(2).bc([P, 32, 16])
        self.tt(bbr, bre, kreb, ALU.mult); self.tt(tb, bim, kimb, ALU.mult); self.tt(bbr, bbr, tb, ALU.subtract)
        self.tt(bbi, bim, kreb, ALU.mult); self.tt(tb, bre, kimb, ALU.mult); self.tt(bbi, bbi, tb, ALU.add)
        cre = self.sb("cre", [P, 32, 16], F32); cim = self.sb("cim", [P, 32, 16], F32)
        self.ld("misc", cre, d["c_re"].r("g n p -> p g n"), slow=True); self.ld("misc", cim, d["c_im"].r("g n p -> p g n"), slow=True)
        Bkr = self.sb("Bkr", [P, 32, 8, 16], F32); Bki = self.sb("Bki", [P, 32, 8, 16], F32); tB = self.sb("tB", [P, 32, 16], F32)
        for s in range(8):
            pr = pwr[:, :, 14 - s].us(2).bc([P, 32, 16]); pi_ = pwi[:, :, 14 - s].us(2).bc([P, 32, 16])
            self.tt(Bkr[:, :, s, :], bbr, pr, ALU.mult); self.tt(tB, bbi, pi_, ALU.mult); self.tt(Bkr[:, :, s, :], Bkr[:, :, s, :], tB, ALU.subtract)
            self.tt(Bki[:, :, s, :], bbr, pi_, ALU.mult); self.tt(tB, bbi, pr, ALU.mult); self.tt(Bki[:, :, s, :], Bki[:, :, s, :], tB, ALU.add)
        CLr = self.sb("CLr", [P, 32, 8, 16], F32); CLi = self.sb("CLi", [P, 32, 8, 16], F32); tC = self.sb("tC", [P, 32, 16], F32)
        tCr = self.sb("tCr", [P, 32, 16], F32); tCi = self.sb("tCi", [P, 32, 16], F32)
        Cq4r = self.Cq[0:64].r("p g (s n) -> p g s n", s=8); Cq4i = self.Cq[64:128].r("p g (s n) -> p g s n", s=8)
        for mi in range(NPW):
            pr = pwr[:, :, mi].us(2).bc([P, 32, 16]); pi_ = pwi[:, :, mi].us(2).bc([P, 32, 16])
            dr = CLr[:, :, mi, :] if mi < 8 else tCr
            di = CLi[:, :, mi, :] if mi < 8 else tCi
            self.tt(dr, cre, pr, ALU.mult); self.tt(tC, cim, pi_, ALU.mult); self.tt(dr, dr, tC, ALU.subtract)
            self.tt(di, cre, pi_, ALU.mult); self.tt(tC, cim, pr, ALU.mult); self.tt(di, di, tC, ALU.add)
            self.ts(di, di, -1.0, None, ALU.mult)
            if mi >= 8:
                self.cp(Cq4r[:, :, mi - 8, :], tCr)
                self.cp(Cq4i[:, :, mi - 8, :], tCi)
        maskS = self.sb("maskS", [128, 128], F32)
        self.ld("misc", maskS, d["c_mask"])
        dsel = self.sb("dsel", [128, 32], F32)
        for s in range(8):
            self.ld("misc", dsel[s * 16:(s + 1) * 16, :], d["s5_d"].r("(g n) -> n g", n=16), slow=True)
        diagd = self.sb("diagd", [128, 128], F32)
        for g in range(32):
            bk = g % 2
            pT = self.ps(bk, [128, 2, 64], F32)
            self.tr(pT[:, 0, :], Bkr[:, g, :, :].r("p s n -> p (s n)"), self.identF[0:64, 0:64], last=False)
            self.tr(pT[:, 1, :], Bki[:, g, :, :].r("p s n -> p (s n)"), self.identF[0:64, 0:64], last=True)
            self.cp(self.BkT[:, g, :, :], pT, eng="act")
            p0 = self.ps(2 + bk, [128, 128], F32)
            self.mm(p0, Bkr[:, g, :, :].r("p s n -> p (s n)"), CLr[:, g, 0:8, :].r("p s n -> p (s n)"), True, False)
            self.mm(p0, Bki[:, g, :, :].r("p s n -> p (s n)"), CLi[:, g, 0:8, :].r("p s n -> p (s n)"), False, True)
            self.ts(diagd, self.identF, dsel[:, g:g + 1], None, ALU.mult, eng="pool")
            t0f = self.sb("t0f", [128, 128], F32) if g == 0 else t0f
            self.tt(t0f, p0, maskS, ALU.mult)
            self.tt(self.T0[:, g, :], t0f, diagd, ALU.add)
        self.s5_tmp_end = self.off
        return keep

    def phase1(self):
        d = self.din
        NB = self.NB1
        W_ = {}

        def p1_weights():
            stage = [self.sb("wstgA", [128, 1024], F32), self.sb("wstgB", [128, 1024], F32)]
            W_["stage"] = stage
            g_norm = self.load_vec_pk("g_norm", d["norm_ab"], D)
            g_ql = self.load_vec_pk("g_ql", d["g_q_lat"], QL)
            self.ts(g_ql, g_ql, math.sqrt(QL), None, ALU.mult)
            for nm, shape in (("wA", [128, 8, 672]), ("wGm", [128, 8, 512]), ("wU", [128, 8, 512]), ("wGs", [128, 8, 512]),
                              ("wuq", [128, 3, 768]), ("wuk", [128, 2, 512]), ("wuv", [128, 2, 512]), ("wglu", [128, 4, 512])):
                W_[nm] = self.sb(nm, shape, BF16)
            self.load_weight(W_["wA"], d["w_in_ab"], D, 0, 672, g_norm, stage)
            self.load_weight(W_["wGm"], d["w_in_ab"], D, 672, 1184, g_norm, stage)
            self.load_weight(W_["wU"], d["w_in_ab"], D, 1184, 1696, g_norm, stage)
            self.load_weight(W_["wGs"], d["w_in_ab"], D, 1696, 2208, g_norm, stage)
            self.load_weight(W_["wuq"], d["w_uq"], QL, 0, 768, g_ql, stage)
            self.load_weight(W_["wuk"], d["w_uk"], KVL, 0, 512, None, stage)
            self.load_weight(W_["wuv"], d["w_uv"], KVL, 0, 512, None, stage)
            self.load_weight(W_["wglu"], d["w_glu"], SW, 0, 512, None, stage)
        self._p1_weights = p1_weights
        keep = self.s5_setup()
        self.S.barrier()
        self.off = keep
        wA, wGm, wU, wGs, wuq, wuk, wuv, wglu = (W_[k] for k in ("wA", "wGm", "wU", "wGs", "wuq", "wuk", "wuv", "wglu"))
        xt = W_["stage"]
        if False:
            g_norm = None
        bglu = self.load_vec_pk("bglu", d["b_glu"], SW)
        gkv = self.load_vec_bc("gkv", d["g_kv_lat"], KVL)
        gkr = self.load_vec_bc("gkr", d["g_k_rope"], RD)
        gq = self.sb("gq", [128, HD], F32)
        self.ld("misc", gq[:, 0:ND], V(d["g_q_nope"].ap.partition_broadcast(128), d["g_q_nope"].res))
        self.ld("misc", gq[:, ND:HD], V(d["g_q_rope"].ap.partition_broadcast(128), d["g_q_rope"].res))
        gkn = self.load_vec_bc("gkn", d["g_k_nope"], ND)
        self.ts(gkv, gkv, math.sqrt(KVL), None, ALU.mult)
        self.ts(gkr, gkr, math.sqrt(RD), None, ALU.mult)
        self.ts(gq[:, 0:ND], gq[:, 0:ND], math.sqrt(ND), None, ALU.mult)
        self.ts(gq[:, ND:HD], gq[:, ND:HD], math.sqrt(RD), None, ALU.mult)
        nepsA = self.sb("nepsA", [128, 8], F32)
        self.memset(nepsA, KVL * EPS)
        self.memset(nepsA[:, 1:2], QL * EPS)
        self.memset(nepsA[:, 4:5], RD * EPS)
        nepsQ = self.sb("nepsQ", [128, 16], F32)
        self.memset(nepsQ[:, 0:8], ND * EPS)
        self.memset(nepsQ[:, 8:16], RD * EPS)

        def two(name, shape, dt, parts=128):
            return [self.sb(name + str(i), [parts] + shape, dt) for i in range(2)]
        TB = dict(
            hb=two("hb", [D], BF16), ss=two("ss", [8], F32), rs=two("rs", [8], F32), ssq=two("ssq", [16], F32), rsq=two("rsq", [16], F32),
            ssk=two("ssk", [8], F32), rsk=two("rsk", [8], F32), qlb=two("qlb", [QL], BF16), qlT=two("qlT", [3, 128], BF16),
            latf=two("latf", [KVL], F32), latb=two("latb", [KVL], BF16), latT=two("latT", [2, 128], BF16),
            krf=two("krf", [RD], F32), krn=two("krn", [RD], F32), cosT=two("cosT", [32], F32), sinT=two("sinT", [32], F32),
            r1=two("r1", [NH, 32], F32), r2=two("r2", [NH, 32], F32), r1k=two("r1k", [1, 32], F32), r2k=two("r2k", [1, 32], F32), qn=two("qn", [NH, HD], F32), sqq=two("sqq", [NH, HD], BF16),
            Qs=two("Qs", [NH, HD], BF16), Ks=two("Ks", [NH, HD], BF16), knf=two("knf", [NH, ND], F32), sqk=two("sqk", [NH, ND], BF16),
            vb=two("vb", [512], BF16))
        hTs = [self.sb("hT%d" % i, [128, 8, NB], BF16) for i in range(2)]
        KTc = [self.sb("KTc%d" % i, [HD, NH, 128], BF16) for i in range(2)]
        QTb = self.sb("QTb", [HD, NH, NB], BF16)
        KTb = self.sb("KTb", [HD, NH, NB], BF16)
        GMt = self.sb("GMt", [128, 4, NB], BF16)
        gsT = [self.sb("gsT%d" % i, [128, 4, NB], BF16) for i in range(3)]
        J = NB // 8
        F1 = getattr(self, "F1", True)
        X2s = self.sb("X2s", [64, 16, 8, 16], BF16) if F1 else self.sb("X2s", [J, 32, 8, 16], BF16)
        Usel = [self.sb("Usel%d" % i, [128, 32, J], BF16) for i in range(3)]
        Sb = [self.sb("Sb%d" % i, [64, 2, 32, J], F32) for i in range(2)]
        Sp = [self.sb("Sp%d" % i, [128, 32, J], BF16) for i in range(2)]
        carry = self.sb("carry", [64, 2, 32], F32)
        sq1 = self.sb("sq1", [64, 2, 32, 4], F32); sq2 = self.sb("sq2", [64, 2, 32, 4], F32)
        sl1 = self.sb("sl1", [64, 2, 32, 8], F32); sl2 = self.sb("sl2", [64, 2, 32, 8], F32)
        Z2 = self.sb("Z2", [64, 8, 16, 16], BF16) if F1 else self.sb("Z2", [J, 8, 32, 16], BF16)
        yz = self.sb("yz", [J, 4, 128], F32); yz2 = self.sb("yz2", [J, 4, 128], F32)
        zT = self.sb("zT", [128, 4, NB], BF16)
        sig = self.sb("sig", [128, NB], F32)
        MSt = self.sb("MSt", [128, 4, NB], BF16)
        self.p1_end = self.off

        lamre_b = self.lamA[:, 0, :].us(1).bc([64, 2, 32])
        lam_pm = self.sb("lam_pm", [64, 2, 32], F32)
        self.cp(lam_pm[:, 0, :], self.lamAn); self.cp(lam_pm[:, 1, :], self.lamA[:, 1, :])

        st_ = dict(xi=0, tc=0)

        def tile_front(io, TT, tk0, ti, VSd, P, bk2, hT):
            b0, b1 = bk2
            xi = st_["xi"]; st_["xi"] += 1
            tp = st_["tc"] % 2; st_["tc"] += 1
            B = {k: v[tp] for k, v in TB.items()}
            x = xt[xi % 2]
            ss, rs = B["ss"], B["rs"]
            self.ld("xt%d" % (xi % 2), x[0:TT, :], io["x"][tk0:tk0 + TT, :])
            self.ld("cos%d" % tp, B["cosT"][0:TT, :], io["cos"][tk0:tk0 + TT, :])
            self.ld("sin%d" % tp, B["sinT"][0:TT, :], io["sin"][tk0:tk0 + TT, :])
            self.act(B["hb"][0:TT, :], x[0:TT, :], AF.Square, accum=ss[0:TT, 0:1])
            self.rstd_from_ss(rs[0:TT, 0:1], ss[0:TT, 0:1], D)
            self.ts(B["hb"][0:TT, :], x[0:TT, :], rs[0:TT, 0:1], None, ALU.mult)
            yield
            pT = self.ps(b0, [128, 8, TT], BF16)
            for k in range(8):
                self.tr(pT[:, k, :], B["hb"][0:TT, k * 128:(k + 1) * 128], self.identB[0:TT, 0:TT], last=(k == 7))
            self.cp(hT[:, :, ti * TT:(ti + 1) * TT], pT, eng="act")
            yield
            pA = self.ps(b0, [TT, 512], F32); pB = self.ps(b1, [TT, 160], F32)
            for k in range(8):
                self.mm(pA, hT[:, k, ti * TT:(ti + 1) * TT], wA[:, k, 0:512], k == 0, k == 7)
            for k in range(8):
                self.mm(pB, hT[:, k, ti * TT:(ti + 1) * TT], wA[:, k, 512:672], k == 0, k == 7)
            yield
            trash = B["sqq"][0:TT].r("p h d -> p (h d)")
            self.act(trash[:, 0:QL], pA[:, 0:QL], AF.Square, accum=ss[0:TT, 1:2])
            self.act(trash[:, 0:128], pA[:, QL:512], AF.Square, accum=ss[0:TT, 2:3])
            self.act(trash[:, 0:128], pB[:, 0:128], AF.Square, accum=ss[0:TT, 3:4])
            self.act(trash[:, 0:RD], pB[:, 128:160], AF.Square, accum=ss[0:TT, 4:5])
            self.tt(ss[0:TT, 2:3], ss[0:TT, 2:3], ss[0:TT, 3:4], ALU.add)
            self.tt(rs[0:TT, 1:5], ss[0:TT, 1:5], nepsA[0:TT, 1:5], ALU.add)
            self.act(rs[0:TT, 1:5], rs[0:TT, 1:5], AF.Sqrt)
            self.recip(rs[0:TT, 1:5], rs[0:TT, 1:5])
            yield
            self.ts(B["qlb"][0:TT, :], pA[:, 0:QL], rs[0:TT, 1:2], None, ALU.mult)
            lf = B["latf"]
            self.stt(lf[0:TT, 0:128], pA[:, QL:512], rs[0:TT, 2:3], gkv[0:TT, 0:128], ALU.mult, ALU.mult)
            self.stt(lf[0:TT, 128:256], pB[:, 0:128], rs[0:TT, 2:3], gkv[0:TT, 128:256], ALU.mult, ALU.mult)
            self.st("latf%d" % tp, io["lat"][tk0:tk0 + TT, :], lf[0:TT, :])
            self.stt(B["krn"][0:TT, :], pB[:, 128:160], rs[0:TT, 4:5], gkr[0:TT, :], ALU.mult, ALU.mult)
            yield
            def q_chain():
                pQT = self.ps(b0, [128, 3, TT], BF16)
                for k in range(3):
                    self.tr(pQT[:, k, :], B["qlb"][0:TT, k * 128:(k + 1) * 128], self.identB[0:TT, 0:TT], last=(k == 2))
                self.cp(B["qlT"][:, :, 0:TT], pQT, eng="act")
                yield
                qn = B["qn"]
                qf = qn[0:TT].r("p h d -> p (h d)")
                pQ1 = self.ps(b0, [TT, 512], F32)
                for k in range(3):
                    self.mm(pQ1, B["qlT"][:, k, 0:TT], wuq[:, k, 0:512], k == 0, k == 2)
                self.cp(qf[:, 0:512], pQ1, eng="act")
                yield
                pQ2 = self.ps(b0, [TT, 256], F32)
                for k in range(3):
                    self.mm(pQ2, B["qlT"][:, k, 0:TT], wuq[:, k, 512:768], k == 0, k == 2)
                self.cp(qf[:, 512:768], pQ2, eng="act")
                yield
                sqv = B["sqq"][0:TT]
                self.act(sqv.r("p h d -> p (h d)"), qf, AF.Square)
                self.red(B["ssq"][0:TT, 0:8], sqv[:, :, 0:ND])
                self.red(B["ssq"][0:TT, 8:16], sqv[:, :, ND:HD])
                yield
                self.tt(B["rsq"][0:TT, :], B["ssq"][0:TT, :], nepsQ[0:TT, :], ALU.add)
                self.act(B["rsq"][0:TT, :], B["rsq"][0:TT, :], AF.Sqrt)
                self.recip(B["rsq"][0:TT, :], B["rsq"][0:TT, :])
                yield
                self.tt(qn[0:TT, :, 0:ND], qn[0:TT, :, 0:ND], B["rsq"][0:TT, 0:8].us(2).bc([TT, NH, ND]), ALU.mult)
                self.tt(qn[0:TT, :, ND:HD], qn[0:TT, :, ND:HD], B["rsq"][0:TT, 8:16].us(2).bc([TT, NH, RD]), ALU.mult)
                self.tt(qn[0:TT], qn[0:TT], gq[0:TT].us(1).bc([TT, NH, HD]), ALU.mult)
                yield
                self.cp(B["Qs"][0:TT, :, 0:ND], qn[0:TT, :, 0:ND], eng="act")
                self.rope(B["Qs"][0:TT, :, ND:HD], qn[0:TT, :, ND:HD], B["cosT"], B["sinT"], B["r1"], B["r2"], TT, NH)
                yield
                pq = self.ps(b0, [HD, NH, TT], BF16)
                for h in range(NH):
                    self.tr(pq[:, h, :], B["Qs"][0:TT, h, :], self.identB[0:TT, 0:TT], last=(h == NH - 1))
                self.cp(QTb[:, :, ti * TT:(ti + 1) * TT], pq, eng="act")

            def k_chain():
                kf = B["krf"]
                self.rope(kf[0:TT, :].r("p (o r) -> p o r", o=1), B["krn"][0:TT, :].r("p (o r) -> p o r", o=1), B["cosT"], B["sinT"], B["r1k"], B["r2k"], TT, 1)
                self.st("krf%d" % tp, io["kr"][tk0:tk0 + TT, :], kf[0:TT, :])
                self.cp(B["Ks"][0:TT, :, ND:HD], kf[0:TT, :].us(1).bc([TT, NH, RD]))
                yield
                yield from self.kv_from_lat(lf, TT, B, wuk, wuv, gkn, KTb, ti * TT, (b1, b1, b1, b1))
                self.st("vb%d" % tp, VSd[P + tk0:P + tk0 + TT, :], B["vb"][0:TT, :])

            gens = [q_chain(), k_chain()]
            alive = [True, True]
            while any(alive):
                for gi in range(2):
                    if alive[gi]:
                        try:
                            next(gens[gi])
                        except StopIteration:
                            alive[gi] = False
                yield

        def cache_tile(io, kt_, KTd, VSd, tp, banks):
            B = {k: v[tp] for k, v in TB.items()}
            lf = B["latf"]; kf = B["krf"]
            self.ld("latf%d" % tp, lf, io["clat"][kt_ * 128:(kt_ + 1) * 128, :])
            self.ld("krf%d" % tp, kf, io["ckr"][kt_ * 128:(kt_ + 1) * 128, :])
            self.cp(B["Ks"][:, :, ND:HD], kf.us(1).bc([128, NH, RD]))
            kb = KTc[tp]
            yield
            yield from self.kv_from_lat(lf, 128, B, wuk, wuv, gkn, kb, 0, banks)
            self.st("KTc%d" % tp, KTd[:, :, kt_ * 128:(kt_ + 1) * 128], kb[:, :, 0:128])
            self.st("vb%d" % tp, VSd[kt_ * 128:(kt_ + 1) * 128, :], B["vb"])


        def stageA_pre(blk):
            si, b = blk["si"], blk["b"]
            kind, idx, Tq, P = self.seqs[si]
            io = self.seq_io(si)
            KTd, VSd = self.dscr["KT" + str(si)], self.dscr["VS" + str(si)]
            nb = blk["nb"]; TT = min(128, nb); ntile = nb // TT
            t0 = b * nb
            if False:
                yield
            blk["tile_gens"] = [tile_front(io, TT, t0 + ti * TT, ti, VSd, P, ((0, 1), (2, 3))[ti % 2], hTs[blk["r"] % 2]) for ti in range(ntile)]

        def tiles_done(blk):
            si, b, nb = blk["si"], blk["b"], blk["nb"]
            kind, idx, Tq, P = self.seqs[si]
            t0 = b * nb
            QTd, KTd = self.dscr["QT" + str(si)], self.dscr["KT" + str(si)]
            self.st("QTb", QTd[:, :, t0:t0 + nb], QTb[:, :, 0:nb])
            self.st("KTb", KTd[:, :, P + t0:P + t0 + nb], KTb[:, :, 0:nb])

        def stageA2(blk):
            si, b, slot3, slot2 = blk["si"], blk["b"], blk["r"] % 3, blk["r"] % 2
            GMd = self.dscr["GM" + str(si)]
            nb = blk["nb"]; Jb = nb // 8; t0 = b * nb
            hT = hTs[slot2]
            for (wG, dst) in ((wGm, GMt), (wGs, gsT[slot3])):
                for m in range(4):
                    pg = self.ps(4 + (m % 2), [128, nb], F32)
                    for k in range(8):
                        self.mm(pg, wG[:, k, m * 128:(m + 1) * 128], hT[:, k, 0:nb], k == 0, k == 7)
                    self.act(dst[:, m, 0:nb], pg, AF.Silu)
                    yield
            self.st("GMt", GMd[:, :, t0:t0 + nb], GMt[:, :, 0:nb])
            for s_ in range(8):
                pu = self.ps(4 + (s_ % 2), [Jb, 512], F32)
                for k in range(8):
                    self.mm(pu, hT[:, k, s_:nb:8], wU[:, k, :], k == 0, k == 7)
                e_ = "act" if s_ % 2 else "dve"
                if F1:
                    self.cp(X2s[0:Jb, :, s_, :], pu[:, 0:256].r("j (g n) -> j g n", g=16), eng=e_)
                    self.cp(X2s[32:32 + Jb, :, s_, :], pu[:, 256:512].r("j (g n) -> j g n", g=16), eng=e_)
                else:
                    self.cp(X2s[0:Jb, :, s_, :], pu.r("j (g n) -> j g n", g=32), eng=e_)
                yield
            for g4 in range(8):
                pus = self.ps(4 + (g4 % 2), [128, 4, Jb], BF16)
                for gg in range(4):
                    g = g4 * 4 + gg
                    p0 = (g // 16) * 32
                    if F1:
                        self.tr(pus[:, gg, :], X2s[p0:p0 + Jb, g % 16, :, :].r("j s n -> j (s n)"), self.identB[p0:p0 + Jb, p0:p0 + Jb], last=(gg == 3))
                    else:
                        self.tr(pus[:, gg, :], X2s[0:Jb, g, :, :].r("j s n -> j (s n)"), self.identB[0:Jb, 0:Jb], last=(gg == 3))
                self.cp(Usel[slot3][:, g4 * 4:(g4 + 1) * 4, 0:Jb], pus, eng=("act" if g4 % 2 else "dve"))
                yield
            for g8 in range(4):
                pw = self.ps(4 + (g8 % 2), [64, 2, 8, Jb], F32)
                for gg in range(8):
                    g = g8 * 8 + gg
                    for c in range(2):
                        self.mm(pw[:, c, gg, :], self.BkT[:, g, c, :], Usel[slot3][:, g, 0:Jb], True, True, last=(gg == 7 and c == 1))
                self.cp(Sb[slot2][:, :, g8 * 8:(g8 + 1) * 8, 0:Jb], pw, eng=("act" if g8 % 2 else "dve"))
                yield

        def stageB(blk):
            si, b, slot2 = blk["si"], blk["b"], blk["r"] % 2
            kind, idx, Tq, P = self.seqs[si]
            io = self.seq_io(si)
            Jb = blk["nb"] // 8
            S_ = Sb[slot2]; Spb = Sp[slot2]
            if b == 0:
                if P > 0:
                    self.ld("carry", carry[:, 0, :], io["sre"].r("g p -> p g"), slow=True)
                    self.ld("carry", carry[:, 1, :], io["sim"].r("g p -> p g"), slow=True)
                else:
                    self.memset(carry, 0.0)
            self.cp(Spb[0:64, :, 0], carry[:, 0, :], eng="pool")
            self.cp(Spb[64:128, :, 0], carry[:, 1, :], eng="pool")
            L = 8; Q = Jb // L
            S5 = S_[:, :, :, 0:Jb].r("p c g (q l) -> p c g q l", l=L)
            a1 = sq1[:, :, :, 0:Q]; a2 = sq2[:, :, :, 0:Q]
            lre_q = self.lamA[:, 0, :].us(1).us(3).bc([64, 2, 32, Q])
            for i in range(1, L):
                prev = S5[:, :, :, :, i - 1]; cur = S5[:, :, :, :, i]
                self.tt(a1, prev, lre_q, ALU.mult, eng="pool")
                self.tt(a2[:, 0], prev[:, 1], lam_pm[:, 0, :].us(2).bc([64, 32, Q]), ALU.mult, eng="pool")
                self.tt(a2[:, 1], prev[:, 0], lam_pm[:, 1, :].us(2).bc([64, 32, Q]), ALU.mult, eng="pool")
                self.tt(a1, a1, a2, ALU.add, eng="pool")
                self.tt(cur, cur, a1, ALU.add, eng="pool")
                yield
            LPr_b = self.LPr.us(1).bc([64, 2, 32, L])
            for q in range(Q):
                C = carry if q == 0 else S5[:, :, :, q - 1, L - 1]
                self.tt(sl1, LPr_b, C.us(3).bc([64, 2, 32, L]), ALU.mult, eng="pool")
                self.tt(sl2[:, 0], self.LPpm[:, 0], C[:, 1].us(2).bc([64, 32, L]), ALU.mult, eng="pool")
                self.tt(sl2[:, 1], self.LPpm[:, 1], C[:, 0].us(2).bc([64, 32, L]), ALU.mult, eng="pool")
                self.tt(sl1, sl1, sl2, ALU.add, eng="pool")
                self.tt(S5[:, :, :, q, :], S5[:, :, :, q, :], sl1, ALU.add, eng="pool")
                yield
            self.cp(carry, S_[:, :, :, Jb - 1], eng="pool")
            if Jb > 1:
                self.cp(Spb[0:64, :, 1:Jb], S_[:, 0, :, 0:Jb - 1], eng="pool")
                self.cp(Spb[64:128, :, 1:Jb], S_[:, 1, :, 0:Jb - 1], eng="pool")
            if b == Tq // blk["nb"] - 1:
                self.st("fin", io["re"].r("g p -> p g"), carry[:, 0, :], slow=True)
                self.st("fin", io["im"].r("g p -> p g"), carry[:, 1, :], slow=True)
            yield

        def stageC(blk):
            si, b, slot3, slot2 = blk["si"], blk["b"], blk["r"] % 3, blk["r"] % 2
            MSd = self.dscr["MS" + str(si)]
            nb = blk["nb"]; Jb = nb // 8; t0 = b * nb
            U_ = Usel[slot3]; Spb = Sp[slot2]; gs_ = gsT[slot3]
            for g4 in range(8):
                py = self.ps(6 + (g4 % 2), [Jb, 4, 128], F32)
                for gg in range(4):
                    g = g4 * 4 + gg
                    self.mm(py[:, gg, :], U_[:, g, 0:Jb], self.T0[:, g, :], True, False)
                    self.mm(py[:, gg, :], Spb[:, g, 0:Jb], self.Cq[:, g, :], False, True, last=(gg == 3))
                self.cp(yz[0:Jb], py, eng="act")
                self.tt(yz2[0:Jb], yz[0:Jb], yz[0:Jb], ALU.mult)
                self.ts(yz2[0:Jb], yz2[0:Jb], 0.044715, 1.0, ALU.mult, ALU.add)
                self.tt(yz2[0:Jb], yz2[0:Jb], yz[0:Jb], ALU.mult)
                self.act(yz2[0:Jb], yz2[0:Jb], AF.Sigmoid, scale=2.0 * math.sqrt(2.0 / math.pi))
                zp0 = (g4 // 4) * 32; zg0 = (g4 % 4) * 4
                if not F1:
                    zp0 = 0; zg0 = g4 * 4
                self.tt(Z2[zp0:zp0 + Jb, :, zg0:zg0 + 4, :].r("j s g n -> j g s n"), yz[0:Jb].r("j g (s n) -> j g s n", s=8),
                        yz2[0:Jb].r("j g (s n) -> j g s n", s=8), ALU.mult)
                yield
            if getattr(self, 'STOPC', 9) < 2:
                return
            for s in range(8):
                if F1:
                    pzA = self.ps(6, [128, 2, Jb], BF16); pzB = self.ps(7, [128, 2, Jb], BF16)
                    for m in range(2):
                        self.tr(pzA[:, m, :], Z2[0:Jb, s, m * 8:(m + 1) * 8, :].r("j g n -> j (g n)"), self.identB[0:Jb, 0:Jb], last=(m == 1))
                    for m in range(2):
                        self.tr(pzB[:, m, :], Z2[32:32 + Jb, s, m * 8:(m + 1) * 8, :].r("j g n -> j (g n)"), self.identB[32:32 + Jb, 32:32 + Jb], last=(m == 1))
                    self.cp(zT[:, 0:2, s:nb:8], pzA, eng="act")
                    self.cp(zT[:, 2:4, s:nb:8], pzB, eng="dve")
                else:
                    pz = self.ps(6 + (s % 2), [128, 4, Jb], BF16)
                    for m in range(4):
                        self.tr(pz[:, m, :], Z2[0:Jb, s, m * 8:(m + 1) * 8, :].r("j g n -> j (g n)"), self.identB[0:Jb, 0:Jb], last=(m == 3))
                    self.cp(zT[:, :, s:nb:8], pz, eng=("act" if s % 2 else "dve"))
                yield
            if getattr(self, 'STOPC', 9) < 3:
                return
            for m in range(4):
                pg = self.ps(6 + (m % 2), [128, nb], F32)
                for k in range(4):
                    self.mm(pg, wglu[:, k, m * 128:(m + 1) * 128], zT[:, k, 0:nb], k == 0, k == 3)
                self.act(sig[:, 0:nb], pg, AF.Sigmoid, bias=bglu[:, m:m + 1])
                self.tt(sig[:, 0:nb], sig[:, 0:nb], zT[:, m, 0:nb], ALU.mult)
                self.tt(MSt[:, m, 0:nb], sig[:, 0:nb], gs_[:, m, 0:nb], ALU.mult)
                yield
            self.st("MSt", MSd[:, :, t0:t0 + nb], MSt[:, :, 0:nb])

        blocks = []
        for si, (kind, idx, Tq, P) in enumerate(self.seqs):
            nb = min(NB, Tq)
            for b in range(Tq // nb):
                blocks.append(dict(si=si, b=b, nb=nb, r=len(blocks)))

        def chain(*gens):
            for g in gens:
                if g is not None:
                    yield from g

        nblk = len(blocks)

        def step(g):
            try:
                return next(g), True
            except StopIteration:
                return None, False

        def get(fn, i):
            return fn(blocks[i]) if 0 <= i < nblk else iter(())

        SEQ_DRV = getattr(self, "SEQ_DRV", False)
        self.st_queue = "sp"
        for t in range(min(nblk + 3, getattr(self, "MAXR", 10 ** 9))):
            if SEQ_DRV:
                skip = getattr(self, "SKIP", "") if t == getattr(self, "MAXR", 10 ** 9) - 1 else ""
                for nm_, g_ in (("C", get(stageC, t - 3)), ("B", get(stageB, t - 2)), ("A2", get(stageA2, t - 1)), ("P", get(stageA_pre, t))):
                    if nm_ in skip.split(","):
                        continue
                    for _ in g_:
                        pass
                if t < nblk and "T" not in skip.split(","):
                    for tg in blocks[t]["tile_gens"]:
                        for _ in tg:
                            pass
                    tiles_done(blocks[t])
                continue
            pre = get(stageA_pre, t)
            others = [get(stageA2, t - 1), get(stageB, t - 2), get(stageC, t - 3)]
            o_alive = [True, True, True]
            p_alive = True
            while p_alive:
                _, p_alive = step(pre)
                for i in range(3):
                    if o_alive[i]:
                        _, o_alive[i] = step(others[i])
            tiles = list(blocks[t]["tile_gens"]) if t < nblk else []
            t_alive = [True] * len(tiles)
            while any(t_alive) or any(o_alive):
                for i, tg in enumerate(tiles):
                    if t_alive[i]:
                        _, t_alive[i] = step(tg)
                for i in range(3):
                    if o_alive[i]:
                        _, o_alive[i] = step(others[i])
            if t < nblk:
                tiles_done(blocks[t])

    def _end_phase1_marker(self):
        pass

    def rope(self, out, in_, cosT, sinT, r1, r2, TT, nh):
        cb = cosT[0:TT, :].us(1).bc([TT, nh, 32])
        a = r1[0:TT, 0:nh, :]; b = r2[0:TT, 0:nh, :]
        self.tt(a, in_, cb, ALU.mult)
        self.tt(b[:, :, 0:16], in_[:, :, 16:32], sinT[0:TT, 0:16].us(1).bc([TT, nh, 16]), ALU.mult)
        self.tt(b[:, :, 16:32], in_[:, :, 0:16], sinT[0:TT, 16:32].us(1).bc([TT, nh, 16]), ALU.mult)
        self.tt(out, a, b, ALU.add)

    def kv_from_lat(self, lf, TT, B, wuk, wuv, gkn, KTdst, col0, banks):
        bl, bk_, bv, bkT = banks
        latb, latT, knf, Ks = B["latb"], B["latT"], B["knf"], B["Ks"]
        self.cp(latb[0:TT, :], lf[0:TT, :])
        pl = self.ps(bl, [128, 2, TT], BF16)
        for k in range(2):
            self.tr(pl[:, k, :], latb[0:TT, k * 128:(k + 1) * 128], self.identB[0:TT, 0:TT], last=(k == 1))
        self.cp(latT[:, :, 0:TT], pl, eng="act")
        yield
        pk = self.ps(bk_, [TT, 512], F32)
        for k in range(2):
            self.mm(pk, latT[:, k, 0:TT], wuk[:, k, :], k == 0, k == 1)
        kf = knf[0:TT].r("p h d -> p (h d)")
        self.cp(kf, pk, eng="act")
        sqv = B["sqk"][0:TT]
        self.act(sqv.r("p h d -> p (h d)"), pk, AF.Square)
        yield
        pv = self.ps(bv, [TT, 512], F32)
        for k in range(2):
            self.mm(pv, latT[:, k, 0:TT], wuv[:, k, :], k == 0, k == 1)
        self.cp(B["vb"][0:TT, :], pv, eng="act")
        yield
        self.red(B["ssk"][0:TT, :], sqv)
        self.rstd_from_ss(B["rsk"][0:TT, :], B["ssk"][0:TT, :], ND)
        yield
        self.tt(knf[0:TT], knf[0:TT], B["rsk"][0:TT, :].us(2).bc([TT, NH, ND]), ALU.mult)
        self.tt(Ks[0:TT, :, 0:ND], knf[0:TT], gkn[0:TT].us(1).bc([TT, NH, ND]), ALU.mult)
        yield
        pkT = self.ps(bkT, [HD, NH, TT], BF16)
        for h in range(NH):
            self.tr(pkT[:, h, :], Ks[0:TT, h, :], self.identB[0:TT, 0:TT], last=(h == NH - 1))
        self.cp(KTdst[:, :, col0:col0 + TT], pkT, eng="act")

    def phase1b(self):
        d = self.din
        work = [(si, kt_) for si, (kind, idx, Tq, P) in enumerate(self.seqs) for kt_ in range(P // 128)]
        if not work:
            return
        self.off = self.base_off
        stage = [self.sb("wstgA1b", [128, 1024], F32), self.sb("wstgB1b", [128, 1024], F32)]
        wuk = self.sb("wuk1b", [128, 2, 512], BF16); self.load_weight(wuk, d["w_uk"], KVL, 0, 512, None, stage)
        wuv = self.sb("wuv1b", [128, 2, 512], BF16); self.load_weight(wuv, d["w_uv"], KVL, 0, 512, None, stage)
        gkn = self.load_vec_bc("gkn1b", d["g_k_nope"], ND)
        NT = 8
        slots = []
        for i in range(NT):
            slots.append(dict(
                latf=self.sb("c_latf%d" % i, [128, KVL], F32), krf=self.sb("c_krf%d" % i, [128, RD], F32),
                latb=self.sb("c_latb%d" % i, [128, KVL], BF16), latT=self.sb("c_latT%d" % i, [128, 2, 128], BF16),
                knf=self.sb("c_knf%d" % i, [128, NH, ND], F32), sqk=self.sb("c_sqk%d" % i, [128, NH, ND], BF16),
                Ks=self.sb("c_Ks%d" % i, [128, NH, HD], BF16), vb=self.sb("c_vb%d" % i, [128, 512], BF16),
                ssk=self.sb("c_ssk%d" % i, [128, 8], F32), rsk=self.sb("c_rsk%d" % i, [128, 8], F32),
                KTc=self.sb("c_KTc%d" % i, [HD, NH, 128], BF16)))

        def cache_tile(si, kt_, th):
            B = slots[th]
            io = self.seq_io(si)
            KTd, VSd = self.dscr["KT" + str(si)], self.dscr["VS" + str(si)]
            lf = B["latf"]; kf = B["krf"]
            self.ld("c_latf%d" % th, lf, io["clat"][kt_ * 128:(kt_ + 1) * 128, :])
            self.ld("c_krf%d" % th, kf, io["ckr"][kt_ * 128:(kt_ + 1) * 128, :])
            self.cp(B["Ks"][:, :, ND:HD], kf.us(1).bc([128, NH, RD]))
            yield
            yield from self.kv_from_lat(lf, 128, B, wuk, wuv, gkn, B["KTc"], 0, (th, th, th, th))
            self.st("c_KTc%d" % th, KTd[:, :, kt_ * 128:(kt_ + 1) * 128], B["KTc"])
            self.st("c_vb%d" % th, VSd[kt_ * 128:(kt_ + 1) * 128, :], B["vb"])

        threads = [None] * NT
        nxt = 0
        while nxt < len(work) or any(t is not None for t in threads):
            for th in range(NT):
                if threads[th] is None and nxt < len(work):
                    threads[th] = cache_tile(work[nxt][0], work[nxt][1], th)
                    nxt += 1
                if threads[th] is not None:
                    try:
                        next(threads[th])
                    except StopIteration:
                        threads[th] = None

    def phase2(self):
        d = self.din
        self.off = self.base_off
        NQ = self.NQ
        xt = [self.sb("x2t%d" % i, [128, D], F32) for i in range(2)]
        wo = self.sb("wo", [128, 8, D], BF16)
        self.load_weight(wo, d["w_out_ab"], D, 0, D, None, xt)
        Kmax = max(P + Tq for (_, _, Tq, P) in self.seqs)
        nktmax = (Kmax + 127) // 128
        KTs = self.sb("KTs", [HD, NH, Kmax], BF16)
        Vp = self.sb("Vp", [128, nktmax, NH, 65], BF16)
        self.memset(Vp[:, :, :, 64:65], 1.0)
        nchmax = (Kmax + 511) // 512
        KTc_ = [KTs[:, :, c * 512:min(Kmax, (c + 1) * 512)].named("KTs_c%d" % c) for c in range(nchmax)]
        Vpc_ = [Vp[:, c * 4:min(nktmax, (c + 1) * 4), :, :].named("Vp_c%d" % c) for c in range(nchmax)]
        self.S.barrier()
        QTq = [self.sb("QTq%d" % i, [HD, NH, NQ], BF16) for i in range(2)]
        GMq = [self.sb("GMq%d" % i, [128, 4, NQ], BF16) for i in range(2)]
        mixs = [self.sb("mix%d" % i, [128, 8, NQ], BF16) for i in range(2)]
        NPT = 4
        PT = [self.sb("PT%d" % i, [128, NQ], BF16) for i in range(NPT)]
        rsum = [self.sb("rsum%d" % i, [1, NQ], F32) for i in range(2)]
        bcs = [self.sb("bcs%d" % i, [64, NQ], F32) for i in range(2)]
        stage = [self.sb("wstgA2", [128, 1024], F32), self.sb("wstgB2", [128, 1024], F32)]
        g_c = self.load_vec_pk("g_c", d["norm_c"], D)
        WC = self.dscr["WC"]
        wtmp = [self.sb("wtmp%d" % i, [128, 1024], BF16) for i in range(2)]
        wi = 0
        srcv = d["w_in_c"].r("(k p) n -> p k n", p=128)
        WCv = WC.r("(k p) n -> p k n", p=128)
        for k in range(8):
            for c in range(3):
                stg = stage[wi % 2]; wt = wtmp[wi % 2]
                self.ld("wstg%d" % (wi % 2), stg, srcv[:, k, c * 1024:(c + 1) * 1024])
                self.ts(wt, stg, g_c[:, k:k + 1], None, ALU.mult, eng="dve")
                self.st("wtmp%d" % (wi % 2), WCv[:, k, c * 1024:(c + 1) * 1024], wt)
                wi += 1
        xi = 0
        qbc = 0
        qlist = [(si_, qb_) for si_, (_k, _i, Tq_, _P) in enumerate(self.seqs) for qb_ in range(Tq_ // min(NQ, Tq_))]

        def issue_q(gk):
            si_, qb_ = qlist[gk]
            Tq_ = self.seqs[si_][2]
            nq_ = min(NQ, Tq_); q0_ = qb_ * nq_; sl_ = gk % 2
            self.ld("QTq%d" % sl_, QTq[sl_][:, :, 0:nq_], self.dscr["QT%d" % si_][:, :, q0_:q0_ + nq_])
            self.ld("GMq%d" % sl_, GMq[sl_][:, :, 0:nq_], self.dscr["GM%d" % si_][:, :, q0_:q0_ + nq_])
            self.ld("mixms%d" % sl_, mixs[sl_][:, 4:8, 0:nq_], self.dscr["MS%d" % si_][:, :, q0_:q0_ + nq_])

        for si, (kind, idx, Tq, P) in enumerate(self.seqs):
            io = self.seq_io(si)
            QTd, KTd, VSd, GMd, MSd, X1d = (self.dscr[n + str(si)] for n in ("QT", "KT", "VS", "GM", "MS", "X1"))
            Ktot = P + Tq
            nkt = (Ktot + 127) // 128
            nch = (Ktot + 511) // 512
            for c in range(nch):
                c0k = c * 512; c1k = min(Ktot, c0k + 512)
                self.ld("KTs_c%d" % c, KTc_[c][:, :, 0:c1k - c0k], KTd[:, :, c0k:c1k])
                for kt_ in range(c * 4, min(nkt, c * 4 + 4)):
                    kk = min(128, Ktot - kt_ * 128)
                    self.ld("Vp_c%d" % c, Vpc_[c][0:kk, kt_ - c * 4, :, 0:64], VSd[kt_ * 128:kt_ * 128 + kk, :].r("k (h d) -> k h d", h=NH))
            nq = min(NQ, Tq)
            TT = min(128, nq)
            for qb in range(Tq // nq):
                q0 = qb * nq
                sl = qbc % 2
                Qq, Gq, mix = QTq[sl], GMq[sl], mixs[sl]
                if qbc == 0:
                    issue_q(0)
                if qbc + 1 < len(qlist):
                    issue_q(qbc + 1)
                qbc += 1
                if kind == "p":
                    tiles = [(kt_, 128, 0, False) for kt_ in range(q0 // 128)]
                    tiles += [(q0 // 128 + m, 128, 128 * m, True) for m in range(nq // 128)]
                else:
                    tiles = [(kt_, min(128, Ktot - kt_ * 128), 0, False) for kt_ in range(nkt)]
                ntl = len(tiles)
                if kind == "s" and NH * nq <= NQ:
                    pOall = self.ps(4, [65, NH, nq], F32)
                    win = []
                    for i in range(ntl + 2):
                        if i < ntl:
                            kt_, kk, c0, diag = tiles[i]
                            pS = self.ps(i % NPT, [128, NH, nq], F32)
                            for h in range(NH):
                                self.mm(pS[0:kk, h, :], KTc_[kt_ // 4][:, h, (kt_ % 4) * 128:(kt_ % 4) * 128 + kk], Qq[:, h, 0:nq], True, True, last=(h == NH - 1))
                            pt = PT[i % NPT]
                            self.act(pt[0:kk, 0:NH * nq], pS[0:kk].r("k h q -> k (h q)"), AF.Exp, scale=ATTN_SCALE)
                            win.append((i, pt))
                        if i >= 2:
                            j, pt = win.pop(0)
                            kt_, kk, c0, diag = tiles[j]
                            for h in range(NH):
                                self.mm(pOall[:, h, :], Vpc_[kt_ // 4][0:kk, kt_ % 4, h, :], pt[0:kk, h * nq:(h + 1) * nq], j == 0 and h == 0, j == ntl - 1 and h == NH - 1, last=(h == NH - 1))
                    self.recip(rsum[0][:, 0:NH * nq], pOall[64:65].r("o h q -> o (h q)"))
                    pBc = self.ps(6, [64, NH * nq], F32)
                    self.mm(pBc, self.onesF[0:1, 0:64], rsum[0][:, 0:NH * nq], True, True)
                    self.cp(bcs[0][:, 0:NH * nq], pBc, eng="act")
                    for h in range(NH):
                        hs = h % 2
                        dst = mix[hs * 64:hs * 64 + 64, h // 2, 0:nq]
                        self.tt(dst, pOall[0:64, h, :], bcs[0][:, h * nq:(h + 1) * nq], ALU.mult)
                        self.tt(dst, dst, Gq[hs * 64:hs * 64 + 64, h // 2, 0:nq], ALU.mult, eng="pool")
                    tiles = []
                    ntl = 0
                items = [(h, ti) for h in range(NH) for ti in range(ntl)]
                pending = []
                LOOK = 3
                window = []
                for i, item in enumerate(items + [None] * LOOK):
                    cur = None
                    if item is not None:
                        h, ti = item
                        kt_, kk, c0, diag = tiles[ti]
                        pS = self.ps(i % NPT, [128, nq], F32)
                        pt = PT[i % NPT]
                        self.mm(pS[0:kk, c0:nq], KTc_[kt_ // 4][:, h, (kt_ % 4) * 128:(kt_ % 4) * 128 + kk], Qq[:, h, c0:nq], True, True)
                        self.act(pt[0:kk, c0:nq], pS[0:kk, c0:nq], AF.Exp, scale=ATTN_SCALE)
                        if diag:
                            self.memset(pt[64:128, c0:c0 + 64], 0.0)
                        cur = (h, ti, pt)
                    window.append(cur)
                    prev = window.pop(0) if len(window) > LOOK else None
                    if prev is not None:
                        h, ti, pt = prev
                        kt_, kk, c0, diag = tiles[ti]
                        pO = self.ps(4 + (h % 2), [65, nq], F32)
                        self.mm(pO[:, c0:nq], Vpc_[kt_ // 4][0:kk, kt_ % 4, h, :], pt[0:kk, c0:nq], ti == 0, ti == ntl - 1)
                        if ti == ntl - 1:
                            hs = h % 2
                            self.recip(rsum[hs][:, 0:nq], pO[64:65, :])

                            def fin(h=h, hs=hs, pO=pO):
                                pBc = self.ps(6 + hs, [64, nq], F32)
                                self.mm(pBc, self.onesF[0:1, 0:64], rsum[hs][:, 0:nq], True, True)
                                self.cp(bcs[hs][:, 0:nq], pBc, eng="act")
                                dst = mix[hs * 64:hs * 64 + 64, h // 2, 0:nq]
                                self.tt(dst, pO[0:64, :], bcs[hs][:, 0:nq], ALU.mult)
                                self.tt(dst, dst, Gq[hs * 64:hs * 64 + 64, h // 2, 0:nq], ALU.mult, eng="pool")
                            pending.append((i + 2, fin))
                    while pending and (pending[0][0] <= i or i == len(items) + LOOK - 1):
                        pending.pop(0)[1]()
                for ti in range(nq // TT):
                    x = xt[xi % 2]; xi += 1
                    self.ld("x2t%d" % (xi % 2), x[0:TT, :], io["x"][q0 + ti * TT:q0 + (ti + 1) * TT, :])
                    for half in range(2):
                        po = self.ps(6 + half, [TT, 512], F32)
                        for k in range(8):
                            self.mm(po, mix[:, k, ti * TT:(ti + 1) * TT], wo[:, k, half * 512:(half + 1) * 512], k == 0, k == 7)
                        self.tt(x[0:TT, half * 512:(half + 1) * 512], x[0:TT, half * 512:(half + 1) * 512], po, ALU.add)
                    self.st("x2t%d" % (xi % 2), X1d[q0 + ti * TT:q0 + (ti + 1) * TT, :], x[0:TT, :])

    def phase3(self):
        d = self.din
        self.off = self.base_off
        NB = self.NB3
        xt = [self.sb("x3t%d" % i, [128, D], F32) for i in range(2)]
        xr = [self.sb("x3r%d" % i, [128, D], F32) for i in range(2)]
        stage = xt
        WC = self.dscr["WC"]
        WCv = WC.r("(k p) n -> p k n", p=128)
        woc = self.sb("woc", [128, 8, D], BF16)
        self.load_weight(woc, d["w_out_c"], D, 0, D, None, stage)
        cb = self.load_vec_pk("cb", d["conv_b"], D)
        lng = self.load_vec_pk("lng", d["ln_g"], D)
        lnb = self.load_vec_pk("lnb", d["ln_b"], D)
        cw = self.sb("cw", [128, 8, CW], F32)
        for k in range(8):
            self.ld("misc", cw[:, k, :], d["conv_w"][:, k * 128:(k + 1) * 128].r("t p -> p t"), slow=True)
        diag = self.sb("diag", [128, 8, CW, 128], BF16)
        for k in range(8):
            for t in range(CW):
                if t % 2:
                    self.ts(diag[:, k, t, :], self.identF, cw[:, k, t:t + 1], None, ALU.mult)
                else:
                    self.act(diag[:, k, t, :], self.identF, AF.Copy, scale=cw[:, k, t:t + 1])
        wch = [self.sb("wch%d" % i, [128, 8, 512], BF16) for i in range(3)]
        H = CW - 1
        hb = self.sb("hb3", [128, D], BF16)
        junk = hb
        hT = self.sb("hT3", [128, 8, NB], BF16)
        ss = self.sb("ss3", [128, 2], F32); rs = self.sb("rs3", [128, 2], F32)
        vT = self.sb("vT", [128, 8, H + NB], BF16)
        vtails = [self.sb("vtail%d" % i, [128, 8, 32], F32) for i in range(2)]
        gTs = [self.sb("gT%d" % i, [128, 8, NB], BF16) for i in range(2)]
        sgm = [self.sb("sgm%d" % i, [128, NB], F32) for i in range(2)]
        vf = [self.sb("vf%d" % i, [128, NB], F32) for i in range(2)]
        ybf = self.sb("ybf", [128, 8, NB], BF16)
        sqb = self.sb("sqb", [128, 8, NB], BF16)
        mean = self.sb("mean", [128, NB], F32); var = self.sb("var", [128, NB], F32); msq = self.sb("msq", [128, NB], F32)
        tmp = [self.sb("tmp3%d" % i, [128, NB], F32) for i in range(2)]
        halo = self.sb("halo", [128, 8, H], BF16)
        mT = self.sb("mT", [128, 8, NB], BF16)
        tailo = self.sb("tailo", [32, D], F32)
        scv = tailo
        st_ = dict(wci=0, xi=0, xri=0)

        blocks = []
        for si, (kind, idx, Tq, P) in enumerate(self.seqs):
            nb = min(NB, Tq)
            for b in range(Tq // nb):
                blocks.append(dict(si=si, b=b, nb=nb, r=len(blocks), last=(b == Tq // nb - 1)))

        def wchunk(c0):
            wci = st_["wci"]; st_["wci"] += 1
            w = wch[wci % 3]
            self.ld("wch%d" % (wci % 3), w, WCv[:, :, c0:c0 + 512])
            return w

        def front(blk):
            si, b, nb, r = blk["si"], blk["b"], blk["nb"], blk["r"]
            kind, idx, Tq, P = self.seqs[si]
            io = self.seq_io(si)
            X1d = self.dscr["X1" + str(si)]
            TT = min(128, nb); ntile = nb // TT; t0 = b * nb
            gT = gTs[r % 2]
            if b == 0:
                if kind == "p":
                    self.memset(vT[:, :, 0:H], 0.0)
                else:
                    self.ld("scv", scv[0:H, :], io["sconv"])
                    for k in range(8):
                        pc = self.ps(7, [128, H], F32)
                        self.tr(pc, scv[0:H, k * 128:(k + 1) * 128], self.identF[0:H, 0:H])
                        self.cp(vT[:, k, 0:H], pc, eng="act")
            else:
                self.cp(vT[:, :, 0:H], halo)
            for ti in range(ntile):
                xi = st_["xi"]; st_["xi"] += 1
                x = xt[xi % 2]
                self.ld("x3t%d" % (xi % 2), x[0:TT, :], X1d[t0 + ti * TT:t0 + (ti + 1) * TT, :])
                self.act(junk[0:TT, :], x[0:TT, :], AF.Square, accum=ss[0:TT, 0:1])
                self.rstd_from_ss(rs[0:TT, 0:1], ss[0:TT, 0:1], D)
                self.ts(hb[0:TT, :], x[0:TT, :], rs[0:TT, 0:1], None, ALU.mult)
                pT = self.ps(0, [128, 8, TT], BF16)
                for k in range(8):
                    self.tr(pT[:, k, :], hb[0:TT, k * 128:(k + 1) * 128], self.identB[0:TT, 0:TT], last=(k == 7))
                self.cp(hT[:, :, ti * TT:(ti + 1) * TT], pT, eng="act")
            for half in range(2):
                wa = wchunk(half * 512)
                wb_ = wchunk(1024 + half * 512)
                for mm_ in range(4):
                    m = half * 4 + mm_
                    pa = self.ps(1 + 2 * (m % 2), [128, nb], F32); pb_ = self.ps(2 + 2 * (m % 2), [128, nb], F32)
                    for k in range(8):
                        self.mm(pa, wa[:, k, mm_ * 128:(mm_ + 1) * 128], hT[:, k, 0:nb], k == 0, k == 7)
                    for k in range(8):
                        self.mm(pb_, wb_[:, k, mm_ * 128:(mm_ + 1) * 128], hT[:, k, 0:nb], k == 0, k == 7)
                    sg = sgm[m % 2]; v_ = vf[m % 2]
                    self.act(sg[:, 0:nb], pb_, AF.Sigmoid)
                    self.tt(v_[:, 0:nb], pa, sg[:, 0:nb], ALU.mult)
                    self.cp(vT[:, m, H:H + nb], v_[:, 0:nb], eng="pool")
                    if blk["last"]:
                        self.cp(vtails[r % 2][:, m, :], v_[:, nb - 32:nb], eng="pool")
            for half in range(2):
                wg = wchunk(2048 + half * 512)
                for mm_ in range(4):
                    m = half * 4 + mm_
                    pg = self.ps(5 + (m % 2), [128, nb], F32)
                    for k in range(8):
                        self.mm(pg, wg[:, k, mm_ * 128:(mm_ + 1) * 128], hT[:, k, 0:nb], k == 0, k == 7)
                    self.act(gT[:, m, 0:nb], pg, AF.Silu)

        def conv_stats(blk):
            nb = blk["nb"]
            for m in range(8):
                pc = self.ps(5 + (m % 2), [128, nb], F32)
                for t in range(CW):
                    self.mm(pc, diag[:, m, t, :], vT[:, m, t:t + nb], t == 0, t == CW - 1)
                self.act(ybf[:, m, 0:nb], pc, AF.Identity, bias=cb[:, m:m + 1])
                self.act(sqb[:, m, 0:nb], pc, AF.Square, bias=cb[:, m:m + 1])
            self.cp(halo, vT[:, :, nb:nb + H], eng="pool")
            pst = self.ps(7, [128, nb], F32); pst2 = self.ps(0, [128, nb], F32)
            for m in range(8):
                self.mm(pst, self.onesB, ybf[:, m, 0:nb], m == 0, m == 7)
            for m in range(8):
                self.mm(pst2, self.onesB, sqb[:, m, 0:nb], m == 0, m == 7)
            self.ts(mean[:, 0:nb], pst, 1.0 / D, None, ALU.mult)
            self.tt(msq[:, 0:nb], mean[:, 0:nb], mean[:, 0:nb], ALU.mult)
            self.stt(var[:, 0:nb], pst2, 1.0 / D, msq[:, 0:nb], ALU.mult, ALU.subtract)
            self.ts(var[:, 0:nb], var[:, 0:nb], EPS, None, ALU.add)
            self.act(var[:, 0:nb], var[:, 0:nb], AF.Sqrt)
            self.recip(var[:, 0:nb], var[:, 0:nb])

        def back(blk):
            si, b, nb, r = blk["si"], blk["b"], blk["nb"], blk["r"]
            io = self.seq_io(si)
            X1d = self.dscr["X1" + str(si)]
            TT = min(128, nb); ntile = nb // TT; t0 = b * nb
            gT = gTs[r % 2]
            for m in range(8):
                tm = tmp[m % 2]
                self.tt(tm[:, 0:nb], ybf[:, m, 0:nb], mean[:, 0:nb], ALU.subtract)
                self.tt(tm[:, 0:nb], tm[:, 0:nb], var[:, 0:nb], ALU.mult, eng="pool")
                self.act(tm[:, 0:nb], tm[:, 0:nb], AF.Silu, bias=lnb[:, m:m + 1], scale=lng[:, m:m + 1])
                self.tt(mT[:, m, 0:nb], tm[:, 0:nb], gT[:, m, 0:nb], ALU.mult)

        def outproj(blk):
            si, b, nb, r = blk["si"], blk["b"], blk["nb"], blk["r"]
            io = self.seq_io(si)
            X1d = self.dscr["X1" + str(si)]
            TT = min(128, nb); ntile = nb // TT; t0 = b * nb
            for ti in range(ntile):
                xri = st_["xri"]; st_["xri"] += 1
                x = xr[xri % 2]
                self.ld("x3r%d" % (xri % 2), x[0:TT, :], X1d[t0 + ti * TT:t0 + (ti + 1) * TT, :])
                for half in range(2):
                    po = self.ps(1 + half, [TT, 512], F32)
                    for k in range(8):
                        self.mm(po, mT[:, k, ti * TT:(ti + 1) * TT], woc[:, k, half * 512:(half + 1) * 512], k == 0, k == 7)
                    self.tt(x[0:TT, half * 512:(half + 1) * 512], x[0:TT, half * 512:(half + 1) * 512], po, ALU.add)
                self.st("x3r%d" % (xri % 2), io["y"][t0 + ti * TT:t0 + (ti + 1) * TT, :], x[0:TT, :])
            if blk["last"]:
                for k in range(8):
                    pc = self.ps(7, [32, 128], F32)
                    self.tr(pc, vtails[r % 2][:, k, :], self.identF)
                    self.cp(tailo[:, k * 128:(k + 1) * 128], pc, eng="act")
                self.st("tailo", io["conv"], tailo[2:32, :])

        nblk = len(blocks)
        front(blocks[0])
        for r in range(nblk):
            conv_stats(blocks[r])
            back(blocks[r])
            if r + 1 < nblk:
                front(blocks[r + 1])
            outproj(blocks[r])


def host_consts(T, PAST, TS):
    half = RD // 2
    inv = (10000.0 ** (-np.arange(half, dtype=np.float32) / half)).astype(np.float32)
    def tab(pos):
        ang = pos.astype(np.float32)[:, None] * inv[None, :]
        c_ = np.cos(ang).astype(np.float32); s_ = np.sin(ang).astype(np.float32)
        return np.concatenate([c_, c_], axis=1), np.concatenate([-s_, s_], axis=1)
    cp_, sp_ = tab(np.arange(T))
    cs_, ss_ = tab(PAST + np.arange(TS))
    ident = np.eye(128, dtype=np.float32)
    s_idx = np.arange(128) // 16
    mask = (s_idx[None, :] >= s_idx[:, None]).astype(np.float32)
    return dict(c_ident=ident, c_mask=mask, c_cos_p=cp_, c_sin_p=sp_, c_cos_s=cs_, c_sin_s=ss_)


WEIGHT_MAP = dict(norm_ab="norm_ab", w_in_ab="w_in_ab", g_q_lat="g_q_lat", w_uq="w_uq", g_kv_lat="g_kv_lat",
                  w_uk="w_uk", w_uv="w_uv", g_q_nope="g_q_nope", g_q_rope="g_q_rope", g_k_nope="g_k_nope",
                  g_k_rope="g_k_rope", lam_re="s5_lam_re", lam_im="s5_lam_im", log_dt="s5_log_dt",
                  b_re="s5_b_re", b_im="s5_b_im", c_re="s5_c_re", c_im="s5_c_im", s5_d="s5_d", w_glu="s5_w_glu",
                  b_glu="s5_b_glu", w_out_ab="w_out_ab", norm_c="norm_c", w_in_c="w_in_c", conv_w="conv_w",
                  conv_b="conv_b", ln_g="ln_g", ln_b="ln_b", w_out_c="w_out_c")


def make_in_maps(inputs, n_cores, NP, NS, T, PAST, TS):
    f = lambda a: np.ascontiguousarray(np.asarray(a, dtype=np.float32))
    consts = host_consts(T, PAST, TS)
    w = {}
    for k, src in WEIGHT_MAP.items():
        a = f(inputs[src])[0]
        if k in ("w_uk", "w_uv"):
            a = a.reshape(a.shape[0], -1)
        w[k] = np.ascontiguousarray(a)
    maps = []
    for c in range(n_cores):
        m = dict(w)
        m.update(consts)
        m["xp"] = f(inputs["x_prompt"][c * NP:(c + 1) * NP])
        m["xs"] = f(inputs["x_sample"][c * NS:(c + 1) * NS])
        m["clat"] = f(inputs["cache_mla_latent"][0, c * NS:(c + 1) * NS])
        m["ckr"] = f(inputs["cache_mla_krope"][0, c * NS:(c + 1) * NS])
        m["sre"] = f(inputs["state_s5_re"][0, c * NS:(c + 1) * NS])
        m["sim"] = f(inputs["state_s5_im"][0, c * NS:(c + 1) * NS])
        m["sconv"] = f(inputs["state_conv"][0, c * NS:(c + 1) * NS])
        maps.append(m)
    return maps


def assemble(results):
    cat = lambda k: np.concatenate([r[k] for r in results], axis=0)
    yp, ys = cat("yp"), cat("ys")
    st = lambda k: cat(k)[None]
    return (yp, ys, st("lat_p"), st("kr_p"), st("re_p"), st("im_p"), st("conv_p"),
            st("lat_s"), st("kr_s"), st("re_s"), st("im_s"), st("conv_s"))


def kernel(**inputs):
    n = 8
    B, T = inputs["x_prompt"].shape[0], inputs["x_prompt"].shape[1]
    BS, TS = inputs["x_sample"].shape[0], inputs["x_sample"].shape[1]
    PAST = inputs["cache_mla_latent"].shape[2]
    NP, NS = B // n, BS // n
    kb = K(NP, T, NS, PAST, TS)
    nc = kb.build()
    maps = make_in_maps(inputs, n, NP, NS, T, PAST, TS)
    res = run_bass_kernel_spmd(nc, maps, core_ids=list(range(n)))
    outs = assemble(res.results)
    return tuple(np.asarray(o, dtype=np.float32) for o in outs)
```

```python
import math
import numpy as np
import concourse.bass as bass
import concourse.mybir as mybir
from concourse.bass_utils import run_bass_kernel_spmd

F32 = mybir.dt.float32
BF16 = mybir.dt.bfloat16
U8 = mybir.dt.uint8
ALU = mybir.AluOpType
AF = mybir.ActivationFunctionType
AX = mybir.AxisListType

D = 1024
QL, KVL, RD, MW, SW = 384, 256, 32, 512, 512
NH, ND, VD = 8, 64, 64
HD = ND + RD
CW = 31
EPS = 1e-6
ATTN_SCALE = float(HD ** -0.5)
MAGIC = 12582912.0
TWO_PI = 2.0 * math.pi

COMPUTE = ("pe", "act", "dve", "pool")


class V:
    def __init__(self, ap, res):
        self.ap = ap
        self.res = res if isinstance(res, tuple) else (res,)

    def __getitem__(self, k):
        return V(self.ap[k], self.res)

    def r(self, pat, **kw):
        return V(self.ap.rearrange(pat, **kw), self.res)

    def bc(self, shape):
        return V(self.ap.broadcast_to(list(shape)), self.res)

    def us(self, axis):
        return V(self.ap.unsqueeze(axis), self.res)

    def named(self, *res):
        return V(self.ap, tuple(res))

    @property
    def shape(self):
        return self.ap.shape


class Sched:
    def __init__(self, nc):
        self.nc = nc
        self.q = {e: [] for e in COMPUTE + ("sp",)}
        self.cnt = {}
        self.sems = {}
        self.known = {e: {} for e in COMPUTE + ("sp",)}
        self.lastw = {}
        self.readers = {}
        self.n_inst = 0

    def _sem(self, key):
        if key not in self.sems:
            self.sems[key] = self.nc.alloc_semaphore("s_" + key)
            self.cnt[key] = 0
        return self.sems[key]

    def _deps(self, reads, writes):
        deps = {}

        def add(tok):
            if tok is None:
                return
            k, v = tok
            if deps.get(k, 0) < v:
                deps[k] = v
        for r in reads:
            add(self.lastw.get(r))
        for w in writes:
            add(self.lastw.get(w))
            for k, v in self.readers.get(w, {}).items():
                add((k, v))
        return deps

    def _emit_waits(self, engine, deps):
        kn = self.known[engine]
        for k, v in deps.items():
            if k == engine and engine == "pe":
                continue
            if kn.get(k, 0) >= v:
                continue
            kn[k] = v
            sem = self.sems[k]
            self.q[engine].append(lambda eng, sem=sem, v=v: eng.wait_ge(sem, v))

    def _commit(self, tok, reads, writes):
        k, v = tok
        for r in reads:
            d = self.readers.setdefault(r, {})
            if d.get(k, 0) < v:
                d[k] = v
        for w in writes:
            self.lastw[w] = tok
            self.readers[w] = {}

    def op(self, engine, fn, reads=(), writes=(), inc=True):
        deps = self._deps(reads, writes)
        self._emit_waits(engine, deps)
        sem = self._sem(engine)
        if inc:
            self.cnt[engine] += 1
            n = self.cnt[engine]
            self.q[engine].append(lambda eng, fn=fn, sem=sem: fn(eng).then_inc(sem, 1))
        else:
            n = self.cnt[engine] + 1
            self.q[engine].append(lambda eng, fn=fn: fn(eng))
        self._commit((engine, n), reads, writes)
        self.n_inst += 1

    def dma(self, stream, fn, reads=(), writes=(), queue="sp"):
        deps = self._deps(reads, writes)
        key = "d" + queue[0] + "_" + stream
        sem = self._sem(key)
        if self.cnt[key] > 0:
            deps[key] = max(deps.get(key, 0), self.cnt[key])
        self._emit_waits(queue, deps)
        self.cnt[key] += 16
        v = self.cnt[key]
        self.q[queue].append(lambda eng, fn=fn, sem=sem: fn(eng).then_inc(sem, 16))
        self._commit((key, v), reads, writes)
        self.n_inst += 1

    def barrier(self):
        deps = {k: self.cnt[k] for k in self.sems if self.cnt[k] > 0}
        for e in COMPUTE + ("sp",):
            self._emit_waits(e, dict(deps))

    def finish(self):
        deps = {k: self.cnt[k] for k in self.sems if self.cnt[k] > 0}
        self._emit_waits("sp", deps)

    def run(self):
        nc = self.nc
        with nc.Block() as block:
            @block.sync
            def _(eng):
                for f in self.q["sp"]:
                    f(eng)

            @block.tensor
            def _(eng):
                for f in self.q["pe"]:
                    f(eng)

            @block.scalar
            def _(eng):
                for f in self.q["act"]:
                    f(eng)

            @block.vector
            def _(eng):
                for f in self.q["dve"]:
                    f(eng)

            @block.gpsimd
            def _(eng):
                for f in self.q["pool"]:
                    f(eng)


def _res(*vs):
    out = []
    for v in vs:
        if v is None or isinstance(v, (int, float)):
            continue
        out.extend(v.res)
    return tuple(out)


def _ap(v):
    return v.ap if isinstance(v, V) else v


class K:
    def __init__(self, NP, T, NS, PAST, TS=64, NB1=256, NQ=512, NB3=512):
        self.NP, self.T, self.NS, self.PAST, self.TS = NP, T, NS, PAST, TS
        self.NB1, self.NQ, self.NB3 = NB1, NQ, NB3
        self.nc = nc = bass.Bass("TRN2", target_bir_lowering=False)
        self.S = Sched(nc)
        self.seqs = [("p", i, T, 0) for i in range(NP)] + [("s", i, TS, PAST) for i in range(NS)]
        self.arena = nc.alloc_sbuf_tensor("arena", [128, 204 * 1024], U8)
        self.off = 0
        self.uid = 0
        self.pb = [nc.alloc_psum_tensor("pb%d" % i, [128, 512], F32) for i in range(8)]
        self.din = {}
        self.dout = {}
        self.dscr = {}

    def sb(self, name, shape, dt, parts=128):
        nbytes = int(np.prod(shape[1:])) * (4 if dt == F32 else 2)
        nbytes = (nbytes + 63) // 64 * 64
        assert self.off + nbytes <= 204 * 1024, (name, self.off, nbytes)
        ap = self.arena[0:shape[0], self.off:self.off + nbytes].bitcast(dt)
        used = int(np.prod(shape[1:]))
        ap = ap[:, 0:used]
        if len(shape) == 3:
            ap = ap.rearrange("p (a b) -> p a b", a=shape[1])
        elif len(shape) == 4:
            ap = ap.rearrange("p (a b c) -> p a b c", a=shape[1], b=shape[2])
        elif len(shape) == 5:
            ap = ap.rearrange("p (a b c d) -> p a b c d", a=shape[1], b=shape[2], c=shape[3])
        self.off += nbytes
        self.uid += 1
        return V(ap, "%s#%d" % (name, self.uid))

    def ps(self, bank, shape, dt=F32, res=None):
        t = self.pb[bank]
        ap = t[0:shape[0], :]
        if dt == BF16:
            ap = ap.bitcast(BF16)
        used = int(np.prod(shape[1:]))
        ap = ap[:, 0:used]
        if len(shape) == 3:
            ap = ap.rearrange("p (a b) -> p a b", a=shape[1])
        elif len(shape) == 4:
            ap = ap.rearrange("p (a b c) -> p a b c", a=shape[1], b=shape[2])
        return V(ap, res or ("pb%d" % bank))

    def inp(self, name, shape, dt=F32):
        ap = self.nc.dram_tensor(name, list(shape), dt, kind="ExternalInput").ap()
        self.din[name] = V(ap, "in_" + name)
        return self.din[name]

    def outp(self, name, shape):
        ap = self.nc.dram_tensor(name, list(shape), F32, kind="ExternalOutput").ap()
        self.dout[name] = V(ap, "out_" + name)
        return self.dout[name]

    def scr(self, name, shape, dt):
        ap = self.nc.dram_tensor(name, list(shape), dt, kind="Internal").ap()
        self.dscr[name] = V(ap, "scr_" + name)
        return self.dscr[name]

    def act(self, out, in_, func, bias=None, scale=None, accum=None):
        kw = {}
        if bias is not None:
            kw["bias"] = _ap(bias)
        if scale is not None:
            kw["scale"] = _ap(scale)
        if accum is not None:
            kw["accum_out"] = accum.ap
        self.S.op("act", lambda e: e.activation(out=out.ap, in_=in_.ap, func=func, **kw),
                  reads=_res(in_, bias, scale), writes=_res(out, accum))

    def ts(self, out, in0, s1, s2, op0, op1=None, eng="dve"):
        kw = {}
        if op1 is not None:
            kw["op1"] = op1
        self.S.op(eng, lambda e: e.tensor_scalar(out=out.ap, in0=in0.ap, scalar1=_ap(s1), scalar2=_ap(s2), op0=op0, **kw),
                  reads=_res(in0, s1, s2), writes=_res(out))

    def tt(self, out, in0, in1, op, eng="dve"):
        self.S.op(eng, lambda e: e.tensor_tensor(out=out.ap, in0=in0.ap, in1=in1.ap, op=op),
                  reads=_res(in0, in1), writes=_res(out))

    def stt(self, out, in0, scalar, in1, op0, op1, eng="dve"):
        self.S.op(eng, lambda e: e.scalar_tensor_tensor(out=out.ap, in0=in0.ap, scalar=_ap(scalar), in1=in1.ap, op0=op0, op1=op1),
                  reads=_res(in0, scalar, in1), writes=_res(out))

    def cp(self, out, in_, eng="dve"):
        if eng == "act":
            self.S.op("act", lambda e: e.copy(out=out.ap, in_=in_.ap), reads=_res(in_), writes=_res(out))
        else:
            self.S.op(eng, lambda e: e.tensor_copy(out=out.ap, in_=in_.ap), reads=_res(in_), writes=_res(out))

    def recip(self, out, in_):
        self.S.op("dve", lambda e: e.reciprocal(out=out.ap, in_=in_.ap), reads=_res(in_), writes=_res(out))

    def red(self, out, in_, eng="dve"):
        self.S.op(eng, lambda e: e.tensor_reduce(out=out.ap, in_=in_.ap, axis=AX.X, op=ALU.add),
                  reads=_res(in_), writes=_res(out))

    def memset(self, out, val, eng="pool"):
        self.S.op(eng, lambda e: e.memset(out.ap, val), writes=_res(out))

    def mm(self, out, lhsT, rhs, start, stop, last=None):
        if last is None:
            last = stop
        self.S.op("pe", lambda e: e.matmul(out.ap, lhsT=lhsT.ap, rhs=rhs.ap, start=start, stop=stop),
                  reads=_res(lhsT, rhs), writes=_res(out), inc=last)

    def tr(self, out, in_, ident, last=True):
        self.S.op("pe", lambda e: e.transpose(out=out.ap, in_=in_.ap, identity=ident.ap),
                  reads=_res(in_, ident), writes=_res(out), inc=last)

    def ld(self, stream, out, in_, slow=False):
        kw = {"allow_slow_non_contiguous": True} if slow else {}
        extra = ("miscq",) if stream == "misc" else ()
        self.S.dma(stream, lambda e: e.dma_start(out=out.ap, in_=in_.ap, **kw), reads=_res(in_), writes=_res(out) + extra, queue="sp")

    def st(self, stream, out, in_, slow=False):
        kw = {"allow_slow_non_contiguous": True} if slow else {}
        self.S.dma(stream, lambda e: e.dma_start(out=out.ap, in_=in_.ap, **kw), reads=_res(in_), writes=_res(out), queue=getattr(self, "st_queue", "pool"))

    def rstd_from_ss(self, rstd, ss, n):
        self.ts(rstd, ss, 1.0 / n, EPS, ALU.mult, ALU.add)
        self.act(rstd, rstd, AF.Sqrt)
        self.recip(rstd, rstd)

    def declare_io(self):
        NP, T, NS, PAST, TS = self.NP, self.T, self.NS, self.PAST, self.TS
        i = self.inp
        i("xp", [NP, T, D]); i("xs", [NS, TS, D])
        i("clat", [NS, PAST, KVL]); i("ckr", [NS, PAST, RD])
        i("sre", [NS, 32, 64]); i("sim", [NS, 32, 64]); i("sconv", [NS, CW - 1, D])
        i("norm_ab", [D]); i("w_in_ab", [D, 2208]); i("g_q_lat", [QL]); i("w_uq", [QL, NH * HD])
        i("g_kv_lat", [KVL]); i("w_uk", [KVL, NH * ND]); i("w_uv", [KVL, NH * VD])
        i("g_q_nope", [ND]); i("g_q_rope", [RD]); i("g_k_nope", [ND]); i("g_k_rope", [RD])
        i("lam_re", [32, 64]); i("lam_im", [32, 64]); i("log_dt", [32])
        i("b_re", [32, 64, 16]); i("b_im", [32, 64, 16]); i("c_re", [32, 16, 64]); i("c_im", [32, 16, 64])
        i("s5_d", [SW]); i("w_glu", [SW, SW]); i("b_glu", [SW]); i("w_out_ab", [D, D])
        i("norm_c", [D]); i("w_in_c", [D, 3 * D]); i("conv_w", [CW, D]); i("conv_b", [D])
        i("ln_g", [D]); i("ln_b", [D]); i("w_out_c", [D, D])
        i("c_ident", [128, 128]); i("c_mask", [128, 128])
        i("c_cos_p", [T, 32]); i("c_sin_p", [T, 32]); i("c_cos_s", [TS, 32]); i("c_sin_s", [TS, 32])
        o = self.outp
        o("yp", [NP, T, D]); o("ys", [NS, TS, D])
        o("lat_p", [NP, T, KVL]); o("kr_p", [NP, T, RD]); o("re_p", [NP, 32, 64]); o("im_p", [NP, 32, 64])
        o("conv_p", [NP, CW - 1, D])
        o("lat_s", [NS, TS, KVL]); o("kr_s", [NS, TS, RD]); o("re_s", [NS, 32, 64]); o("im_s", [NS, 32, 64])
        o("conv_s", [NS, CW - 1, D])
        for si, (kind, idx, Tq, P) in enumerate(self.seqs):
            self.scr("QT%d" % si, [HD, NH, Tq], BF16)
            self.scr("KT%d" % si, [HD, NH, P + Tq], BF16)
            self.scr("VS%d" % si, [P + Tq, NH * VD], BF16)
            self.scr("GM%d" % si, [128, 4, Tq], BF16)
            self.scr("MS%d" % si, [128, 4, Tq], BF16)
            self.scr("X1%d" % si, [Tq, D], F32)
        self.scr("WC", [D, 3 * D], BF16)

    def seq_io(self, si):
        kind, idx, Tq, P = self.seqs[si]
        if kind == "p":
            return dict(x=self.din["xp"][idx], y=self.dout["yp"][idx], lat=self.dout["lat_p"][idx], kr=self.dout["kr_p"][idx],
                        re=self.dout["re_p"][idx], im=self.dout["im_p"][idx], conv=self.dout["conv_p"][idx],
                        cos=self.din["c_cos_p"], sin=self.din["c_sin_p"])
        return dict(x=self.din["xs"][idx], y=self.dout["ys"][idx], lat=self.dout["lat_s"][idx], kr=self.dout["kr_s"][idx],
                    re=self.dout["re_s"][idx], im=self.dout["im_s"][idx], conv=self.dout["conv_s"][idx],
                    cos=self.din["c_cos_s"], sin=self.din["c_sin_s"],
                    clat=self.din["clat"][idx], ckr=self.din["ckr"][idx], sre=self.din["sre"][idx], sim=self.din["sim"][idx],
                    sconv=self.din["sconv"][idx])

    def load_weight(self, dst, src, Kd, c0, c1, gain=None, stage=None):
        nk = Kd // 128
        n = c1 - c0
        srcv = src.r("(k p) n -> p k n", p=128)
        for k in range(nk):
            stg = stage[self._stg % 2]
            self._stg += 1
            self.ld("wstg%d" % (self._stg % 2), stg[:, 0:n], srcv[:, k, c0:c1])
            eng = "dve" if (self._stg % 2) else "pool"
            if gain is not None:
                self.ts(dst[:, k, :], stg[:, 0:n], gain[:, k:k + 1], None, ALU.mult, eng=eng)
            else:
                self.cp(dst[:, k, :], stg[:, 0:n], eng=eng)

    def load_vec_pk(self, name, src, Kd):
        v = self.sb(name, [128, Kd // 128], F32)
        self.ld("misc", v, src.r("(k p) -> p k", p=128), slow=True)
        return v

    def load_vec_bc(self, name, src, n, parts=128):
        v = self.sb(name, [parts, n], F32)
        self.ld("misc", v, V(src.ap.partition_broadcast(parts), src.res))
        return v

    def build(self):
        self.declare_io()
        self._stg = 0
        self.phase_const()
        self.phase1()
        self.st_queue = "pool"
        self.S.barrier()
        self.phase1b()
        self.S.barrier()
        if getattr(self, "STOP", 9) >= 2:
            self.phase2()
            self.S.barrier()
        if getattr(self, "STOP", 9) >= 3:
            self.phase3()
        self.S.finish()
        self.S.run()
        return self.nc

    def phase_const(self):
        self.identF = self.sb("identF", [128, 128], F32)
        self.ld("misc", self.identF, self.din["c_ident"])
        self.identB = self.sb("identB", [128, 128], BF16)
        self.cp(self.identB, self.identF)
        self.onesB = self.sb("onesB", [128, 128], BF16)
        self.memset(self.onesB, 1.0)
        self.onesF = self.sb("onesF", [128, 128], F32)
        self.memset(self.onesF, 1.0)
        self.base_off = self.off

    def s5_setup(self):
        d = self.din
        mark = None
        self.BkT = self.sb("BkT", [128, 32, 2, 64], BF16)
        self.Cq = self.sb("Cq", [128, 32, 128], BF16)
        self.T0 = self.sb("T0", [128, 32, 128], BF16)
        self.lamA = self.sb("lamA", [64, 2, 32], F32)
        self.lamAn = self.sb("lamAn", [64, 32], F32)
        self.LPr = self.sb("LPr", [64, 32, 8], F32)
        self.LPpm = self.sb("LPpm", [64, 2, 32, 8], F32)
        self.s5_persist_end = self.off
        if hasattr(self, "_p1_weights"):
            self._p1_weights()
        keep = self.off
        P = 64
        lre = self.sb("lre", [P, 32], F32); lim = self.sb("lim", [P, 32], F32)
        self.ld("misc", lre, d["lam_re"].r("g p -> p g"), slow=True)
        self.ld("misc", lim, d["lam_im"].r("g p -> p g"), slow=True)
        dt = self.load_vec_bc("dt", d["log_dt"], 32, parts=P)
        self.act(dt, dt, AF.Exp)
        ar = self.sb("ar", [P, 32], F32); ai = self.sb("ai", [P, 32], F32)
        self.tt(ar, lre, dt, ALU.mult); self.tt(ai, lim, dt, ALU.mult)
        NPW = 16
        pwr = self.sb("pwr", [P, 32, NPW], F32); pwi = self.sb("pwi", [P, 32, NPW], F32)
        mag = self.sb("mag", [P, 32], F32); ang = self.sb("ang", [P, 32], F32); kk = self.sb("kk", [P, 32], F32)
        sn = self.sb("sn", [P, 32], F32); cs = self.sb("cs", [P, 32], F32)
        for mi in range(NPW):
            m = float(mi - 7)
            self.act(mag, ar, AF.Exp, scale=m)
            for (dst, shift) in ((sn, 0.0), (cs, math.pi / 2)):
                self.ts(ang, ai, m, shift, ALU.mult, ALU.add)
                self.ts(kk, ang, 1.0 / TWO_PI, MAGIC, ALU.mult, ALU.add)
                self.ts(kk, kk, -MAGIC, -TWO_PI, ALU.add, ALU.mult)
                self.tt(ang, ang, kk, ALU.add)
                self.act(dst, ang, AF.Sin)
            self.tt(pwr[:, :, mi], mag, cs, ALU.mult)
            self.tt(pwi[:, :, mi], mag, sn, ALU.mult)
        for i in range(8):
            m = 8.0 * (i + 1)
            self.act(mag, ar, AF.Exp, scale=m)
            for (dst, shift) in ((sn, 0.0), (cs, math.pi / 2)):
                self.ts(ang, ai, m, shift, ALU.mult, ALU.add)
                self.ts(kk, ang, 1.0 / TWO_PI, MAGIC, ALU.mult, ALU.add)
                self.ts(kk, kk, -MAGIC, -TWO_PI, ALU.add, ALU.mult)
                self.tt(ang, ang, kk, ALU.add)
                self.act(dst, ang, AF.Sin)
            self.tt(self.LPr[:, :, i], mag, cs, ALU.mult)
            self.tt(self.LPpm[:, 1, :, i], mag, sn, ALU.mult)
            self.ts(self.LPpm[:, 0, :, i], self.LPpm[:, 1, :, i], -1.0, None, ALU.mult)
        self.cp(self.lamA[:, 0, :], pwr[:, :, 15]); self.cp(self.lamA[:, 1, :], pwi[:, :, 15])
        self.ts(self.lamAn, pwi[:, :, 15], -1.0, None, ALU.mult)
        nr = self.sb("nr", [P, 32], F32); den = self.sb("den", [P, 32], F32); t1 = self.sb("t1", [P, 32], F32); t2 = self.sb("t2", [P, 32], F32)
        kre = self.sb("kre", [P, 32], F32); kim = self.sb("kim", [P, 32], F32)
        self.ts(nr, pwr[:, :, 8], -1.0, None, ALU.add)
        ni = pwi[:, :, 8]
        self.tt(den, lre, lre, ALU.mult); self.tt(t1, lim, lim, ALU.mult); self.tt(den, den, t1, ALU.add); self.recip(den, den)
        self.tt(t1, nr, lre, ALU.mult); self.tt(t2, ni, lim, ALU.mult); self.tt(t1, t1, t2, ALU.add); self.tt(kre, t1, den, ALU.mult)
        self.tt(t1, ni, lre, ALU.mult); self.tt(t2, nr, lim, ALU.mult); self.tt(t1, t1, t2, ALU.subtract); self.tt(kim, t1, den, ALU.mult)
        bre = self.sb("bre", [P, 32, 16], F32); bim = self.sb("bim", [P, 32, 16], F32)
        self.ld("misc", bre, d["b_re"].r("g p n -> p g n")); self.ld("misc", bim, d["b_im"].r("g p n -> p g n"))
        bbr = self.sb("bbr", [P, 32, 16], F32); bbi = self.sb("bbi", [P, 32, 16], F32); tb = self.sb("tb", [P, 32, 16], F32)
        kreb = kre.us(2).bc([P, 32, 16]); kimb = kim.us(2).bc([P, 32, 16])
        self.tt(bbr, bre, kreb, ALU.mult); self.tt(tb, bim, kimb, ALU.mult); self.tt(bbr, bbr, tb, ALU.subtract)
        self.tt(bbi, bim, kreb, ALU.mult); self.tt(tb, bre, kimb, ALU.mult); self.tt(bbi, bbi, tb, ALU.add)
        cre = self.sb("cre", [P, 32, 16], F32); cim = self.sb("cim", [P, 32, 16], F32)
        self.ld("misc", cre, d["c_re"].r("g n p -> p g n"), slow=True); self.ld("misc", cim, d["c_im"].r("g n p -> p g n"), slow=True)
        Bkr = self.sb("Bkr", [P, 32, 8, 16], F32); Bki = self.sb("Bki", [P, 32, 8, 16], F32); tB = self.sb("tB", [P, 32, 16], F32)
        for s in range(8):
            pr = pwr[:, :, 14 - s].us(2).bc([P, 32, 16]); pi_ = pwi[:, :, 14 - s].us(2).bc([P, 32, 16])
            self.tt(Bkr[:, :, s, :], bbr, pr, ALU.mult); self.tt(tB, bbi, pi_, ALU.mult); self.tt(Bkr[:, :, s, :], Bkr[:, :, s, :], tB, ALU.subtract)
            self.tt(Bki[:, :, s, :], bbr, pi_, ALU.mult); self.tt(tB, bbi, pr, ALU.mult); self.tt(Bki[:, :, s, :], Bki[:, :, s, :], tB, ALU.add)
        CLr = self.sb("CLr", [P, 32, 8, 16], F32); CLi = self.sb("CLi", [P, 32, 8, 16], F32); tC = self.sb("tC", [P, 32, 16], F32)
        tCr = self.sb("tCr", [P, 32, 16], F32); tCi = self.sb("tCi", [P, 32, 16], F32)
        Cq4r = self.Cq[0:64].r("p g (s n) -> p g s n", s=8); Cq4i = self.Cq[64:128].r("p g (s n) -> p g s n", s=8)
        for mi in range(NPW):
            pr = pwr[:, :, mi].us(2).bc([P, 32, 16]); pi_ = pwi[:, :, mi].us(2).bc([P, 32, 16])
            dr = CLr[:, :, mi, :] if mi < 8 else tCr
            di = CLi[:, :, mi, :] if mi < 8 else tCi
            self.tt(dr, cre, pr, ALU.mult); self.tt(tC, cim, pi_, ALU.mult); self.tt(dr, dr, tC, ALU.subtract)
            self.tt(di, cre, pi_, ALU.mult); self.tt(tC, cim, pr, ALU.mult); self.tt(di, di, tC, ALU.add)
            self.ts(di, di, -1.0, None, ALU.mult)
            if mi >= 8:
                self.cp(Cq4r[:, :, mi - 8, :], tCr)
                self.cp(Cq4i[:, :, mi - 8, :], tCi)
        maskS = self.sb("maskS", [128, 128], F32)
        self.ld("misc", maskS, d["c_mask"])
        dsel = self.sb("dsel", [128, 32], F32)
        for s in range(8):
            self.ld("misc", dsel[s * 16:(s + 1) * 16, :], d["s5_d"].r("(g n) -> n g", n=16), slow=True)
        diagd = self.sb("diagd", [128, 128], F32)
        for g in range(32):
            bk = g % 2
            pT = self.ps(bk, [128, 2, 64], F32)
            self.tr(pT[:, 0, :], Bkr[:, g, :, :].r("p s n -> p (s n)"), self.identF[0:64, 0:64], last=False)
            self.tr(pT[:, 1, :], Bki[:, g, :, :].r("p s n -> p (s n)"), self.identF[0:64, 0:64], last=True)
            self.cp(self.BkT[:, g, :, :], pT, eng="act")
            p0 = self.ps(2 + bk, [128, 128], F32)
            self.mm(p0, Bkr[:, g, :, :].r("p s n -> p (s n)"), CLr[:, g, 0:8, :].r("p s n -> p (s n)"), True, False)
            self.mm(p0, Bki[:, g, :, :].r("p s n -> p (s n)"), CLi[:, g, 0:8, :].r("p s n -> p (s n)"), False, True)
            self.ts(diagd, self.identF, dsel[:, g:g + 1], None, ALU.mult, eng="pool")
            t0f = self.sb("t0f", [128, 128], F32) if g == 0 else t0f
            self.tt(t0f, p0, maskS, ALU.mult)
            self.tt(self.T0[:, g, :], t0f, diagd, ALU.add)
        self.s5_tmp_end = self.off
        return keep

    def phase1(self):
        d = self.din
        NB = self.NB1
        W_ = {}

        def p1_weights():
            stage = [self.sb("wstgA", [128, 1024], F32), self.sb("wstgB", [128, 1024], F32)]
            W_["stage"] = stage
            g_norm = self.load_vec_pk("g_norm", d["norm_ab"], D)
            g_ql = self.load_vec_pk("g_ql", d["g_q_lat"], QL)
            self.ts(g_ql, g_ql, math.sqrt(QL), None, ALU.mult)
            for nm, shape in (("wA", [128, 8, 672]), ("wGm", [128, 8, 512]), ("wU", [128, 8, 512]), ("wGs", [128, 8, 512]),
                              ("wuq", [128, 3, 768]), ("wuk", [128, 2, 512]), ("wuv", [128, 2, 512]), ("wglu", [128, 4, 512])):
                W_[nm] = self.sb(nm, shape, BF16)
            self.load_weight(W_["wA"], d["w_in_ab"], D, 0, 672, g_norm, stage)
            self.load_weight(W_["wGm"], d["w_in_ab"], D, 672, 1184, g_norm, stage)
            self.load_weight(W_["wU"], d["w_in_ab"], D, 1184, 1696, g_norm, stage)
            self.load_weight(W_["wGs"], d["w_in_ab"], D, 1696, 2208, g_norm, stage)
            self.load_weight(W_["wuq"], d["w_uq"], QL, 0, 768, g_ql, stage)
            self.load_weight(W_["wuk"], d["w_uk"], KVL, 0, 512, None, stage)
            self.load_weight(W_["wuv"], d["w_uv"], KVL, 0, 512, None, stage)
            self.load_weight(W_["wglu"], d["w_glu"], SW, 0, 512, None, stage)
        self._p1_weights = p1_weights
        keep = self.s5_setup()
        self.S.barrier()
        self.off = keep
        wA, wGm, wU, wGs, wuq, wuk, wuv, wglu = (W_[k] for k in ("wA", "wGm", "wU", "wGs", "wuq", "wuk", "wuv", "wglu"))
        xt = W_["stage"]
        if False:
            g_norm = None
        bglu = self.load_vec_pk("bglu", d["b_glu"], SW)
        gkv = self.load_vec_bc("gkv", d["g_kv_lat"], KVL)
        gkr = self.load_vec_bc("gkr", d["g_k_rope"], RD)
        gq = self.sb("gq", [128, HD], F32)
        self.ld("misc", gq[:, 0:ND], V(d["g_q_nope"].ap.partition_broadcast(128), d["g_q_nope"].res))
        self.ld("misc", gq[:, ND:HD], V(d["g_q_rope"].ap.partition_broadcast(128), d["g_q_rope"].res))
        gkn = self.load_vec_bc("gkn", d["g_k_nope"], ND)
        self.ts(gkv, gkv, math.sqrt(KVL), None, ALU.mult)
        self.ts(gkr, gkr, math.sqrt(RD), None, ALU.mult)
        self.ts(gq[:, 0:ND], gq[:, 0:ND], math.sqrt(ND), None, ALU.mult)
        self.ts(gq[:, ND:HD], gq[:, ND:HD], math.sqrt(RD), None, ALU.mult)
        nepsA = self.sb("nepsA", [128, 8], F32)
        self.memset(nepsA, KVL * EPS)
        self.memset(nepsA[:, 1:2], QL * EPS)
        self.memset(nepsA[:, 4:5], RD * EPS)
        nepsQ = self.sb("nepsQ", [128, 16], F32)
        self.memset(nepsQ[:, 0:8], ND * EPS)
        self.memset(nepsQ[:, 8:16], RD * EPS)

        def two(name, shape, dt, parts=128):
            return [self.sb(name + str(i), [parts] + shape, dt) for i in range(2)]
        TB = dict(
            hb=two("hb", [D], BF16), ss=two("ss", [8], F32), rs=two("rs", [8], F32), ssq=two("ssq", [16], F32), rsq=two("rsq", [16], F32),
            ssk=two("ssk", [8], F32), rsk=two("rsk", [8], F32), qlb=two("qlb", [QL], BF16), qlT=two("qlT", [3, 128], BF16),
            latf=two("latf", [KVL], F32), latb=two("latb", [KVL], BF16), latT=two("latT", [2, 128], BF16),
            krf=two("krf", [RD], F32), krn=two("krn", [RD], F32), cosT=two("cosT", [32], F32), sinT=two("sinT", [32], F32),
            r1=two("r1", [NH, 32], F32), r2=two("r2", [NH, 32], F32), r1k=two("r1k", [1, 32], F32), r2k=two("r2k", [1, 32], F32), qn=two("qn", [NH, HD], F32), sqq=two("sqq", [NH, HD], BF16),
            Qs=two("Qs", [NH, HD], BF16), Ks=two("Ks", [NH, HD], BF16), knf=two("knf", [NH, ND], F32), sqk=two("sqk", [NH, ND], BF16),
            vb=two("vb", [512], BF16))
        hTs = [self.sb("hT%d" % i, [128, 8, NB], BF16) for i in range(2)]
        KTc = [self.sb("KTc%d" % i, [HD, NH, 128], BF16) for i in range(2)]
        QTb = self.sb("QTb", [HD, NH, NB], BF16)
        KTb = self.sb("KTb", [HD, NH, NB], BF16)
        GMt = self.sb("GMt", [128, 4, NB], BF16)
        gsT = [self.sb("gsT%d" % i, [128, 4, NB], BF16) for i in range(3)]
        J = NB // 8
        F1 = getattr(self, "F1", True)
        X2s = self.sb("X2s", [64, 16, 8, 16], BF16) if F1 else self.sb("X2s", [J, 32, 8, 16], BF16)
        Usel = [self.sb("Usel%d" % i, [128, 32, J], BF16) for i in range(3)]
        Sb = [self.sb("Sb%d" % i, [64, 2, 32, J], F32) for i in range(2)]
        Sp = [self.sb("Sp%d" % i, [128, 32, J], BF16) for i in range(2)]
        carry = self.sb("carry", [64, 2, 32], F32)
        sq1 = self.sb("sq1", [64, 2, 32, 4], F32); sq2 = self.sb("sq2", [64, 2, 32, 4], F32)
        sl1 = self.sb("sl1", [64, 2, 32, 8], F32); sl2 = self.sb("sl2", [64, 2, 32, 8], F32)
        Z2 = self.sb("Z2", [64, 8, 16, 16], BF16) if F1 else self.sb("Z2", [J, 8, 32, 16], BF16)
        yz = self.sb("yz", [J, 4, 128], F32); yz2 = self.sb("yz2", [J, 4, 128], F32)
        zT = self.sb("zT", [128, 4, NB], BF16)
        sig = self.sb("sig", [128, NB], F32)
        MSt = self.sb("MSt", [128, 4, NB], BF16)
        self.p1_end = self.off

        lamre_b = self.lamA[:, 0, :].us(1).bc([64, 2, 32])
        lam_pm = self.sb("lam_pm", [64, 2, 32], F32)
        self.cp(lam_pm[:, 0, :], self.lamAn); self.cp(lam_pm[:, 1, :], self.lamA[:, 1, :])

        st_ = dict(xi=0, tc=0)

        def tile_front(io, TT, tk0, ti, VSd, P, bk2, hT):
            b0, b1 = bk2
            xi = st_["xi"]; st_["xi"] += 1
            tp = st_["tc"] % 2; st_["tc"] += 1
            B = {k: v[tp] for k, v in TB.items()}
            x = xt[xi % 2]
            ss, rs = B["ss"], B["rs"]
            self.ld("xt%d" % (xi % 2), x[0:TT, :], io["x"][tk0:tk0 + TT, :])
            self.ld("cos%d" % tp, B["cosT"][0:TT, :], io["cos"][tk0:tk0 + TT, :])
            self.ld("sin%d" % tp, B["sinT"][0:TT, :], io["sin"][tk0:tk0 + TT, :])
            self.act(B["hb"][0:TT, :], x[0:TT, :], AF.Square, accum=ss[0:TT, 0:1])
            self.rstd_from_ss(rs[0:TT, 0:1], ss[0:TT, 0:1], D)
            self.ts(B["hb"][0:TT, :], x[0:TT, :], rs[0:TT, 0:1], None, ALU.mult)
            yield
            pT = self.ps(b0, [128, 8, TT], BF16)
            for k in range(8):
                self.tr(pT[:, k, :], B["hb"][0:TT, k * 128:(k + 1) * 128], self.identB[0:TT, 0:TT], last=(k == 7))
            self.cp(hT[:, :, ti * TT:(ti + 1) * TT], pT, eng="act")
            yield
            pA = self.ps(b0, [TT, 512], F32); pB = self.ps(b1, [TT, 160], F32)
            for k in range(8):
                self.mm(pA, hT[:, k, ti * TT:(ti + 1) * TT], wA[:, k, 0:512], k == 0, k == 7)
            for k in range(8):
                self.mm(pB, hT[:, k, ti * TT:(ti + 1) * TT], wA[:, k, 512:672], k == 0, k == 7)
            yield
            trash = B["sqq"][0:TT].r("p h d -> p (h d)")
            self.act(trash[:, 0:QL], pA[:, 0:QL], AF.Square, accum=ss[0:TT, 1:2])
            self.act(trash[:, 0:128], pA[:, QL:512], AF.Square, accum=ss[0:TT, 2:3])
            self.act(trash[:, 0:128], pB[:, 0:128], AF.Square, accum=ss[0:TT, 3:4])
            self.act(trash[:, 0:RD], pB[:, 128:160], AF.Square, accum=ss[0:TT, 4:5])
            self.tt(ss[0:TT, 2:3], ss[0:TT, 2:3], ss[0:TT, 3:4], ALU.add)
            self.tt(rs[0:TT, 1:5], ss[0:TT, 1:5], nepsA[0:TT, 1:5], ALU.add)
            self.act(rs[0:TT, 1:5], rs[0:TT, 1:5], AF.Sqrt)
            self.recip(rs[0:TT, 1:5], rs[0:TT, 1:5])
            yield
            self.ts(B["qlb"][0:TT, :], pA[:, 0:QL], rs[0:TT, 1:2], None, ALU.mult)
            lf = B["latf"]
            self.stt(lf[0:TT, 0:128], pA[:, QL:512], rs[0:TT, 2:3], gkv[0:TT, 0:128], ALU.mult, ALU.mult)
            self.stt(lf[0:TT, 128:256], pB[:, 0:128], rs[0:TT, 2:3], gkv[0:TT, 128:256], ALU.mult, ALU.mult)
            self.st("latf%d" % tp, io["lat"][tk0:tk0 + TT, :], lf[0:TT, :])
            self.stt(B["krn"][0:TT, :], pB[:, 128:160], rs[0:TT, 4:5], gkr[0:TT, :], ALU.mult, ALU.mult)
            yield
            def q_chain():
                pQT = self.ps(b0, [128, 3, TT], BF16)
                for k in range(3):
                    self.tr(pQT[:, k, :], B["qlb"][0:TT, k * 128:(k + 1) * 128], self.identB[0:TT, 0:TT], last=(k == 2))
                self.cp(B["qlT"][:, :, 0:TT], pQT, eng="act")
                yield
                qn = B["qn"]
                qf = qn[0:TT].r("p h d -> p (h d)")
                pQ1 = self.ps(b0, [TT, 512], F32)
                for k in range(3):
                    self.mm(pQ1, B["qlT"][:, k, 0:TT], wuq[:, k, 0:512], k == 0, k == 2)
                self.cp(qf[:, 0:512], pQ1, eng="act")
                yield
                pQ2 = self.ps(b0, [TT, 256], F32)
                for k in range(3):
                    self.mm(pQ2, B["qlT"][:, k, 0:TT], wuq[:, k, 512:768], k == 0, k == 2)
                self.cp(qf[:, 512:768], pQ2, eng="act")
                yield
                sqv = B["sqq"][0:TT]
                self.act(sqv.r("p h d -> p (h d)"), qf, AF.Square)
                self.red(B["ssq"][0:TT, 0:8], sqv[:, :, 0:ND])
                self.red(B["ssq"][0:TT, 8:16], sqv[:, :, ND:HD])
                yield
                self.tt(B["rsq"][0:TT, :], B["ssq"][0:TT, :], nepsQ[0:TT, :], ALU.add)
                self.act(B["rsq"][0:TT, :], B["rsq"][0:TT, :], AF.Sqrt)
                self.recip(B["rsq"][0:TT, :], B["rsq"][0:TT, :])
                yield
                self.tt(qn[0:TT, :, 0:ND], qn[0:TT, :, 0:ND], B["rsq"][0:TT, 0:8].us(2).bc([TT, NH, ND]), ALU.mult)
                self.tt(qn[0:TT, :, ND:HD], qn[0:TT, :, ND:HD], B["rsq"][0:TT, 8:16].us(2).bc([TT, NH, RD]), ALU.mult)
                self.tt(qn[0:TT], qn[0:TT], gq[0:TT].us(1).bc([TT, NH, HD]), ALU.mult)
                yield
                self.cp(B["Qs"][0:TT, :, 0:ND], qn[0:TT, :, 0:ND], eng="act")
                self.rope(B["Qs"][0:TT, :, ND:HD], qn[0:TT, :, ND:HD], B["cosT"], B["sinT"], B["r1"], B["r2"], TT, NH)
                yield
                pq = self.ps(b0, [HD, NH, TT], BF16)
                for h in range(NH):
                    self.tr(pq[:, h, :], B["Qs"][0:TT, h, :], self.identB[0:TT, 0:TT], last=(h == NH - 1))
                self.cp(QTb[:, :, ti * TT:(ti + 1) * TT], pq, eng="act")

            def k_chain():
                kf = B["krf"]
                self.rope(kf[0:TT, :].r("p (o r) -> p o r", o=1), B["krn"][0:TT, :].r("p (o r) -> p o r", o=1), B["cosT"], B["sinT"], B["r1k"], B["r2k"], TT, 1)
                self.st("krf%d" % tp, io["kr"][tk0:tk0 + TT, :], kf[0:TT, :])
                self.cp(B["Ks"][0:TT, :, ND:HD], kf[0:TT, :].us(1).bc([TT, NH, RD]))
                yield
                yield from self.kv_from_lat(lf, TT, B, wuk, wuv, gkn, KTb, ti * TT, (b1, b1, b1, b1))
                self.st("vb%d" % tp, VSd[P + tk0:P + tk0 + TT, :], B["vb"][0:TT, :])

            gens = [q_chain(), k_chain()]
            alive = [True, True]
            while any(alive):
                for gi in range(2):
                    if alive[gi]:
                        try:
                            next(gens[gi])
                        except StopIteration:
                            alive[gi] = False
                yield

        def cache_tile(io, kt_, KTd, VSd, tp, banks):
            B = {k: v[tp] for k, v in TB.items()}
            lf = B["latf"]; kf = B["krf"]
            self.ld("latf%d" % tp, lf, io["clat"][kt_ * 128:(kt_ + 1) * 128, :])
            self.ld("krf%d" % tp, kf, io["ckr"][kt_ * 128:(kt_ + 1) * 128, :])
            self.cp(B["Ks"][:, :, ND:HD], kf.us(1).bc([128, NH, RD]))
            kb = KTc[tp]
            yield
            yield from self.kv_from_lat(lf, 128, B, wuk, wuv, gkn, kb, 0, banks)
            self.st("KTc%d" % tp, KTd[:, :, kt_ * 128:(kt_ + 1) * 128], kb[:, :, 0:128])
            self.st("vb%d" % tp, VSd[kt_ * 128:(kt_ + 1) * 128, :], B["vb"])


        def stageA_pre(blk):
            si, b = blk["si"], blk["b"]
            kind, idx, Tq, P = self.seqs[si]
            io = self.seq_io(si)
            KTd, VSd = self.dscr["KT" + str(si)], self.dscr["VS" + str(si)]
            nb = blk["nb"]; TT = min(128, nb); ntile = nb // TT
            t0 = b * nb
            if False:
                yield
            blk["tile_gens"] = [tile_front(io, TT, t0 + ti * TT, ti, VSd, P, ((0, 1), (2, 3))[ti % 2], hTs[blk["r"] % 2]) for ti in range(ntile)]

        def tiles_done(blk):
            si, b, nb = blk["si"], blk["b"], blk["nb"]
            kind, idx, Tq, P = self.seqs[si]
            t0 = b * nb
            QTd, KTd = self.dscr["QT" + str(si)], self.dscr["KT" + str(si)]
            self.st("QTb", QTd[:, :, t0:t0 + nb], QTb[:, :, 0:nb])
            self.st("KTb", KTd[:, :, P + t0:P + t0 + nb], KTb[:, :, 0:nb])

        def stageA2(blk):
            si, b, slot3, slot2 = blk["si"], blk["b"], blk["r"] % 3, blk["r"] % 2
            GMd = self.dscr["GM" + str(si)]
            nb = blk["nb"]; Jb = nb // 8; t0 = b * nb
            hT = hTs[slot2]
            for (wG, dst) in ((wGm, GMt), (wGs, gsT[slot3])):
                for m in range(4):
                    pg = self.ps(4 + (m % 2), [128, nb], F32)
                    for k in range(8):
                        self.mm(pg, wG[:, k, m * 128:(m + 1) * 128], hT[:, k, 0:nb], k == 0, k == 7)
                    self.act(dst[:, m, 0:nb], pg, AF.Silu)
                    yield
            self.st("GMt", GMd[:, :, t0:t0 + nb], GMt[:, :, 0:nb])
            for s_ in range(8):
                pu = self.ps(4 + (s_ % 2), [Jb, 512], F32)
                for k in range(8):
                    self.mm(pu, hT[:, k, s_:nb:8], wU[:, k, :], k == 0, k == 7)
                e_ = "act" if s_ % 2 else "dve"
                if F1:
                    self.cp(X2s[0:Jb, :, s_, :], pu[:, 0:256].r("j (g n) -> j g n", g=16), eng=e_)
                    self.cp(X2s[32:32 + Jb, :, s_, :], pu[:, 256:512].r("j (g n) -> j g n", g=16), eng=e_)
                else:
                    self.cp(X2s[0:Jb, :, s_, :], pu.r("j (g n) -> j g n", g=32), eng=e_)
                yield
            for g4 in range(8):
                pus = self.ps(4 + (g4 % 2), [128, 4, Jb], BF16)
                for gg in range(4):
                    g = g4 * 4 + gg
                    p0 = (g // 16) * 32
                    if F1:
                        self.tr(pus[:, gg, :], X2s[p0:p0 + Jb, g % 16, :, :].r("j s n -> j (s n)"), self.identB[p0:p0 + Jb, p0:p0 + Jb], last=(gg == 3))
                    else:
                        self.tr(pus[:, gg, :], X2s[0:Jb, g, :, :].r("j s n -> j (s n)"), self.identB[0:Jb, 0:Jb], last=(gg == 3))
                self.cp(Usel[slot3][:, g4 * 4:(g4 + 1) * 4, 0:Jb], pus, eng=("act" if g4 % 2 else "dve"))
                yield
            for g8 in range(4):
                pw = self.ps(4 + (g8 % 2), [64, 2, 8, Jb], F32)
                for gg in range(8):
                    g = g8 * 8 + gg
                    for c in range(2):
                        self.mm(pw[:, c, gg, :], self.BkT[:, g, c, :], Usel[slot3][:, g, 0:Jb], True, True, last=(gg == 7 and c == 1))
                self.cp(Sb[slot2][:, :, g8 * 8:(g8 + 1) * 8, 0:Jb], pw, eng=("act" if g8 % 2 else "dve"))
                yield

        def stageB(blk):
            si, b, slot2 = blk["si"], blk["b"], blk["r"] % 2
            kind, idx, Tq, P = self.seqs[si]
            io = self.seq_io(si)
            Jb = blk["nb"] // 8
            S_ = Sb[slot2]; Spb = Sp[slot2]
            if b == 0:
                if P > 0:
                    self.ld("carry", carry[:, 0, :], io["sre"].r("g p -> p g"), slow=True)
                    self.ld("carry", carry[:, 1, :], io["sim"].r("g p -> p g"), slow=True)
                else:
                    self.memset(carry, 0.0)
            self.cp(Spb[0:64, :, 0], carry[:, 0, :], eng="pool")
            self.cp(Spb[64:128, :, 0], carry[:, 1, :], eng="pool")
            L = 8; Q = Jb // L
            S5 = S_[:, :, :, 0:Jb].r("p c g (q l) -> p c g q l", l=L)
            a1 = sq1[:, :, :, 0:Q]; a2 = sq2[:, :, :, 0:Q]
            lre_q = self.lamA[:, 0, :].us(1).us(3).bc([64, 2, 32, Q])
            for i in range(1, L):
                prev = S5[:, :, :, :, i - 1]; cur = S5[:, :, :, :, i]
                self.tt(a1, prev, lre_q, ALU.mult, eng="pool")
                self.tt(a2[:, 0], prev[:, 1], lam_pm[:, 0, :].us(2).bc([64, 32, Q]), ALU.mult, eng="pool")
                self.tt(a2[:, 1], prev[:, 0], lam_pm[:, 1, :].us(2).bc([64, 32, Q]), ALU.mult, eng="pool")
                self.tt(a1, a1, a2, ALU.add, eng="pool")
                self.tt(cur, cur, a1, ALU.add, eng="pool")
                yield
            LPr_b = self.LPr.us(1).bc([64, 2, 32, L])
            for q in range(Q):
                C = carry if q == 0 else S5[:, :, :, q - 1, L - 1]
                self.tt(sl1, LPr_b, C.us(3).bc([64, 2, 32, L]), ALU.mult, eng="pool")
                self.tt(sl2[:, 0], self.LPpm[:, 0], C[:, 1].us(2).bc([64, 32, L]), ALU.mult, eng="pool")
                self.tt(sl2[:, 1], self.LPpm[:, 1], C[:, 0].us(2).bc([64, 32, L]), ALU.mult, eng="pool")
                self.tt(sl1, sl1, sl2, ALU.add, eng="pool")
                self.tt(S5[:, :, :, q, :], S5[:, :, :, q, :], sl1, ALU.add, eng="pool")
                yield
            self.cp(carry, S_[:, :, :, Jb - 1], eng="pool")
            if Jb > 1:
                self.cp(Spb[0:64, :, 1:Jb], S_[:, 0, :, 0:Jb - 1], eng="pool")
                self.cp(Spb[64:128, :, 1:Jb], S_[:, 1, :, 0:Jb - 1], eng="pool")
            if b == Tq // blk["nb"] - 1:
                self.st("fin", io["re"].r("g p -> p g"), carry[:, 0, :], slow=True)
                self.st("fin", io["im"].r("g p -> p g"), carry[:, 1, :], slow=True)
            yield

        def stageC(blk):
            si, b, slot3, slot2 = blk["si"], blk["b"], blk["r"] % 3, blk["r"] % 2
            MSd = self.dscr["MS" + str(si)]
            nb = blk["nb"]; Jb = nb // 8; t0 = b * nb
            U_ = Usel[slot3]; Spb = Sp[slot2]; gs_ = gsT[slot3]
            for g4 in range(8):
                py = self.ps(6 + (g4 % 2), [Jb, 4, 128], F32)
                for gg in range(4):
                    g = g4 * 4 + gg
                    self.mm(py[:, gg, :], U_[:, g, 0:Jb], self.T0[:, g, :], True, False)
                    self.mm(py[:, gg, :], Spb[:, g, 0:Jb], self.Cq[:, g, :], False, True, last=(gg == 3))
                self.cp(yz[0:Jb], py, eng="act")
                self.tt(yz2[0:Jb], yz[0:Jb], yz[0:Jb], ALU.mult)
                self.ts(yz2[0:Jb], yz2[0:Jb], 0.044715, 1.0, ALU.mult, ALU.add)
                self.tt(yz2[0:Jb], yz2[0:Jb], yz[0:Jb], ALU.mult)
                self.act(yz2[0:Jb], yz2[0:Jb], AF.Sigmoid, scale=2.0 * math.sqrt(2.0 / math.pi))
                zp0 = (g4 // 4) * 32; zg0 = (g4 % 4) * 4
                if not F1:
                    zp0 = 0; zg0 = g4 * 4
                self.tt(Z2[zp0:zp0 + Jb, :, zg0:zg0 + 4, :].r("j s g n -> j g s n"), yz[0:Jb].r("j g (s n) -> j g s n", s=8),
                        yz2[0:Jb].r("j g (s n) -> j g s n", s=8), ALU.mult)
                yield
            if getattr(self, 'STOPC', 9) < 2:
                return
            for s in range(8):
                if F1:
                    pzA = self.ps(6, [128, 2, Jb], BF16); pzB = self.ps(7, [128, 2, Jb], BF16)
                    for m in range(2):
                        self.tr(pzA[:, m, :], Z2[0:Jb, s, m * 8:(m + 1) * 8, :].r("j g n -> j (g n)"), self.identB[0:Jb, 0:Jb], last=(m == 1))
                    for m in range(2):
                        self.tr(pzB[:, m, :], Z2[32:32 + Jb, s, m * 8:(m + 1) * 8, :].r("j g n -> j (g n)"), self.identB[32:32 + Jb, 32:32 + Jb], last=(m == 1))
                    self.cp(zT[:, 0:2, s:nb:8], pzA, eng="act")
                    self.cp(zT[:, 2:4, s:nb:8], pzB, eng="dve")
                else:
                    pz = self.ps(6 + (s % 2), [128, 4, Jb], BF16)
                    for m in range(4):
                        self.tr(pz[:, m, :], Z2[0:Jb, s, m * 8:(m + 1) * 8, :].r("j g n -> j (g n)"), self.identB[0:Jb, 0:Jb], last=(m == 3))
                    self.cp(zT[:, :, s:nb:8], pz, eng=("act" if s % 2 else "dve"))
                yield
            if getattr(self, 'STOPC', 9) < 3:
                return
            for m in range(4):
                pg = self.ps(6 + (m % 2), [128, nb], F32)
                for k in range(4):
                    self.mm(pg, wglu[:, k, m * 128:(m + 1) * 128], zT[:, k, 0:nb], k == 0, k == 3)
                self.act(sig[:, 0:nb], pg, AF.Sigmoid, bias=bglu[:, m:m + 1])
                self.tt(sig[:, 0:nb], sig[:, 0:nb], zT[:, m, 0:nb], ALU.mult)
                self.tt(MSt[:, m, 0:nb], sig[:, 0:nb], gs_[:, m, 0:nb], ALU.mult)
                yield
            self.st("MSt", MSd[:, :, t0:t0 + nb], MSt[:, :, 0:nb])

        blocks = []
        for si, (kind, idx, Tq, P) in enumerate(self.seqs):
            nb = min(NB, Tq)
            for b in range(Tq // nb):
                blocks.append(dict(si=si, b=b, nb=nb, r=len(blocks)))

        def chain(*gens):
            for g in gens:
                if g is not None:
                    yield from g

        nblk = len(blocks)

        def step(g):
            try:
                return next(g), True
            except StopIteration:
                return None, False

        def get(fn, i):
            return fn(blocks[i]) if 0 <= i < nblk else iter(())

        SEQ_DRV = getattr(self, "SEQ_DRV", False)
        self.st_queue = "sp"
        for t in range(min(nblk + 3, getattr(self, "MAXR", 10 ** 9))):
            if SEQ_DRV:
                skip = getattr(self, "SKIP", "") if t == getattr(self, "MAXR", 10 ** 9) - 1 else ""
                for nm_, g_ in (("C", get(stageC, t - 3)), ("B", get(stageB, t - 2)), ("A2", get(stageA2, t - 1)), ("P", get(stageA_pre, t))):
                    if nm_ in skip.split(","):
                        continue
                    for _ in g_:
                        pass
                if t < nblk and "T" not in skip.split(","):
                    for tg in blocks[t]["tile_gens"]:
                        for _ in tg:
                            pass
                    tiles_done(blocks[t])
                continue
            pre = get(stageA_pre, t)
            others = [get(stageA2, t - 1), get(stageB, t - 2), get(stageC, t - 3)]
            o_alive = [True, True, True]
            p_alive = True
            while p_alive:
                _, p_alive = step(pre)
                for i in range(3):
                    if o_alive[i]:
                        _, o_alive[i] = step(others[i])
            tiles = list(blocks[t]["tile_gens"]) if t < nblk else []
            t_alive = [True] * len(tiles)
            while any(t_alive) or any(o_alive):
                for i, tg in enumerate(tiles):
                    if t_alive[i]:
                        _, t_alive[i] = step(tg)
                for i in range(3):
                    if o_alive[i]:
                        _, o_alive[i] = step(others[i])
            if t < nblk:
                tiles_done(blocks[t])

    def _end_phase1_marker(self):
        pass

    def rope(self, out, in_, cosT, sinT, r1, r2, TT, nh):
        cb = cosT[0:TT, :].us(1).bc([TT, nh, 32])
        a = r1[0:TT, 0:nh, :]; b = r2[0:TT, 0:nh, :]
        self.tt(a, in_, cb, ALU.mult)
        self.tt(b[:, :, 0:16], in_[:, :, 16:32], sinT[0:TT, 0:16].us(1).bc([TT, nh, 16]), ALU.mult)
        self.tt(b[:, :, 16:32], in_[:, :, 0:16], sinT[0:TT, 16:32].us(1).bc([TT, nh, 16]), ALU.mult)
        self.tt(out, a, b, ALU.add)

    def kv_from_lat(self, lf, TT, B, wuk, wuv, gkn, KTdst, col0, banks):
        bl, bk_, bv, bkT = banks
        latb, latT, knf, Ks = B["latb"], B["latT"], B["knf"], B["Ks"]
        self.cp(latb[0:TT, :], lf[0:TT, :])
        pl = self.ps(bl, [128, 2, TT], BF16)
        for k in range(2):
            self.tr(pl[:, k, :], latb[0:TT, k * 128:(k + 1) * 128], self.identB[0:TT, 0:TT], last=(k == 1))
        self.cp(latT[:, :, 0:TT], pl, eng="act")
        yield
        pk = self.ps(bk_, [TT, 512], F32)
        for k in range(2):
            self.mm(pk, latT[:, k, 0:TT], wuk[:, k, :], k == 0, k == 1)
        kf = knf[0:TT].r("p h d -> p (h d)")
        self.cp(kf, pk, eng="act")
        sqv = B["sqk"][0:TT]
        self.act(sqv.r("p h d -> p (h d)"), pk, AF.Square)
        yield
        pv = self.ps(bv, [TT, 512], F32)
        for k in range(2):
            self.mm(pv, latT[:, k, 0:TT], wuv[:, k, :], k == 0, k == 1)
        self.cp(B["vb"][0:TT, :], pv, eng="act")
        yield
        self.red(B["ssk"][0:TT, :], sqv)
        self.rstd_from_ss(B["rsk"][0:TT, :], B["ssk"][0:TT, :], ND)
        yield
        self.tt(knf[0:TT], knf[0:TT], B["rsk"][0:TT, :].us(2).bc([TT, NH, ND]), ALU.mult)
        self.tt(Ks[0:TT, :, 0:ND], knf[0:TT], gkn[0:TT].us(1).bc([TT, NH, ND]), ALU.mult)
        yield
        pkT = self.ps(bkT, [HD, NH, TT], BF16)
        for h in range(NH):
            self.tr(pkT[:, h, :], Ks[0:TT, h, :], self.identB[0:TT, 0:TT], last=(h == NH - 1))
        self.cp(KTdst[:, :, col0:col0 + TT], pkT, eng="act")

    def phase1b(self):
        d = self.din
        work = [(si, kt_) for si, (kind, idx, Tq, P) in enumerate(self.seqs) for kt_ in range(P // 128)]
        if not work:
            return
        self.off = self.base_off
        stage = [self.sb("wstgA1b", [128, 1024], F32), self.sb("wstgB1b", [128, 1024], F32)]
        wuk = self.sb("wuk1b", [128, 2, 512], BF16); self.load_weight(wuk, d["w_uk"], KVL, 0, 512, None, stage)
        wuv = self.sb("wuv1b", [128, 2, 512], BF16); self.load_weight(wuv, d["w_uv"], KVL, 0, 512, None, stage)
        gkn = self.load_vec_bc("gkn1b", d["g_k_nope"], ND)
        NT = 8
        slots = []
        for i in range(NT):
            slots.append(dict(
                latf=self.sb("c_latf%d" % i, [128, KVL], F32), krf=self.sb("c_krf%d" % i, [128, RD], F32),
                latb=self.sb("c_latb%d" % i, [128, KVL], BF16), latT=self.sb("c_latT%d" % i, [128, 2, 128], BF16),
                knf=self.sb("c_knf%d" % i, [128, NH, ND], F32), sqk=self.sb("c_sqk%d" % i, [128, NH, ND], BF16),
                Ks=self.sb("c_Ks%d" % i, [128, NH, HD], BF16), vb=self.sb("c_vb%d" % i, [128, 512], BF16),
                ssk=self.sb("c_ssk%d" % i, [128, 8], F32), rsk=self.sb("c_rsk%d" % i, [128, 8], F32),
                KTc=self.sb("c_KTc%d" % i, [HD, NH, 128], BF16)))

        def cache_tile(si, kt_, th):
            B = slots[th]
            io = self.seq_io(si)
            KTd, VSd = self.dscr["KT" + str(si)], self.dscr["VS" + str(si)]
            lf = B["latf"]; kf = B["krf"]
            self.ld("c_latf%d" % th, lf, io["clat"][kt_ * 128:(kt_ + 1) * 128, :])
            self.ld("c_krf%d" % th, kf, io["ckr"][kt_ * 128:(kt_ + 1) * 128, :])
            self.cp(B["Ks"][:, :, ND:HD], kf.us(1).bc([128, NH, RD]))
            yield
            yield from self.kv_from_lat(lf, 128, B, wuk, wuv, gkn, B["KTc"], 0, (th, th, th, th))
            self.st("c_KTc%d" % th, KTd[:, :, kt_ * 128:(kt_ + 1) * 128], B["KTc"])
            self.st("c_vb%d" % th, VSd[kt_ * 128:(kt_ + 1) * 128, :], B["vb"])

        threads = [None] * NT
        nxt = 0
        while nxt < len(work) or any(t is not None for t in threads):
            for th in range(NT):
                if threads[th] is None and nxt < len(work):
                    threads[th] = cache_tile(work[nxt][0], work[nxt][1], th)
                    nxt += 1
                if threads[th] is not None:
                    try:
                        next(threads[th])
                    except StopIteration:
                        threads[th] = None

    def phase2(self):
        d = self.din
        self.off = self.base_off
        NQ = self.NQ
        xt = [self.sb("x2t%d" % i, [128, D], F32) for i in range(2)]
        wo = self.sb("wo", [128, 8, D], BF16)
        self.load_weight(wo, d["w_out_ab"], D, 0, D, None, xt)
        Kmax = max(P + Tq for (_, _, Tq, P) in self.seqs)
        nktmax = (Kmax + 127) // 128
        KTs = self.sb("KTs", [HD, NH, Kmax], BF16)
        Vp = self.sb("Vp", [128, nktmax, NH, 65], BF16)
        self.memset(Vp[:, :, :, 64:65], 1.0)
        nchmax = (Kmax + 511) // 512
        KTc_ = [KTs[:, :, c * 512:min(Kmax, (c + 1) * 512)].named("KTs_c%d" % c) for c in range(nchmax)]
        Vpc_ = [Vp[:, c * 4:min(nktmax, (c + 1) * 4), :, :].named("Vp_c%d" % c) for c in range(nchmax)]
        self.S.barrier()
        QTq = [self.sb("QTq%d" % i, [HD, NH, NQ], BF16) for i in range(2)]
        GMq = [self.sb("GMq%d" % i, [128, 4, NQ], BF16) for i in range(2)]
        mixs = [self.sb("mix%d" % i, [128, 8, NQ], BF16) for i in range(2)]
        NPT = 4
        PT = [self.sb("PT%d" % i, [128, NQ], BF16) for i in range(NPT)]
        rsum = [self.sb("rsum%d" % i, [1, NQ], F32) for i in range(2)]
        bcs = [self.sb("bcs%d" % i, [64, NQ], F32) for i in range(2)]
        stage = [self.sb("wstgA2", [128, 1024], F32), self.sb("wstgB2", [128, 1024], F32)]
        g_c = self.load_vec_pk("g_c", d["norm_c"], D)
        WC = self.dscr["WC"]
        wtmp = [self.sb("wtmp%d" % i, [128, 1024], BF16) for i in range(2)]
        wi = 0
        srcv = d["w_in_c"].r("(k p) n -> p k n", p=128)
        WCv = WC.r("(k p) n -> p k n", p=128)
        for k in range(8):
            for c in range(3):
                stg = stage[wi % 2]; wt = wtmp[wi % 2]
                self.ld("wstg%d" % (wi % 2), stg, srcv[:, k, c * 1024:(c + 1) * 1024])
                self.ts(wt, stg, g_c[:, k:k + 1], None, ALU.mult, eng="dve")
                self.st("wtmp%d" % (wi % 2), WCv[:, k, c * 1024:(c + 1) * 1024], wt)
                wi += 1
        xi = 0
        qbc = 0
        qlist = [(si_, qb_) for si_, (_k, _i, Tq_, _P) in enumerate(self.seqs) for qb_ in range(Tq_ // min(NQ, Tq_))]

        def issue_q(gk):
            si_, qb_ = qlist[gk]
            Tq_ = self.seqs[si_][2]
            nq_ = min(NQ, Tq_); q0_ = qb_ * nq_; sl_ = gk % 2
            self.ld("QTq%d" % sl_, QTq[sl_][:, :, 0:nq_], self.dscr["QT%d" % si_][:, :, q0_:q0_ + nq_])
            self.ld("GMq%d" % sl_, GMq[sl_][:, :, 0:nq_], self.dscr["GM%d" % si_][:, :, q0_:q0_ + nq_])
            self.ld("mixms%d" % sl_, mixs[sl_][:, 4:8, 0:nq_], self.dscr["MS%d" % si_][:, :, q0_:q0_ + nq_])

        for si, (kind, idx, Tq, P) in enumerate(self.seqs):
            io = self.seq_io(si)
            QTd, KTd, VSd, GMd, MSd, X1d = (self.dscr[n + str(si)] for n in ("QT", "KT", "VS", "GM", "MS", "X1"))
            Ktot = P + Tq
            nkt = (Ktot + 127) // 128
            nch = (Ktot + 511) // 512
            for c in range(nch):
                c0k = c * 512; c1k = min(Ktot, c0k + 512)
                self.ld("KTs_c%d" % c, KTc_[c][:, :, 0:c1k - c0k], KTd[:, :, c0k:c1k])
                for kt_ in range(c * 4, min(nkt, c * 4 + 4)):
                    kk = min(128, Ktot - kt_ * 128)
                    self.ld("Vp_c%d" % c, Vpc_[c][0:kk, kt_ - c * 4, :, 0:64], VSd[kt_ * 128:kt_ * 128 + kk, :].r("k (h d) -> k h d", h=NH))
            nq = min(NQ, Tq)
            TT = min(128, nq)
            for qb in range(Tq // nq):
                q0 = qb * nq
                sl = qbc % 2
                Qq, Gq, mix = QTq[sl], GMq[sl], mixs[sl]
                if qbc == 0:
                    issue_q(0)
                if qbc + 1 < len(qlist):
                    issue_q(qbc + 1)
                qbc += 1
                if kind == "p":
                    tiles = [(kt_, 128, 0, False) for kt_ in range(q0 // 128)]
                    tiles += [(q0 // 128 + m, 128, 128 * m, True) for m in range(nq // 128)]
                else:
                    tiles = [(kt_, min(128, Ktot - kt_ * 128), 0, False) for kt_ in range(nkt)]
                ntl = len(tiles)
                if kind == "s" and NH * nq <= NQ:
                    pOall = self.ps(4, [65, NH, nq], F32)
                    win = []
                    for i in range(ntl + 2):
                        if i < ntl:
                            kt_, kk, c0, diag = tiles[i]
                            pS = self.ps(i % NPT, [128, NH, nq], F32)
                            for h in range(NH):
                                self.mm(pS[0:kk, h, :], KTc_[kt_ // 4][:, h, (kt_ % 4) * 128:(kt_ % 4) * 128 + kk], Qq[:, h, 0:nq], True, True, last=(h == NH - 1))
                            pt = PT[i % NPT]
                            self.act(pt[0:kk, 0:NH * nq], pS[0:kk].r("k h q -> k (h q)"), AF.Exp, scale=ATTN_SCALE)
                            win.append((i, pt))
                        if i >= 2:
                            j, pt = win.pop(0)
                            kt_, kk, c0, diag = tiles[j]
                            for h in range(NH):
                                self.mm(pOall[:, h, :], Vpc_[kt_ // 4][0:kk, kt_ % 4, h, :], pt[0:kk, h * nq:(h + 1) * nq], j == 0 and h == 0, j == ntl - 1 and h == NH - 1, last=(h == NH - 1))
                    self.recip(rsum[0][:, 0:NH * nq], pOall[64:65].r("o h q -> o (h q)"))
                    pBc = self.ps(6, [64, NH * nq], F32)
                    self.mm(pBc, self.onesF[0:1, 0:64], rsum[0][:, 0:NH * nq], True, True)
                    self.cp(bcs[0][:, 0:NH * nq], pBc, eng="act")
                    for h in range(NH):
                        hs = h % 2
                        dst = mix[hs * 64:hs * 64 + 64, h // 2, 0:nq]
                        self.tt(dst, pOall[0:64, h, :], bcs[0][:, h * nq:(h + 1) * nq], ALU.mult)
                        self.tt(dst, dst, Gq[hs * 64:hs * 64 + 64, h // 2, 0:nq], ALU.mult, eng="pool")
                    tiles = []
                    ntl = 0
                items = [(h, ti) for h in range(NH) for ti in range(ntl)]
                pending = []
                LOOK = 3
                window = []
                for i, item in enumerate(items + [None] * LOOK):
                    cur = None
                    if item is not None:
                        h, ti = item
                        kt_, kk, c0, diag = tiles[ti]
                        pS = self.ps(i % NPT, [128, nq], F32)
                        pt = PT[i % NPT]
                        self.mm(pS[0:kk, c0:nq], KTc_[kt_ // 4][:, h, (kt_ % 4) * 128:(kt_ % 4) * 128 + kk], Qq[:, h, c0:nq], True, True)
                        self.act(pt[0:kk, c0:nq], pS[0:kk, c0:nq], AF.Exp, scale=ATTN_SCALE)
                        if diag:
                            self.memset(pt[64:128, c0:c0 + 64], 0.0)
                        cur = (h, ti, pt)
                    window.append(cur)
                    prev = window.pop(0) if len(window) > LOOK else None
                    if prev is not None:
                        h, ti, pt = prev
                        kt_, kk, c0, diag = tiles[ti]
                        pO = self.ps(4 + (h % 2), [65, nq], F32)
                        self.mm(pO[:, c0:nq], Vpc_[kt_ // 4][0:kk, kt_ % 4, h, :], pt[0:kk, c0:nq], ti == 0, ti == ntl - 1)
                        if ti == ntl - 1:
                            hs = h % 2
                            self.recip(rsum[hs][:, 0:nq], pO[64:65, :])

                            def fin(h=h, hs=hs, pO=pO):
                                pBc = self.ps(6 + hs, [64, nq], F32)
                                self.mm(pBc, self.onesF[0:1, 0:64], rsum[hs][:, 0:nq], True, True)
                                self.cp(bcs[hs][:, 0:nq], pBc, eng="act")
                                dst = mix[hs * 64:hs * 64 + 64, h // 2, 0:nq]
                                self.tt(dst, pO[0:64, :], bcs[hs][:, 0:nq], ALU.mult)
                                self.tt(dst, dst, Gq[hs * 64:hs * 64 + 64, h // 2, 0:nq], ALU.mult, eng="pool")
                            pending.append((i + 2, fin))
                    while pending and (pending[0][0] <= i or i == len(items) + LOOK - 1):
                        pending.pop(0)[1]()
                for ti in range(nq // TT):
                    x = xt[xi % 2]; xi += 1
                    self.ld("x2t%d" % (xi % 2), x[0:TT, :], io["x"][q0 + ti * TT:q0 + (ti + 1) * TT, :])
                    for half in range(2):
                        po = self.ps(6 + half, [TT, 512], F32)
                        for k in range(8):
                            self.mm(po, mix[:, k, ti * TT:(ti + 1) * TT], wo[:, k, half * 512:(half + 1) * 512], k == 0, k == 7)
                        self.tt(x[0:TT, half * 512:(half + 1) * 512], x[0:TT, half * 512:(half + 1) * 512], po, ALU.add)
                    self.st("x2t%d" % (xi % 2), X1d[q0 + ti * TT:q0 + (ti + 1) * TT, :], x[0:TT, :])

    def phase3(self):
        d = self.din
        self.off = self.base_off
        NB = self.NB3
        xt = [self.sb("x3t%d" % i, [128, D], F32) for i in range(2)]
        xr = [self.sb("x3r%d" % i, [128, D], F32) for i in range(2)]
        stage = xt
        WC = self.dscr["WC"]
        WCv = WC.r("(k p) n -> p k n", p=128)
        woc = self.sb("woc", [128, 8, D], BF16)
        self.load_weight(woc, d["w_out_c"], D, 0, D, None, stage)
        cb = self.load_vec_pk("cb", d["conv_b"], D)
        lng = self.load_vec_pk("lng", d["ln_g"], D)
        lnb = self.load_vec_pk("lnb", d["ln_b"], D)
        cw = self.sb("cw", [128, 8, CW], F32)
        for k in range(8):
            self.ld("misc", cw[:, k, :], d["conv_w"][:, k * 128:(k + 1) * 128].r("t p -> p t"), slow=True)
        diag = self.sb("diag", [128, 8, CW, 128], BF16)
        for k in range(8):
            for t in range(CW):
                if t % 2:
                    self.ts(diag[:, k, t, :], self.identF, cw[:, k, t:t + 1], None, ALU.mult)
                else:
                    self.act(diag[:, k, t, :], self.identF, AF.Copy, scale=cw[:, k, t:t + 1])
        wch = [self.sb("wch%d" % i, [128, 8, 512], BF16) for i in range(3)]
        H = CW - 1
        hb = self.sb("hb3", [128, D], BF16)
        junk = hb
        hT = self.sb("hT3", [128, 8, NB], BF16)
        ss = self.sb("ss3", [128, 2], F32); rs = self.sb("rs3", [128, 2], F32)
        vT = self.sb("vT", [128, 8, H + NB], BF16)
        vtails = [self.sb("vtail%d" % i, [128, 8, 32], F32) for i in range(2)]
        gTs = [self.sb("gT%d" % i, [128, 8, NB], BF16) for i in range(2)]
        sgm = [self.sb("sgm%d" % i, [128, NB], F32) for i in range(2)]
        vf = [self.sb("vf%d" % i, [128, NB], F32) for i in range(2)]
        ybf = self.sb("ybf", [128, 8, NB], BF16)
        sqb = self.sb("sqb", [128, 8, NB], BF16)
        mean = self.sb("mean", [128, NB], F32); var = self.sb("var", [128, NB], F32); msq = self.sb("msq", [128, NB], F32)
        tmp = [self.sb("tmp3%d" % i, [128, NB], F32) for i in range(2)]
        halo = self.sb("halo", [128, 8, H], BF16)
        mT = self.sb("mT", [128, 8, NB], BF16)
        tailo = self.sb("tailo", [32, D], F32)
        scv = tailo
        st_ = dict(wci=0, xi=0, xri=0)

        blocks = []
        for si, (kind, idx, Tq, P) in enumerate(self.seqs):
            nb = min(NB, Tq)
            for b in range(Tq // nb):
                blocks.append(dict(si=si, b=b, nb=nb, r=len(blocks), last=(b == Tq // nb - 1)))

        def wchunk(c0):
            wci = st_["wci"]; st_["wci"] += 1
            w = wch[wci % 3]
            self.ld("wch%d" % (wci % 3), w, WCv[:, :, c0:c0 + 512])
            return w

        def front(blk):
            si, b, nb, r = blk["si"], blk["b"], blk["nb"], blk["r"]
            kind, idx, Tq, P = self.seqs[si]
            io = self.seq_io(si)
            X1d = self.dscr["X1" + str(si)]
            TT = min(128, nb); ntile = nb // TT; t0 = b * nb
            gT = gTs[r % 2]
            if b == 0:
                if kind == "p":
                    self.memset(vT[:, :, 0:H], 0.0)
                else:
                    self.ld("scv", scv[0:H, :], io["sconv"])
                    for k in range(8):
                        pc = self.ps(7, [128, H], F32)
                        self.tr(pc, scv[0:H, k * 128:(k + 1) * 128], self.identF[0:H, 0:H])
                        self.cp(vT[:, k, 0:H], pc, eng="act")
            else:
                self.cp(vT[:, :, 0:H], halo)
            xis = []
            for ti in range(ntile):
                xis.append(st_["xi"]); st_["xi"] += 1

            def ldx(ti):
                self.ld("x3t%d" % (xis[ti] % 2), xt[xis[ti] % 2][0:TT, :], X1d[t0 + ti * TT:t0 + (ti + 1) * TT, :])
            ldx(0)
            for ti in range(ntile):
                if ti + 1 < ntile:
                    ldx(ti + 1)
                x = xt[xis[ti] % 2]
                self.act(junk[0:TT, :], x[0:TT, :], AF.Square, accum=ss[0:TT, 0:1])
                self.rstd_from_ss(rs[0:TT, 0:1], ss[0:TT, 0:1], D)
                self.ts(hb[0:TT, :], x[0:TT, :], rs[0:TT, 0:1], None, ALU.mult)
                pT = self.ps(0, [128, 8, TT], BF16)
                for k in range(8):
                    self.tr(pT[:, k, :], hb[0:TT, k * 128:(k + 1) * 128], self.identB[0:TT, 0:TT], last=(k == 7))
                self.cp(hT[:, :, ti * TT:(ti + 1) * TT], pT, eng="act")
            for half in range(2):
                wa = wchunk(half * 512)
                wb_ = wchunk(1024 + half * 512)
                for mm_ in range(4):
                    m = half * 4 + mm_
                    pa = self.ps(1 + 2 * (m % 2), [128, nb], F32); pb_ = self.ps(2 + 2 * (m % 2), [128, nb], F32)
                    for k in range(8):
                        self.mm(pa, wa[:, k, mm_ * 128:(mm_ + 1) * 128], hT[:, k, 0:nb], k == 0, k == 7)
                    for k in range(8):
                        self.mm(pb_, wb_[:, k, mm_ * 128:(mm_ + 1) * 128], hT[:, k, 0:nb], k == 0, k == 7)
                    sg = sgm[m % 2]; v_ = vf[m % 2]
                    self.act(sg[:, 0:nb], pb_, AF.Sigmoid)
                    self.tt(v_[:, 0:nb], pa, sg[:, 0:nb], ALU.mult)
                    self.cp(vT[:, m, H:H + nb], v_[:, 0:nb], eng="pool")
                    if blk["last"]:
                        self.cp(vtails[r % 2][:, m, :], v_[:, nb - 32:nb], eng="pool")
            for half in range(2):
                wg = wchunk(2048 + half * 512)
                for mm_ in range(4):
                    m = half * 4 + mm_
                    pg = self.ps(5 + (m % 2), [128, nb], F32)
                    for k in range(8):
                        self.mm(pg, wg[:, k, mm_ * 128:(mm_ + 1) * 128], hT[:, k, 0:nb], k == 0, k == 7)
                    self.act(gT[:, m, 0:nb], pg, AF.Silu)

        def conv_stats(blk):
            nb = blk["nb"]
            for m in range(8):
                pc = self.ps(5 + (m % 2), [128, nb], F32)
                for t in range(CW):
                    self.mm(pc, diag[:, m, t, :], vT[:, m, t:t + nb], t == 0, t == CW - 1)
                self.act(ybf[:, m, 0:nb], pc, AF.Identity, bias=cb[:, m:m + 1])
                self.act(sqb[:, m, 0:nb], pc, AF.Square, bias=cb[:, m:m + 1])
            self.cp(halo, vT[:, :, nb:nb + H], eng="pool")
            pst = self.ps(7, [128, nb], F32); pst2 = self.ps(0, [128, nb], F32)
            for m in range(8):
                self.mm(pst, self.onesB, ybf[:, m, 0:nb], m == 0, m == 7)
            for m in range(8):
                self.mm(pst2, self.onesB, sqb[:, m, 0:nb], m == 0, m == 7)
            self.ts(mean[:, 0:nb], pst, 1.0 / D, None, ALU.mult)
            self.tt(msq[:, 0:nb], mean[:, 0:nb], mean[:, 0:nb], ALU.mult)
            self.stt(var[:, 0:nb], pst2, 1.0 / D, msq[:, 0:nb], ALU.mult, ALU.subtract)
            self.ts(var[:, 0:nb], var[:, 0:nb], EPS, None, ALU.add)
            self.act(var[:, 0:nb], var[:, 0:nb], AF.Sqrt)
            self.recip(var[:, 0:nb], var[:, 0:nb])

        def back(blk):
            si, b, nb, r = blk["si"], blk["b"], blk["nb"], blk["r"]
            io = self.seq_io(si)
            X1d = self.dscr["X1" + str(si)]
            TT = min(128, nb); ntile = nb // TT; t0 = b * nb
            gT = gTs[r % 2]
            for m in range(8):
                tm = tmp[m % 2]
                self.tt(tm[:, 0:nb], ybf[:, m, 0:nb], mean[:, 0:nb], ALU.subtract)
                self.tt(tm[:, 0:nb], tm[:, 0:nb], var[:, 0:nb], ALU.mult, eng="pool")
                self.act(tm[:, 0:nb], tm[:, 0:nb], AF.Silu, bias=lnb[:, m:m + 1], scale=lng[:, m:m + 1])
                self.tt(mT[:, m, 0:nb], tm[:, 0:nb], gT[:, m, 0:nb], ALU.mult)

        def outproj(blk):
            si, b, nb, r = blk["si"], blk["b"], blk["nb"], blk["r"]
            io = self.seq_io(si)
            X1d = self.dscr["X1" + str(si)]
            TT = min(128, nb); ntile = nb // TT; t0 = b * nb
            for ti in range(ntile):
                xri = st_["xri"]; st_["xri"] += 1
                x = xr[xri % 2]
                self.ld("x3r%d" % (xri % 2), x[0:TT, :], X1d[t0 + ti * TT:t0 + (ti + 1) * TT, :])
                for half in range(2):
                    po = self.ps(1 + half, [TT, 512], F32)
                    for k in range(8):
                        self.mm(po, mT[:, k, ti * TT:(ti + 1) * TT], woc[:, k, half * 512:(half + 1) * 512], k == 0, k == 7)
                    self.tt(x[0:TT, half * 512:(half + 1) * 512], x[0:TT, half * 512:(half + 1) * 512], po, ALU.add)
                self.st("x3r%d" % (xri % 2), io["y"][t0 + ti * TT:t0 + (ti + 1) * TT, :], x[0:TT, :])
            if blk["last"]:
                for k in range(8):
                    pc = self.ps(7, [32, 128], F32)
                    self.tr(pc, vtails[r % 2][:, k, :], self.identF)
                    self.cp(tailo[:, k * 128:(k + 1) * 128], pc, eng="act")
                self.st("tailo", io["conv"], tailo[2:32, :])

        nblk = len(blocks)
        front(blocks[0])
        for r in range(nblk):
            conv_stats(blocks[r])
            back(blocks[r])
            if r + 1 < nblk:
                front(blocks[r + 1])
            outproj(blocks[r])


def host_consts(T, PAST, TS):
    half = RD // 2
    inv = (10000.0 ** (-np.arange(half, dtype=np.float32) / half)).astype(np.float32)
    def tab(pos):
        ang = pos.astype(np.float32)[:, None] * inv[None, :]
        c_ = np.cos(ang).astype(np.float32); s_ = np.sin(ang).astype(np.float32)
        return np.concatenate([c_, c_], axis=1), np.concatenate([-s_, s_], axis=1)
    cp_, sp_ = tab(np.arange(T))
    cs_, ss_ = tab(PAST + np.arange(TS))
    ident = np.eye(128, dtype=np.float32)
    s_idx = np.arange(128) // 16
    mask = (s_idx[None, :] >= s_idx[:, None]).astype(np.float32)
    return dict(c_ident=ident, c_mask=mask, c_cos_p=cp_, c_sin_p=sp_, c_cos_s=cs_, c_sin_s=ss_)


WEIGHT_MAP = dict(norm_ab="norm_ab", w_in_ab="w_in_ab", g_q_lat="g_q_lat", w_uq="w_uq", g_kv_lat="g_kv_lat",
                  w_uk="w_uk", w_uv="w_uv", g_q_nope="g_q_nope", g_q_rope="g_q_rope", g_k_nope="g_k_nope",
                  g_k_rope="g_k_rope", lam_re="s5_lam_re", lam_im="s5_lam_im", log_dt="s5_log_dt",
                  b_re="s5_b_re", b_im="s5_b_im", c_re="s5_c_re", c_im="s5_c_im", s5_d="s5_d", w_glu="s5_w_glu",
                  b_glu="s5_b_glu", w_out_ab="w_out_ab", norm_c="norm_c", w_in_c="w_in_c", conv_w="conv_w",
                  conv_b="conv_b", ln_g="ln_g", ln_b="ln_b", w_out_c="w_out_c")


def make_in_maps(inputs, n_cores, NP, NS, T, PAST, TS):
    f = lambda a: np.ascontiguousarray(np.asarray(a, dtype=np.float32))
    consts = host_consts(T, PAST, TS)
    w = {}
    for k, src in WEIGHT_MAP.items():
        a = f(inputs[src])[0]
        if k in ("w_uk", "w_uv"):
            a = a.reshape(a.shape[0], -1)
        w[k] = np.ascontiguousarray(a)
    maps = []
    for c in range(n_cores):
        m = dict(w)
        m.update(consts)
        m["xp"] = f(inputs["x_prompt"][c * NP:(c + 1) * NP])
        m["xs"] = f(inputs["x_sample"][c * NS:(c + 1) * NS])
        m["clat"] = f(inputs["cache_mla_latent"][0, c * NS:(c + 1) * NS])
        m["ckr"] = f(inputs["cache_mla_krope"][0, c * NS:(c + 1) * NS])
        m["sre"] = f(inputs["state_s5_re"][0, c * NS:(c + 1) * NS])
        m["sim"] = f(inputs["state_s5_im"][0, c * NS:(c + 1) * NS])
        m["sconv"] = f(inputs["state_conv"][0, c * NS:(c + 1) * NS])
        maps.append(m)
    return maps


def assemble(results):
    cat = lambda k: np.concatenate([r[k] for r in results], axis=0)
    yp, ys = cat("yp"), cat("ys")
    st = lambda k: cat(k)[None]
    return (yp, ys, st("lat_p"), st("kr_p"), st("re_p"), st("im_p"), st("conv_p"),
            st("lat_s"), st("kr_s"), st("re_s"), st("im_s"), st("conv_s"))


def kernel(**inputs):
    n = 8
    B, T = inputs["x_prompt"].shape[0], inputs["x_prompt"].shape[1]
    BS, TS = inputs["x_sample"].shape[0], inputs["x_sample"].shape[1]
    PAST = inputs["cache_mla_latent"].shape[2]
    NP, NS = B // n, BS // n
    kb = K(NP, T, NS, PAST, TS)
    nc = kb.build()
    maps = make_in_maps(inputs, n, NP, NS, T, PAST, TS)
    res = run_bass_kernel_spmd(nc, maps, core_ids=list(range(n)))
    outs = assemble(res.results)
    return tuple(np.asarray(o, dtype=np.float32) for o in outs)
```
